# Optimizing a Trainium2 kernel written in Bass

```python
import jax, jax.numpy as jnp
from jax import lax
import numpy as np

D_MODEL = 1024
BATCH = 4
SEQ = 4096
DEPTH = 2

CHUNK = 64
N_A_LAYERS = DEPTH // 2
N_B_LAYERS = DEPTH - N_A_LAYERS
D_FF = 2816
GLA_HEADS = 4
GLA_DK = D_MODEL // 2
GLA_DV = D_MODEL
GLA_HEAD_K = GLA_DK // GLA_HEADS
GLA_HEAD_V = GLA_DV // GLA_HEADS
GATE_RANK = 16
GATE_TAU = 16.0
SB_HEADS = 16
SB_DIM = D_MODEL
SB_HEAD = SB_DIM // SB_HEADS
Q_BLOCK = 128
DEEPNORM_ALPHA = (2 * DEPTH) ** 0.25
DEEPNORM_BETA = (8 * DEPTH) ** -0.25
LN_EPS = 1e-5
RMS_EPS = 1e-6

kernel_name = 'yoco_gla_stickbreak_macaron_deepnorm'


def layer_norm(x, g, b):
    xf = x.astype(jnp.float32)
    mu = jnp.mean(xf, axis=-1, keepdims=True)
    var = jnp.mean(jnp.square(xf - mu), axis=-1, keepdims=True)
    y = (xf - mu) * lax.rsqrt(var + LN_EPS) * g.astype(jnp.float32) + b.astype(jnp.float32)
    return y.astype(x.dtype)


def swiglu(x, w_up, w_down):
    gate, up = jnp.split(x @ w_up, 2, axis=-1)
    return (jax.nn.silu(gate) * up) @ w_down


def gla_mixer(x, w_in, w_gk, b_gk, norm_g, w_out):
    bsz, seq, _ = x.shape
    nc = seq // CHUNK
    f32 = jnp.float32
    proj = x @ w_in
    q, k, v, r, low = jnp.split(
        proj, [GLA_DK, 2 * GLA_DK, 2 * GLA_DK + GLA_DV, 2 * GLA_DK + 2 * GLA_DV], axis=-1)
    log_g = jax.nn.log_sigmoid((low @ w_gk + b_gk).astype(f32)) / GATE_TAU

    def chunked(t, hd):
        return t.astype(f32).reshape(bsz, nc, CHUNK, GLA_HEADS, hd)

    q = chunked(q, GLA_HEAD_K) * GLA_HEAD_K ** -0.5
    k = chunked(k, GLA_HEAD_K)
    v = chunked(v, GLA_HEAD_V)
    log_g = chunked(log_g, GLA_HEAD_K)
    b_cum = jnp.cumsum(log_g, axis=2)
    b_tot = b_cum[:, :, -1]
    k_dec = k * jnp.exp(b_tot[:, :, None] - b_cum)
    scores = jnp.einsum('bnihd,bnjhd->bnhij', q, k_dec)
    o_intra = jnp.einsum('bnhij,bnjhe->bnihe', scores, v)
    u = jnp.einsum('bnjhd,bnjhe->nbhde', k_dec, v)
    decay = jnp.moveaxis(jnp.exp(b_tot), 1, 0)

    def step(s, inp):
        dec, uc = inp
        return dec[..., None] * s + uc, s

    s0 = jnp.zeros((bsz, GLA_HEADS, GLA_HEAD_K, GLA_HEAD_V), f32)
    _, s_prev = lax.scan(step, s0, (decay, u))
    o_inter = jnp.einsum('bnihd,nbhde->bnihe', q * jnp.exp(b_tot)[:, :, None], s_prev)
    o = o_intra + o_inter
    o = o * lax.rsqrt(jnp.mean(jnp.square(o), axis=-1, keepdims=True) + RMS_EPS) * norm_g.astype(f32)
    o = o.reshape(bsz, seq, GLA_DV).astype(x.dtype)
    return (jax.nn.silu(r) * o) @ w_out


def shared_kv(x, w_kv):
    bsz, seq, _ = x.shape
    k, v = jnp.split(x @ w_kv, 2, axis=-1)
    k = k.reshape(bsz, seq, SB_HEADS, SB_HEAD).transpose(0, 2, 1, 3)
    v = v.reshape(bsz, seq, SB_HEADS, SB_HEAD).transpose(0, 2, 1, 3)
    return k, v


def stick_breaking(x, k, v, w_q, w_out):
    bsz, seq, _ = x.shape
    nb = seq // Q_BLOCK
    q = (x @ w_q).reshape(bsz, nb, Q_BLOCK, SB_HEADS, SB_HEAD).transpose(1, 0, 3, 2, 4)
    s_pos = jnp.arange(seq)
    scale = SB_HEAD ** -0.5

    def block(inp):
        qb, start = inp
        z = jnp.einsum('bhid,bhjd->bhij', qb, k, preferred_element_type=jnp.float32) * scale
        t_pos = start + jnp.arange(Q_BLOCK)
        mask = s_pos[None, :] < t_pos[:, None]
        log_keep = jnp.where(mask, jax.nn.log_sigmoid(-z), 0.0)
        log_a = jax.nn.log_sigmoid(z) + lax.cumsum(log_keep, axis=3, reverse=True) - log_keep
        a = jnp.where(mask, jnp.exp(log_a), 0.0)
        return jnp.einsum('bhij,bhjd->bhid', a.astype(v.dtype), v)

    o = lax.map(block, (q, jnp.arange(nb) * Q_BLOCK))
    o = o.transpose(1, 0, 3, 2, 4).reshape(bsz, seq, SB_DIM)
    return o @ w_out


def setup_inputs(seed: int = 0) -> dict:
    key = jax.random.key(seed)
    ks = jax.random.split(key, 20)

    def nrm(k, shape, scale):
        return jax.random.normal(k, shape, jnp.float32) * scale

    ds = D_MODEL ** -0.5
    x = nrm(ks[0], (BATCH, SEQ, D_MODEL), 1.0)
    ln_g = 1.0 + nrm(ks[1], (DEPTH, 3, D_MODEL), 0.02)
    ln_b = nrm(ks[2], (DEPTH, 3, D_MODEL), 0.02)
    ffn_w_up = nrm(ks[3], (DEPTH, 2, D_MODEL, 2 * D_FF), ds)
    ffn_w_down = nrm(ks[4], (DEPTH, 2, D_FF, D_MODEL), D_FF ** -0.5 * DEEPNORM_BETA)
    gla_w_in = jnp.concatenate([
        nrm(ks[5], (N_A_LAYERS, D_MODEL, 2 * GLA_DK), ds),
        nrm(ks[6], (N_A_LAYERS, D_MODEL, GLA_DV), ds * DEEPNORM_BETA),
        nrm(ks[7], (N_A_LAYERS, D_MODEL, GLA_DV), ds),
        nrm(ks[8], (N_A_LAYERS, D_MODEL, GATE_RANK), ds),
    ], axis=-1)
    gla_w_gk = nrm(ks[9], (N_A_LAYERS, GATE_RANK, GLA_DK), GATE_RANK ** -0.5)
    gla_b_gk = nrm(ks[10], (N_A_LAYERS, GLA_DK), 0.01)
    gla_norm_g = 1.0 + nrm(ks[11], (N_A_LAYERS, GLA_HEAD_V), 0.02)
    gla_w_out = nrm(ks[12], (N_A_LAYERS, GLA_DV, D_MODEL), GLA_DV ** -0.5 * DEEPNORM_BETA)
    sb_w_kv = jnp.concatenate([
        nrm(ks[13], (D_MODEL, SB_DIM), ds),
        nrm(ks[14], (D_MODEL, SB_DIM), ds * DEEPNORM_BETA),
    ], axis=-1)
    sb_w_q = nrm(ks[15], (N_B_LAYERS, D_MODEL, SB_DIM), ds)
    sb_w_out = nrm(ks[16], (N_B_LAYERS, SB_DIM, D_MODEL), SB_DIM ** -0.5 * DEEPNORM_BETA)
    return {'x': x, 'ln_g': ln_g, 'ln_b': ln_b, 'ffn_w_up': ffn_w_up, 'ffn_w_down': ffn_w_down,
            'gla_w_in': gla_w_in, 'gla_w_gk': gla_w_gk, 'gla_b_gk': gla_b_gk,
            'gla_norm_g': gla_norm_g, 'gla_w_out': gla_w_out,
            'sb_w_kv': sb_w_kv, 'sb_w_q': sb_w_q, 'sb_w_out': sb_w_out}


def reference(x, ln_g, ln_b, ffn_w_up, ffn_w_down, gla_w_in, gla_w_gk, gla_b_gk,
              gla_norm_g, gla_w_out, sb_w_kv, sb_w_q, sb_w_out):
    k_sh, v_sh = None, None
    for layer in range(DEPTH):
        if layer == N_A_LAYERS:
            k_sh, v_sh = shared_kv(x, sb_w_kv)
        x = layer_norm(DEEPNORM_ALPHA * x + 0.5 * swiglu(x, ffn_w_up[layer, 0], ffn_w_down[layer, 0]),
                       ln_g[layer, 0], ln_b[layer, 0])
        if layer < N_A_LAYERS:
            mix = gla_mixer(x, gla_w_in[layer], gla_w_gk[layer], gla_b_gk[layer],
                            gla_norm_g[layer], gla_w_out[layer])
        else:
            j = layer - N_A_LAYERS
            mix = stick_breaking(x, k_sh, v_sh, sb_w_q[j], sb_w_out[j])
        x = layer_norm(DEEPNORM_ALPHA * x + mix, ln_g[layer, 1], ln_b[layer, 1])
        x = layer_norm(DEEPNORM_ALPHA * x + 0.5 * swiglu(x, ffn_w_up[layer, 1], ffn_w_down[layer, 1]),
                       ln_g[layer, 2], ln_b[layer, 2])
    return x
```

```python
from contextlib import ExitStack
import numpy as np
import ml_dtypes
import concourse.bass as bass
import concourse.mybir as mybir
from concourse.bass_utils import run_bass_kernel_spmd

F32 = mybir.dt.float32
BF16 = mybir.dt.bfloat16
ALU = mybir.AluOpType
ACTF = mybir.ActivationFunctionType

ENGS = ['sync', 'tensor', 'vector', 'scalar', 'gpsimd']
SAME_ENGINE_SYNC = {'vector': True, 'scalar': True, 'gpsimd': True, 'tensor': False, 'sync': False}

D = 1024
DFF = 2816
DEPTH = 2
ALPHA = float((2 * DEPTH) ** 0.25)
LN_EPS = 1e-5
RMS_EPS = 1e-6
GH, GDK, GDV = 4, 128, 256
GATE_RANK = 16
GATE_TAU = 16.0
SBH, SBD = 16, 64
NEG_BIG = -240.0


class Op:
    __slots__ = ('eng', 'fn', 'deps', 'dma', 'tok', 'sig')

    def __init__(self, eng, fn, deps, dma):
        self.eng, self.fn, self.deps, self.dma = eng, fn, deps, dma
        self.tok = None
        self.sig = 0


class Prog:
    def __init__(self, nc):
        self.nc = nc
        self.ops = {e: [] for e in ENGS}
        self.bufs = {}
        self.dma_counts = {}
        self.keep = set()

    def add(self, eng, fn, reads=(), writes=(), dma=None, deps=()):
        d = set(t for t in deps if t is not None)
        for k in reads:
            st = self.bufs.get(k)
            if st is not None and st[0] is not None:
                d.add(st[0])
        for k in writes:
            st = self.bufs.get(k)
            if st is not None:
                if st[0] is not None:
                    d.add(st[0])
                d.update(st[1].values())
        op = Op(eng, fn, d, dma)
        idx = len(self.ops[eng])
        self.ops[eng].append(op)
        if dma is not None:
            c = self.dma_counts.get(dma, 0) + 1
            self.dma_counts[dma] = c
            tok = ('d', dma, c)
        else:
            tok = ('e', eng, idx)
        op.tok = tok
        for k in reads:
            st = self.bufs.setdefault(k, [None, {}])
            rk = eng if dma is None else ('d', dma)
            st[1][rk] = tok
        for k in writes:
            self.bufs[k] = [tok, {}]
        return tok

    def seal(self, keys, dma_key):
        tok = ('d', dma_key, self.dma_counts[dma_key])
        for k in keys:
            st = self.bufs.get(k)
            if st is None:
                continue
            if st[0] is not None and st[0][0] == 'd' and st[0][1] == dma_key:
                st[0] = tok
            rk = ('d', dma_key)
            if rk in st[1]:
                st[1][rk] = tok

    def barrier(self):
        toks = [('d', k, cnt) for k, cnt in self.dma_counts.items()]
        for e in ENGS:
            for op in reversed(self.ops[e]):
                if op.dma is None and op.fn is not None:
                    toks.append(op.tok)
                    break
        for e in ENGS:
            self.add(e, None, deps=toks)
        self.bufs = {k: v for k, v in self.bufs.items() if k in self.keep}

    def all_dma_tokens(self, prefix=None):
        return [('d', k, c) for k, c in self.dma_counts.items()
                if prefix is None or str(k).startswith(prefix)]

    def emit(self):
        nc = self.nc
        needed = set()
        for e in ENGS:
            for op in self.ops[e]:
                for t in op.deps:
                    if t[0] == 'e':
                        if t[1] == e and not SAME_ENGINE_SYNC[e]:
                            continue
                        needed.add(t)
        for e in ENGS:
            n = 0
            for op in self.ops[e]:
                if op.dma is None and op.tok in needed:
                    n += 1
                    op.sig = n
        with ExitStack() as es:
            esem = {e: es.enter_context(nc.semaphore("s_" + e)) for e in ENGS}
            dsem = {}
            for i, k in enumerate(self.dma_counts):
                dsem[k] = es.enter_context(nc.semaphore("d%d" % i))
            block = es.enter_context(nc.Block())
            for e in ENGS:
                ops = self.ops[e]

                def body(eng, e=e, ops=ops):
                    waited = {}
                    for op in ops:
                        best = {}
                        for t in op.deps:
                            if t[0] == 'e':
                                if t[1] == e and not SAME_ENGINE_SYNC[e]:
                                    continue
                                sem = esem[t[1]]
                                val = self.ops[t[1]][t[2]].sig
                                key = ('e', t[1])
                            else:
                                sem = dsem[t[1]]
                                val = 16 * t[2]
                                key = ('d', t[1])
                            if key not in best or best[key][1] < val:
                                best[key] = (sem, val)
                        for key, (sem, val) in best.items():
                            if waited.get(key, 0) >= val:
                                continue
                            eng.wait_ge(sem, val)
                            waited[key] = val
                        if op.fn is None:
                            continue
                        ins = op.fn(eng)
                        if op.dma is not None:
                            ins.then_inc(dsem[op.dma], 16)
                        elif op.sig:
                            ins.then_inc(esem[e], 1)
                getattr(block, e)(body)


class Ctx:
    ARENA_BYTES = 207 * 1024

    def __init__(self, dims):
        self.dm = dims
        self.nc = bass.Bass("TRN2", target_bir_lowering=False)
        self.P = Prog(self.nc)
        self.ps = [self.nc.alloc_psum_tensor("psb%d" % i, [128, 512], F32) for i in range(8)]
        self.uid = 0
        self.dram = {}
        self.arena = self.nc.alloc_sbuf_tensor("arena", [128, self.ARENA_BYTES // 4], F32)
        self.off = 0

    def inp(self, name, shape, dt=F32):
        t = self.nc.dram_tensor(name, list(shape), dt, kind="ExternalInput").ap()
        self.dram[name] = t
        return t

    def outp(self, name, shape, dt=F32):
        t = self.nc.dram_tensor(name, list(shape), dt, kind="ExternalOutput").ap()
        self.dram[name] = t
        return t

    def sb(self, name, shape, dt=F32):
        esz = 2 if dt == BF16 else 4
        n = 1
        for d in shape[1:]:
            n *= d
        nbytes = (n * esz + 63) // 64 * 64
        if self.off + nbytes > self.ARENA_BYTES:
            raise RuntimeError("SBUF arena overflow allocating %s (%d + %d)" % (name, self.off, nbytes))
        a = self.arena[0:shape[0], self.off // 4:(self.off + nbytes) // 4]
        self.off += nbytes
        if dt != F32:
            a = a.bitcast(dt)
        a = a[:, 0:n]
        if len(shape) == 3:
            a = a.rearrange("p (a b) -> p a b", a=shape[1])
        elif len(shape) != 2:
            raise RuntimeError("bad shape")
        return a

    def mark(self):
        return self.off

    def release(self, mark):
        self.off = mark
        self.P.barrier()

    def psk(self, i):
        return ('ps', i)


def load_consts(c):
    P = c.P
    ident_d = c.inp("c_ident", [128, 128])
    c.ident = c.sb("ident", [128, 128], F32)
    c.identb = c.sb("identb", [128, 128], BF16)
    P.add('sync', lambda e: e.dma_start(out=c.ident[:], in_=ident_d), writes=['ident'], dma='c0')
    P.add('gpsimd', lambda e: e.dma_start(out=c.identb[:], in_=ident_d), writes=['identb'], dma='c1')
    c.epsb = c.sb("epsb", [128, 2], F32)
    c.oneb = c.sb("oneb", [128, 1], F32)
    P.add('vector', lambda e: e.memset(c.oneb[:], 1.0), writes=['oneb'])
    P.add('vector', lambda e: e.memset(c.epsb[:, 0:1], LN_EPS), writes=['epsb'])
    P.add('vector', lambda e: e.memset(c.epsb[:, 1:2], RMS_EPS), writes=['epsb'])


def load_ln_params(c, name, g_d, b_d):
    Dm = c.dm['D']
    g = c.sb(name + "_g", [128, Dm], F32)
    b = c.sb(name + "_b", [128, Dm], F32)
    c.P.add('sync', lambda e: e.dma_start(out=g[:], in_=g_d.partition_broadcast(128)), writes=[name + '_g'], dma='lnp')
    c.P.add('sync', lambda e: e.dma_start(out=b[:], in_=b_d.partition_broadcast(128)), writes=[name + '_b'], dma='lnp')
    c.P.seal([name + '_g', name + '_b'], 'lnp')
    return (g, b, name + '_g', name + '_b')


def transposes_to_xT(c, x_res, xkey, tts, xT, xTkey, banks):
    P = c.P
    KC = c.dm['D'] // 128
    bi = 0
    for i, tt in enumerate(tts):
        for k0 in range(0, KC, 4):
            bank = banks[bi % len(banks)]
            bi += 1
            pt = c.ps[bank]
            for kk in range(4):
                kc = k0 + kk
                P.add('tensor', lambda e, pt=pt, kk=kk, kc=kc, tt=tt: e.transpose(
                    out=pt[:, kk * 128:(kk + 1) * 128], in_=x_res[:, tt, kc * 128:(kc + 1) * 128], identity=c.ident[:]),
                    reads=[(xkey, tt), 'ident'], writes=[c.psk(bank)])
            eng = 'scalar' if (bi % 2 == 0) else 'vector'
            src = pt[:].rearrange("p (a b) -> p a b", a=4)
            dst = xT[:, k0:k0 + 4, i * 128:(i + 1) * 128]
            if eng == 'scalar':
                P.add('scalar', lambda e, src=src, dst=dst: e.copy(out=dst, in_=src),
                      reads=[c.psk(bank)], writes=[(xTkey, i)])
            else:
                P.add('vector', lambda e, src=src, dst=dst: e.tensor_copy(out=dst, in_=src),
                      reads=[c.psk(bank)], writes=[(xTkey, i)])


def ln_epilogue(c, banks2, x_res, xkey, tt, lnp, tmp, tmpkey, extra_reads=()):
    P = c.P
    Dm = c.dm['D']
    g, b, gk, bk = lnp
    t2, st = tmp
    nh = Dm // 512
    xk = (xkey, tt)
    for h in range(nh):
        P.add('vector', lambda e, h=h: e.scalar_tensor_tensor(
            out=x_res[:, tt, h * 512:(h + 1) * 512], in0=x_res[:, tt, h * 512:(h + 1) * 512], scalar=ALPHA,
            in1=c.ps[banks2[h]][:], op0=ALU.mult, op1=ALU.add),
            reads=[xk, c.psk(banks2[h])] + list(extra_reads), writes=[xk])
    for h in range(nh):
        P.add('vector', lambda e, h=h: e.bn_stats(out=st[:, 6 * h:6 * h + 6], in_=x_res[:, tt, h * 512:(h + 1) * 512]),
              reads=[xk], writes=[(tmpkey, 'st', h)])
    P.add('vector', lambda e: e.bn_aggr(out=st[:, 12:14], in_=st[:, 0:6 * nh]),
          reads=[(tmpkey, 'st', h) for h in range(nh)], writes=[(tmpkey, 'mv')])
    P.add('scalar', lambda e: e.activation(out=st[:, 16:17], in_=st[:, 13:14], func=ACTF.Ln, bias=c.epsb[:, 0:1]),
          reads=[(tmpkey, 'mv'), 'epsb'], writes=[(tmpkey, 'lnv')])
    P.add('scalar', lambda e: e.activation(out=st[:, 14:15], in_=st[:, 16:17], func=ACTF.Exp, scale=-0.5),
          reads=[(tmpkey, 'lnv')], writes=[(tmpkey, 'rstd')])
    P.add('vector', lambda e: e.scalar_tensor_tensor(out=st[:, 15:16], in0=st[:, 12:13], scalar=-1.0, in1=st[:, 14:15],
                                                     op0=ALU.mult, op1=ALU.mult),
          reads=[(tmpkey, 'mv'), (tmpkey, 'rstd')], writes=[(tmpkey, 'nb')])
    P.add('scalar', lambda e: e.activation(out=t2[:], in_=x_res[:, tt, :], func=ACTF.Identity, bias=st[:, 15:16], scale=st[:, 14:15]),
          reads=[xk, (tmpkey, 'nb'), (tmpkey, 'rstd')], writes=[(tmpkey, 't2')])
    P.add('vector', lambda e: e.tensor_tensor(out=t2[:], in0=t2[:], in1=g[:], op=ALU.mult),
          reads=[(tmpkey, 't2'), gk], writes=[(tmpkey, 't2')])
    P.add('gpsimd', lambda e: e.tensor_tensor(out=x_res[:, tt, :], in0=t2[:], in1=b[:], op=ALU.add),
          reads=[(tmpkey, 't2'), bk], writes=[xk])


def alloc_ln_tmp(c, name):
    Dm = c.dm['D']
    return [(c.sb(name + "_t2_%d" % i, [128, Dm], F32), c.sb(name + "_st_%d" % i, [128, 32], F32)) for i in range(2)]


def ffn_sublayer(c, x_res, xkey, NT, w_up_d, w_dn_d, lnp, bufs):
    P = c.P
    Dm, Dff = c.dm['D'], c.dm['DFF']
    KC, JC = Dm // 128, Dff // 128
    ST = min(1024, NT)
    nst = NT // ST
    xT, hT, wd, wslots, sgs, lntmp = bufs['xT'], bufs['hT'], bufs['wd'], bufs['wslots'], bufs['sg'], bufs['lntmp']
    uid = c.uid
    c.uid += 1
    nsl = len(wslots)
    nh = Dm // 512
    for st in range(nst):
        tts = list(range(st * (ST // 128), (st + 1) * (ST // 128)))
        transposes_to_xT(c, x_res, xkey, tts, xT, 'xT', banks=[0, 2])
        for j in range(JC):
            P.add('gpsimd', lambda e, j=j: e.dma_start(out=wd[:, j, :], in_=w_dn_d[j * 128:(j + 1) * 128, :]),
                  writes=[('wd', j)], dma='wd')
        P.seal([('wd', j) for j in range(JC)], 'wd')
        ntk = ST // 512 if ST >= 512 else 1
        tw = min(512, ST)
        for j in range(JC):
            s = (uid * 1000 + st * JC + j) % nsl
            wg, wu = wslots[s]
            P.add('gpsimd', lambda e, j=j, wg=wg: e.dma_start(
                out=wg[:], in_=w_up_d[:, j * 128:(j + 1) * 128].rearrange("(kc p) n -> p kc n", p=128)),
                writes=[('wg', s)], dma='wg%d' % s)
            P.add('gpsimd', lambda e, j=j, wu=wu: e.dma_start(
                out=wu[:], in_=w_up_d[:, Dff + j * 128:Dff + (j + 1) * 128].rearrange("(kc p) n -> p kc n", p=128)),
                writes=[('wu', s)], dma='wu%d' % s)
            for t5 in range(ntk):
                par = (j * ntk + t5) % 2
                bg, bu = (0, 1) if par == 0 else (2, 3)
                cols = slice(t5 * tw, (t5 + 1) * tw)
                xkeys = [('xT', i) for i in range(t5 * (tw // 128), (t5 + 1) * (tw // 128))]
                for kc in range(KC):
                    P.add('tensor', lambda e, kc=kc, wg=wg, bg=bg, cols=cols: e.matmul(
                        c.ps[bg][:, 0:tw], lhsT=wg[:, kc, :], rhs=xT[:, kc, cols], start=(kc == 0), stop=(kc == KC - 1)),
                        reads=[('wg', s)] + xkeys, writes=[c.psk(bg)])
                for kc in range(KC):
                    P.add('tensor', lambda e, kc=kc, wu=wu, bu=bu, cols=cols: e.matmul(
                        c.ps[bu][:, 0:tw], lhsT=wu[:, kc, :], rhs=xT[:, kc, cols], start=(kc == 0), stop=(kc == KC - 1)),
                        reads=[('wu', s)] + xkeys, writes=[c.psk(bu)])
                sg = sgs[par]
                P.add('scalar', lambda e, sg=sg, bg=bg: e.activation(out=sg[:, 0:tw], in_=c.ps[bg][:, 0:tw], func=ACTF.Silu),
                      reads=[c.psk(bg)], writes=[('sg', par)])
                P.add('vector', lambda e, sg=sg, bu=bu, j=j, cols=cols: e.scalar_tensor_tensor(
                    out=hT[:, j, cols], in0=sg[:, 0:tw], scalar=0.5, in1=c.ps[bu][:, 0:tw], op0=ALU.mult, op1=ALU.mult),
                    reads=[('sg', par), c.psk(bu)], writes=[('hT', j, t5)])
        for i, tt in enumerate(tts):
            par = i % 2
            banks2 = [4 + 2 * par + h for h in range(nh)]
            t5 = (i * 128) // tw
            for h in range(nh):
                for j in range(JC):
                    P.add('tensor', lambda e, j=j, h=h, i=i, banks2=banks2: e.matmul(
                        c.ps[banks2[h]][:], lhsT=hT[:, j, i * 128:(i + 1) * 128], rhs=wd[:, j, h * 512:(h + 1) * 512],
                        start=(j == 0), stop=(j == JC - 1)),
                        reads=[('hT', j, t5), ('wd', j)], writes=[c.psk(banks2[h])])
            ln_epilogue(c, banks2, x_res, xkey, tt, lnp, lntmp[par], ('lntmp', par))


def alloc_ffn_bufs(c, NT):
    Dm, Dff = c.dm['D'], c.dm['DFF']
    KC, JC = Dm // 128, Dff // 128
    ST = min(1024, NT)
    b = {}
    b['xT'] = c.sb("xT", [128, KC, ST], BF16)
    b['hT'] = c.sb("hT", [128, JC, ST], BF16)
    b['wd'] = c.sb("wd", [128, JC, Dm], BF16)
    b['wslots'] = [(c.sb("wg%d" % i, [128, KC, 128], BF16), c.sb("wu%d" % i, [128, KC, 128], BF16)) for i in range(2)]
    b['sg'] = [c.sb("sg%d" % i, [128, 512], F32) for i in range(2)]
    b['lntmp'] = alloc_ln_tmp(c, "ln")
    return b


def load_x(c, x_d, x_res, xkey, NT):
    for tt in range(NT // 128):
        c.P.add('sync', lambda e, tt=tt: e.dma_start(out=x_res[:, tt, :], in_=x_d[tt * 128:(tt + 1) * 128, :]),
                writes=[(xkey, tt)], dma='ldx')
    c.P.seal([(xkey, tt) for tt in range(NT // 128)], 'ldx')


def store_x(c, x_d, x_res, xkey, NT, tag='stx'):
    for tt in range(NT // 128):
        c.P.add('sync', lambda e, tt=tt: e.dma_start(out=x_d[tt * 128:(tt + 1) * 128, :], in_=x_res[:, tt, :]),
                reads=[(xkey, tt)], dma=tag)
    c.P.seal([(xkey, tt) for tt in range(NT // 128)], tag)


def finish(c):
    toks = c.P.all_dma_tokens()
    c.P.add('sync', None, deps=toks)
    c.P.emit()
    return c.nc


def build_ffn_test(dims, NT):
    c = Ctx(dims)
    Dm, Dff = dims['D'], dims['DFF']
    x_d = c.inp("x", [NT, Dm])
    wup = c.inp("w_up", [Dm, 2 * Dff])
    wdn = c.inp("w_dn", [Dff, Dm])
    lng = c.inp("ln_g", [Dm])
    lnb = c.inp("ln_b", [Dm])
    out = c.outp("out", [NT, Dm])
    load_consts(c)
    x_res = c.sb("x_res", [128, NT // 128, Dm], F32)
    load_x(c, x_d, x_res, 'x', NT)
    lnp = load_ln_params(c, "ln0", lng, lnb)
    bufs = alloc_ffn_bufs(c, NT)
    ffn_sublayer(c, x_res, 'x', NT, wup, wdn, lnp, bufs)
    store_x(c, out, x_res, 'x', NT)
    return finish(c)


def load_gla_consts(c):
    t2_d = c.inp("c_t2s", [128, 128])
    ind_d = c.inp("c_ind", [128, 2])
    c.t2s = c.sb("t2s", [128, 128], F32)
    c.ind = c.sb("ind", [128, 2], F32)
    c.P.add('sync', lambda e: e.dma_start(out=c.t2s[:], in_=t2_d), writes=['t2s'], dma='c2')
    c.P.add('sync', lambda e: e.dma_start(out=c.ind[:], in_=ind_d), writes=['ind'], dma='c3')


def gla_consts_host():
    t = np.arange(128)
    same = (t[:, None] // 64) == (t[None, :] // 64)
    t2s = np.where(same & (t[:, None] > t[None, :]), -1.0 / GATE_TAU, 0.0).astype(np.float32)
    ind = np.zeros((128, 2), np.float32)
    ind[:64, 0] = -1.0 / GATE_TAU
    ind[64:, 1] = -1.0 / GATE_TAU
    return t2s, ind


def alloc_gla_bufs(c, full, S=None):
    Dm = c.dm['D']
    KC = Dm // 128
    b = {}
    b['win'] = c.sb("g_win", [128, KC, 3088], BF16)
    b['wgk'] = c.sb("g_wgk", [17, 512], BF16)
    b['xT'] = c.sb("g_xT", [128, KC, 512], BF16)
    b['lowT'] = c.sb("g_lowT", [17, 512], BF16)
    b['e'] = c.sb("g_e", [128, 512], F32)
    b['l'] = c.sb("g_l", [128, 512], F32)
    b['ed'] = c.sb("g_ed", [128, 512], F32)
    b['kdec'] = c.sb("g_kdec", [128, 512], BF16)
    b['v'] = c.sb("g_v", [128, 1024], BF16)
    b['decT'] = c.sb("g_decT", [128, 8], F32)
    b['S'] = S if S is not None else c.sb("g_S", [128, 4, 256], F32)
    if full:
        b['wout'] = c.sb("g_wout", [128, 8, Dm], BF16)
        b['qT'] = c.sb("g_qT", [128, 4, 512], BF16)
        b['sr'] = c.sb("g_sr", [128, 1024], F32)
        b['Sb'] = c.sb("g_Sb", [128, 4, 256], BF16)
        b['o'] = c.sb("g_o", [128, 4, 256], F32)
        b['junk'] = c.sb("g_junk", [128, 256], F32)
        b['ss'] = c.sb("g_ss", [128, 8], F32)
        b['gated'] = c.sb("g_gated", [128, 1024], BF16)
        b['gT'] = c.sb("g_gT", [128, 8, 128], BF16)
        b['ng'] = c.sb("g_ng", [128, 256], F32)
        b['lntmp'] = alloc_ln_tmp(c, "gln")
    return b


def gla_pass(c, x_res, xkey, NT, full, w, b, lnp=None):
    P = c.P
    Dm = c.dm['D']
    KC = Dm // 128
    nh = Dm // 512
    win, wgk, xT, lowT = b['win'], b['wgk'], b['xT'], b['lowT']
    S = b['S']
    for kc in range(KC):
        P.add('gpsimd', lambda e, kc=kc: e.dma_start(out=win[:, kc, :], in_=w['w_in'][kc * 128:(kc + 1) * 128, :]),
              writes=[('g_win', kc)], dma='g_win')
    P.seal([('g_win', kc) for kc in range(KC)], 'g_win')
    winkeys = [('g_win', kc) for kc in range(KC)]
    P.add('gpsimd', lambda e: e.dma_start(out=wgk[0:16, :], in_=w['w_gk']), writes=['g_wgk0'], dma='g_wgk')
    P.add('gpsimd', lambda e: e.dma_start(out=wgk[16:17, :], in_=w['b_gk'].rearrange("(o n) -> o n", o=1)),
          writes=['g_wgk1'], dma='g_wgk')
    P.seal(['g_wgk0', 'g_wgk1'], 'g_wgk')
    if isinstance(w.get('s_init'), str):
        pass
    elif w.get('s_init') is not None:
        P.add('sync', lambda e: e.dma_start(out=S[:], in_=w['s_init'].rearrange("h p d -> p h d")), writes=[('g_S', h) for h in range(GH)], dma='g_sinit')
    else:
        P.add('vector', lambda e: e.memset(S[:], 0.0), writes=[('g_S', h) for h in range(GH)])
    P.add('vector', lambda e: e.memset(lowT[:], 1.0), writes=['g_lowT'])
    if full:
        wout, qT, sr, Sb, o_sb, junk, ss, gated, gT, ng = (b[k] for k in ('wout', 'qT', 'sr', 'Sb', 'o', 'junk', 'ss', 'gated', 'gT', 'ng'))
        for kc in range(8):
            P.add('gpsimd', lambda e, kc=kc: e.dma_start(out=wout[:, kc, :], in_=w['w_out'][kc * 128:(kc + 1) * 128, :]),
                  writes=[('g_wout', kc)], dma='g_wout')
        P.seal([('g_wout', kc) for kc in range(8)], 'g_wout')
        P.add('sync', lambda e: e.dma_start(out=ng[:], in_=w['norm_g'].partition_broadcast(128)), writes=['g_ng'], dma='g_ng')
    e_sb, l_sb, ed, kdec, v_sb, decT = b['e'], b['l'], b['ed'], b['kdec'], b['v'], b['decT']
    GW = min(512, NT)
    ngrp = NT // GW
    tpg = GW // 128
    for gi in range(ngrp):
        tts = list(range(gi * tpg, (gi + 1) * tpg))
        transposes_to_xT(c, x_res, xkey, tts, xT, 'g_xT', banks=[0])
        xkeys = [('g_xT', i) for i in range(tpg)]
        if full:
            for h in range(GH):
                for kc in range(KC):
                    P.add('tensor', lambda e, h=h, kc=kc: e.matmul(
                        c.ps[0][:, 0:GW], lhsT=win[:, kc, h * 128:(h + 1) * 128], rhs=xT[:, kc, 0:GW],
                        start=(kc == 0), stop=(kc == KC - 1)),
                        reads=winkeys + xkeys, writes=[c.psk(0)])
                P.add('scalar', lambda e, h=h: e.activation(out=qT[:, h, 0:GW], in_=c.ps[0][:, 0:GW], func=ACTF.Copy,
                                                            scale=float(GDK ** -0.5)),
                      reads=[c.psk(0)], writes=[('g_qT', h)])
        for kc in range(KC):
            P.add('tensor', lambda e, kc=kc: e.matmul(
                c.ps[0][0:16, 0:GW], lhsT=win[:, kc, 3072:3088], rhs=xT[:, kc, 0:GW], start=(kc == 0), stop=(kc == KC - 1)),
                reads=winkeys + xkeys, writes=[c.psk(0)])
        P.add('vector', lambda e: e.tensor_copy(out=lowT[0:16, 0:GW], in_=c.ps[0][0:16, 0:GW]),
              reads=[c.psk(0)], writes=['g_lowT'])
        for ti, tt in enumerate(tts):
            tcol = slice(ti * 128, (ti + 1) * 128)
            for kc in range(KC):
                P.add('tensor', lambda e, kc=kc, tcol=tcol: e.matmul(
                    c.ps[1][:], lhsT=xT[:, kc, tcol], rhs=win[:, kc, 512:1024], start=(kc == 0), stop=(kc == KC - 1)),
                    reads=winkeys + [('g_xT', ti)], writes=[c.psk(1)])
            for hv in range(2):
                for kc in range(KC):
                    P.add('tensor', lambda e, kc=kc, hv=hv, tcol=tcol: e.matmul(
                        c.ps[2 + hv][:], lhsT=xT[:, kc, tcol], rhs=win[:, kc, 1024 + hv * 512:1024 + (hv + 1) * 512],
                        start=(kc == 0), stop=(kc == KC - 1)),
                        reads=winkeys + [('g_xT', ti)], writes=[c.psk(2 + hv)])
            if full:
                for hv in range(2):
                    for kc in range(KC):
                        P.add('tensor', lambda e, kc=kc, hv=hv, tcol=tcol: e.matmul(
                            c.ps[4 + hv][:], lhsT=xT[:, kc, tcol], rhs=win[:, kc, 2048 + hv * 512:2048 + (hv + 1) * 512],
                            start=(kc == 0), stop=(kc == KC - 1)),
                            reads=winkeys + [('g_xT', ti)], writes=[c.psk(4 + hv)])
            P.add('tensor', lambda e, tcol=tcol: e.matmul(c.ps[0][:], lhsT=lowT[:, tcol], rhs=wgk[:], start=True, stop=True),
                  reads=['g_lowT', 'g_wgk0', 'g_wgk1'], writes=[c.psk(0)])
            P.add('scalar', lambda e: e.activation(out=e_sb[:], in_=c.ps[0][:], func=ACTF.Exp, scale=-1.0),
                  reads=[c.psk(0)], writes=['g_e'])
            P.add('scalar', lambda e: e.activation(out=l_sb[:], in_=e_sb[:], func=ACTF.Ln, bias=c.oneb[:, 0:1]),
                  reads=['g_e', 'oneb'], writes=['g_l'])
            P.add('tensor', lambda e: e.matmul(c.ps[0][:], lhsT=c.t2s[:], rhs=l_sb[:], start=True, stop=True),
                  reads=['t2s', 'g_l'], writes=[c.psk(0)])
            P.add('scalar', lambda e: e.activation(out=ed[:], in_=c.ps[0][:], func=ACTF.Exp),
                  reads=[c.psk(0)], writes=['g_ed'])
            for h in range(GH):
                P.add('tensor', lambda e, h=h: e.matmul(c.ps[6][:, 2 * h:2 * h + 2], lhsT=l_sb[:, h * 128:(h + 1) * 128],
                                                        rhs=c.ind[:], start=True, stop=True),
                      reads=['g_l', 'ind'], writes=[c.psk(6)])
            P.add('scalar', lambda e: e.activation(out=decT[:], in_=c.ps[6][:, 0:8], func=ACTF.Exp),
                  reads=[c.psk(6)], writes=['g_decT'])
            P.add('vector', lambda e: e.tensor_tensor(out=kdec[:], in0=c.ps[1][:], in1=ed[:], op=ALU.mult),
                  reads=[c.psk(1), 'g_ed'], writes=['g_kdec'])
            for hv in range(2):
                P.add('scalar' if hv == 0 else 'vector',
                      (lambda e, hv=hv: e.copy(out=v_sb[:, hv * 512:(hv + 1) * 512], in_=c.ps[2 + hv][:])) if hv == 0 else
                      (lambda e, hv=hv: e.tensor_copy(out=v_sb[:, hv * 512:(hv + 1) * 512], in_=c.ps[2 + hv][:])),
                      reads=[c.psk(2 + hv)], writes=[('g_v', hv)])
            if full:
                for hv in range(2):
                    P.add('scalar', lambda e, hv=hv: e.activation(out=sr[:, hv * 512:(hv + 1) * 512], in_=c.ps[4 + hv][:], func=ACTF.Silu),
                          reads=[c.psk(4 + hv)], writes=[('g_sr', hv)])
            for cc in range(2):
                rows = slice(cc * 64, (cc + 1) * 64)
                for h in range(GH):
                    P.add('tensor', lambda e, h=h, rows=rows: e.matmul(
                        c.ps[2 + h // 2][:, (h % 2) * 256:(h % 2 + 1) * 256], lhsT=kdec[rows, h * 128:(h + 1) * 128],
                        rhs=v_sb[rows, h * 256:(h + 1) * 256], start=True, stop=True),
                        reads=['g_kdec', ('g_v', h // 2)], writes=[c.psk(2 + h // 2)])
                for h in range(GH):
                    P.add('vector', lambda e, h=h, cc=cc: e.scalar_tensor_tensor(
                        out=S[:, h, :], in0=S[:, h, :], scalar=decT[:, 2 * h + cc:2 * h + cc + 1],
                        in1=c.ps[2 + h // 2][:, (h % 2) * 256:(h % 2 + 1) * 256], op0=ALU.mult, op1=ALU.add),
                        reads=[('g_S', h), 'g_decT', c.psk(2 + h // 2)], writes=[('g_S', h)])
                if not full:
                    continue
                P.add('scalar', lambda e: e.copy(out=Sb[:], in_=S[:]), reads=[('g_S', h) for h in range(GH)], writes=['g_Sb'])
                for h in range(GH):
                    P.add('tensor', lambda e, h=h, tcol=tcol: e.matmul(
                        c.ps[4 + h // 2][:, (h % 2) * 256:(h % 2 + 1) * 256], lhsT=qT[:, h, tcol], rhs=Sb[:, h, :],
                        start=True, stop=True),
                        reads=[('g_qT', h), 'g_Sb'], writes=[c.psk(4 + h // 2)])
                for hv in range(2):
                    src = c.ps[4 + hv][rows, :].rearrange("p (a b) -> p a b", a=2)
                    dst = o_sb[rows, 2 * hv:2 * hv + 2, :]
                    if hv == 0:
                        P.add('vector', lambda e, src=src, dst=dst: e.tensor_copy(out=dst, in_=src),
                              reads=[c.psk(4 + hv)], writes=[('g_o', cc, hv)])
                    else:
                        P.add('scalar', lambda e, src=src, dst=dst: e.copy(out=dst, in_=src),
                              reads=[c.psk(4 + hv)], writes=[('g_o', cc, hv)])
            if not full:
                continue
            okeys = [('g_o', cc, hv) for cc in range(2) for hv in range(2)]
            for h in range(GH):
                P.add('scalar', lambda e, h=h: e.activation(out=junk[:], in_=o_sb[:, h, :], func=ACTF.Square,
                                                            accum_out=ss[:, h:h + 1]),
                      reads=okeys, writes=[('g_ss', h), 'g_junk'])
            P.add('scalar', lambda e: e.activation(out=ss[:, 4:8], in_=ss[:, 0:4], func=ACTF.Ln, bias=c.epsb[:, 1:2],
                                                   scale=1.0 / GDV),
                  reads=[('g_ss', h) for h in range(GH)] + ['epsb'], writes=['g_ss2'])
            P.add('scalar', lambda e: e.activation(out=ss[:, 4:8], in_=ss[:, 4:8], func=ACTF.Exp, scale=-0.5),
                  reads=['g_ss2'], writes=['g_ss2'])
            for h in range(GH):
                P.add('vector', lambda e, h=h: e.scalar_tensor_tensor(
                    out=o_sb[:, h, :], in0=o_sb[:, h, :], scalar=ss[:, 4 + h:5 + h], in1=ng[:], op0=ALU.mult, op1=ALU.mult),
                    reads=okeys + ['g_ss2', 'g_ng'], writes=[('g_on', h)])
            for hv in range(2):
                P.add('gpsimd' if hv == 0 else 'vector', lambda e, hv=hv: e.tensor_tensor(
                    out=gated[:, hv * 512:(hv + 1) * 512],
                    in0=o_sb[:, 2 * hv:2 * hv + 2, :].rearrange("p a b -> p (a b)"),
                    in1=sr[:, hv * 512:(hv + 1) * 512], op=ALU.mult),
                    reads=[('g_on', 2 * hv), ('g_on', 2 * hv + 1), ('g_sr', hv)], writes=[('g_gated', hv)])
            psb = c.ps[6][:].bitcast(BF16)
            for kc in range(8):
                P.add('tensor', lambda e, kc=kc: e.transpose(out=psb[:, kc * 128:(kc + 1) * 128],
                                                             in_=gated[:, kc * 128:(kc + 1) * 128], identity=c.identb[:]),
                      reads=[('g_gated', kc // 4), 'identb'], writes=[c.psk(6)])
            P.add('vector', lambda e: e.tensor_copy(out=gT[:].rearrange("p a b -> p (a b)"), in_=psb),
                  reads=[c.psk(6)], writes=['g_gT'])
            mb = [7, 1][:nh]
            for hh in range(nh):
                for kc in range(8):
                    P.add('tensor', lambda e, kc=kc, hh=hh: e.matmul(
                        c.ps[mb[hh]][:], lhsT=gT[:, kc, :], rhs=wout[:, kc, hh * 512:(hh + 1) * 512],
                        start=(kc == 0), stop=(kc == 7)),
                        reads=['g_gT', ('g_wout', kc)], writes=[c.psk(mb[hh])])
            ln_epilogue(c, mb, x_res, xkey, tt, lnp, b['lntmp'][tt % 2], ('glntmp', tt % 2))
    if w.get('f_out') is not None:
        P.add('sync', lambda e: e.dma_start(out=w['f_out'].rearrange("h p d -> p h d"), in_=S[:]), reads=[('g_S', h) for h in range(GH)], dma='g_fout')


def build_gla_test(dims, NT, full):
    c = Ctx(dims)
    Dm = dims['D']
    x_d = c.inp("x", [NT, Dm])
    w = dict(w_in=c.inp("w_in", [Dm, 3088]), w_gk=c.inp("w_gk", [16, 512]), b_gk=c.inp("b_gk", [512]),
             norm_g=c.inp("norm_g", [256]), w_out=c.inp("w_out", [1024, Dm]), s_init=c.inp("s_init", [4, 128, 256]),
             f_out=c.outp("f_out", [4, 128, 256]))
    lng = c.inp("ln_g", [Dm])
    lnb = c.inp("ln_b", [Dm])
    out = c.outp("out", [NT, Dm])
    load_consts(c)
    load_gla_consts(c)
    x_res = c.sb("x_res", [128, NT // 128, Dm], F32)
    load_x(c, x_d, x_res, 'x', NT)
    lnp = load_ln_params(c, "ln0", lng, lnb)
    bufs = alloc_gla_bufs(c, full)
    gla_pass(c, x_res, 'x', NT, full, w, bufs, lnp)
    store_x(c, out, x_res, 'x', NT)
    return finish(c)


def attn_consts_host():
    i = np.arange(128)[:, None]
    j = np.arange(512)[None, :]
    return np.where((j <= 127 - i) & (j < 128), NEG_BIG, 0.0).astype(np.float32)


def attn_kernel(c, SEQ, x3r_d, x4_d, wk_d, wv_d, wq_d, oT_d, negmask_d):
    P = c.P
    Dm = c.dm['D']
    KC = Dm // 128
    NB = SEQ // 128
    NG = SEQ // 512
    wk = c.sb("a_wk", [128, KC, 512], BF16)
    wv = c.sb("a_wv", [128, KC, 512], BF16)
    wq = c.sb("a_wq", [128, KC, 512], BF16)
    for nm, t, d in (('a_wk', wk, wk_d), ('a_wv', wv, wv_d), ('a_wq', wq, wq_d)):
        P.add('gpsimd', lambda e, t=t, d=d: e.dma_start(out=t[:], in_=d.rearrange("(kc p) n -> p kc n", p=128)),
              writes=[nm], dma=nm)
    negm = c.sb("a_negm", [128, 512], BF16)
    P.add('gpsimd', lambda e: e.dma_start(out=negm[:], in_=negmask_d), writes=['a_negm'], dma='a_negm')
    zeros = c.sb("a_zeros", [128, 512], F32)
    P.add('gpsimd', lambda e: e.memset(zeros[:], 0.0), writes=['a_zeros'])
    KT = c.sb("a_KT", [128, 4, SEQ], BF16)
    V = c.sb("a_V", [128, NB, 512], BF16)
    qT = c.sb("a_qT", [128, 4, SEQ], BF16)
    oT = c.sb("a_oT", [128, 4, SEQ], BF16)
    xs = c.sb("a_xs", [128, 4, Dm], F32)
    xTa = c.sb("a_xT", [128, KC, 512], BF16)
    for which in range(2):
        src_d = x3r_d if which == 0 else x4_d
        for g in range(NG):
            for ti in range(4):
                P.add('sync', lambda e, g=g, ti=ti, src_d=src_d: e.dma_start(
                    out=xs[:, ti, :], in_=src_d[(g * 4 + ti) * 128:(g * 4 + ti + 1) * 128, :]),
                    writes=[('a_xs', ti)], dma='a_xs')
            P.seal([('a_xs', ti) for ti in range(4)], 'a_xs')
            transposes_to_xT(c, xs, 'a_xs', [0, 1, 2, 3], xTa, 'a_xT', banks=[0, 1])
            xkeys = [('a_xT', i) for i in range(4)]
            gcol = slice(g * 512, (g + 1) * 512)
            bk = 0
            if which == 0:
                for hp in range(4):
                    bank = bk % 2
                    bk += 1
                    for kc in range(KC):
                        P.add('tensor', lambda e, hp=hp, kc=kc, bank=bank: e.matmul(
                            c.ps[bank][:], lhsT=wk[:, kc, hp * 128:(hp + 1) * 128], rhs=xTa[:, kc, :],
                            start=(kc == 0), stop=(kc == KC - 1)), reads=['a_wk'] + xkeys, writes=[c.psk(bank)])
                    P.add('scalar', lambda e, hp=hp, bank=bank, gcol=gcol: e.copy(out=KT[:, hp, gcol], in_=c.ps[bank][:]),
                          reads=[c.psk(bank)], writes=[('a_KT', hp, g)])
                for ti in range(4):
                    bank = bk % 2
                    bk += 1
                    for kc in range(KC):
                        P.add('tensor', lambda e, ti=ti, kc=kc, bank=bank: e.matmul(
                            c.ps[bank][:], lhsT=xTa[:, kc, ti * 128:(ti + 1) * 128], rhs=wv[:, kc, :],
                            start=(kc == 0), stop=(kc == KC - 1)), reads=['a_wv', ('a_xT', ti)], writes=[c.psk(bank)])
                    P.add('vector', lambda e, ti=ti, bank=bank, g=g: e.tensor_copy(out=V[:, g * 4 + ti, :], in_=c.ps[bank][:]),
                          reads=[c.psk(bank)], writes=[('a_V', g * 4 + ti)])
            else:
                for hp in range(4):
                    bank = bk % 2
                    bk += 1
                    for kc in range(KC):
                        P.add('tensor', lambda e, hp=hp, kc=kc, bank=bank: e.matmul(
                            c.ps[bank][:], lhsT=wq[:, kc, hp * 128:(hp + 1) * 128], rhs=xTa[:, kc, :],
                            start=(kc == 0), stop=(kc == KC - 1)), reads=['a_wq'] + xkeys, writes=[c.psk(bank)])
                    P.add('scalar', lambda e, hp=hp, bank=bank, gcol=gcol: e.activation(
                        out=qT[:, hp, gcol], in_=c.ps[bank][:], func=ACTF.Copy, scale=float(SBD ** -0.5)),
                        reads=[c.psk(bank)], writes=[('a_qT', hp, g)])
    NPB = 3
    pbs = [c.sb("a_pb%d" % i, [128, 513], F32) for i in range(NPB)]
    As = [c.sb("a_A%d" % i, [128, 512], BF16) for i in range(2)]
    ATs = [c.sb("a_AT%d" % i, [128, 512], BF16) for i in range(2)]
    tile_i = 0
    head_i = 0
    for qb in range(NB):
        r0 = 128 * (NB - 1 - qb)
        nk = 128 * (qb + 1)
        ntile = (nk + 511) // 512
        qcol = slice(qb * 128, (qb + 1) * 128)
        for h in range(8):
            hp, half = h // 2, h % 2
            prow = slice(half * 64, (half + 1) * 64)
            ob = 6 + head_i % 2
            head_i += 1
            prev_slot = None
            for kt in range(ntile):
                c0 = r0 + 512 * kt
                w = min(512, SEQ - c0)
                nblk = w // 128
                zb = 2 + tile_i % 2
                ab = 4 + tile_i % 2
                ps_ = tile_i % NPB
                as_ = tile_i % 2
                tile_i += 1
                pb, A, AT = pbs[ps_], As[as_], ATs[as_]
                kkeys = [('a_KT', hp, gg) for gg in range(c0 // 512, (c0 + w - 1) // 512 + 1)]
                P.add('tensor', lambda e, hp=hp, prow=prow, qcol=qcol, c0=c0, w=w, zb=zb, kt=kt: e.matmul(
                    c.ps[zb][:, 0:w], lhsT=qT[prow, hp, qcol], rhs=KT[prow, hp, c0:c0 + w], start=True, stop=(kt != 0)),
                    reads=[('a_qT', hp, qb // 4)] + kkeys, writes=[c.psk(zb)])
                if kt == 0:
                    P.add('tensor', lambda e, w=w, zb=zb: e.matmul(
                        c.ps[zb][:, 0:w], lhsT=c.identb[:], rhs=negm[:, 0:w], start=False, stop=True),
                        reads=['identb', 'a_negm'], writes=[c.psk(zb)])
                P.add('scalar', lambda e, pb=pb, zb=zb, w=w: e.activation(out=pb[:, 1:w + 1], in_=c.ps[zb][:, 0:w],
                                                                          func=ACTF.Sigmoid, scale=-1.0),
                      reads=[c.psk(zb)], writes=[('a_pb', ps_)])
                if kt == 0:
                    P.add('vector', lambda e, pb=pb: e.memset(pb[:, 0:1], 1.0), writes=[('a_pb0', ps_)],
                          reads=[])
                else:
                    ppb, pw = pbs[prev_slot[0]], prev_slot[1]
                    P.add('vector', lambda e, pb=pb, ppb=ppb, pw=pw: e.tensor_copy(out=pb[:, 0:1], in_=ppb[:, pw:pw + 1]),
                          reads=[('a_pb', prev_slot[0])], writes=[('a_pb0', ps_)])
                P.add('vector', lambda e, pb=pb, w=w: e.tensor_tensor_scan(
                    out=pb[:, 1:w + 1], data0=pb[:, 1:w + 1], data1=zeros[:, 0:w], initial=pb[:, 0:1],
                    op0=ALU.mult, op1=ALU.add),
                    reads=[('a_pb', ps_), ('a_pb0', ps_), 'a_zeros'], writes=[('a_pb', ps_)])
                P.add('gpsimd', lambda e, pb=pb, A=A, w=w: e.tensor_tensor(out=A[:, 0:w], in0=pb[:, 0:w], in1=pb[:, 1:w + 1],
                                                                          op=ALU.subtract),
                      reads=[('a_pb', ps_), ('a_pb0', ps_)], writes=[('a_A', as_)])
                psb = c.ps[ab][:].bitcast(BF16)
                for bi in range(nblk):
                    P.add('tensor', lambda e, bi=bi, A=A, psb=psb: e.transpose(
                        out=psb[:, bi * 128:(bi + 1) * 128], in_=A[:, bi * 128:(bi + 1) * 128], identity=c.identb[:]),
                        reads=[('a_A', as_), 'identb'], writes=[c.psk(ab)])
                P.add('scalar', lambda e, AT=AT, psb=psb, w=w: e.copy(out=AT[:, 0:w], in_=psb[:, 0:w]),
                      reads=[c.psk(ab)], writes=[('a_AT', as_)])
                for bi in range(nblk):
                    vt = c0 // 128 + bi
                    first = (kt == 0 and bi == 0)
                    last = (kt == ntile - 1 and bi == nblk - 1)
                    P.add('tensor', lambda e, bi=bi, vt=vt, hp=hp, AT=AT, ob=ob, first=first, last=last: e.matmul(
                        c.ps[ob][:, 0:128], lhsT=V[:, vt, hp * 128:(hp + 1) * 128], rhs=AT[:, bi * 128:(bi + 1) * 128],
                        start=first, stop=last),
                        reads=[('a_V', vt), ('a_AT', as_)], writes=[c.psk(ob)])
                prev_slot = (ps_, w)
            eng = 'vector' if half == 0 else 'scalar'
            if eng == 'vector':
                P.add('vector', lambda e, prow=prow, hp=hp, qcol=qcol, ob=ob: e.tensor_copy(
                    out=oT[prow, hp, qcol], in_=c.ps[ob][prow, 0:128]), reads=[c.psk(ob)], writes=[('a_oT', hp, qb, half)])
            else:
                P.add('scalar', lambda e, prow=prow, hp=hp, qcol=qcol, ob=ob: e.copy(
                    out=oT[prow, hp, qcol], in_=c.ps[ob][prow, 0:128]), reads=[c.psk(ob)], writes=[('a_oT', hp, qb, half)])
    for hp in range(4):
        P.add('sync', lambda e, hp=hp: e.dma_start(out=oT_d[hp, :, :], in_=oT[:, hp, :]),
              reads=[('a_oT', hp, qb, half) for qb in range(NB) for half in range(2)], dma='a_out')


def build_attn(dims, SEQ):
    c = Ctx(dims)
    Dm = dims['D']
    x3r = c.inp("x3r", [SEQ, Dm])
    x4 = c.inp("x4", [SEQ, Dm])
    wk = c.inp("wk", [Dm, 512])
    wv = c.inp("wv", [Dm, 512])
    wq = c.inp("wq", [Dm, 512])
    negm = c.inp("c_negm", [128, 512])
    oT = c.outp("oT", [4, 128, SEQ], BF16)
    load_consts(c)
    attn_kernel(c, SEQ, x3r, x4, wk, wv, wq, oT, negm)
    return finish(c)


FULL = dict(D=D, DFF=DFF)
NT_CORE = 2048
SEQ = 4096


def alloc_lnp(c, name="lnp"):
    Dm = c.dm['D']
    return (c.sb(name + "_g", [128, Dm], F32), c.sb(name + "_b", [128, Dm], F32), name + '_g', name + '_b')


def set_lnp(c, lnp, g_d, b_d):
    g, b, gk, bk = lnp
    c.P.add('sync', lambda e: e.dma_start(out=g[:], in_=g_d.partition_broadcast(128)), writes=[gk], dma='lnp')
    c.P.add('sync', lambda e: e.dma_start(out=b[:], in_=b_d.partition_broadcast(128)), writes=[bk], dma='lnp')
    c.P.seal([gk, bk], 'lnp')


def gla_weight_inputs(c, Dm):
    return dict(w_in=c.inp("gla_w_in", [Dm, 3088]), w_gk=c.inp("gla_w_gk", [16, 512]), b_gk=c.inp("gla_b_gk", [512]),
                norm_g=c.inp("gla_norm_g", [256]), w_out=c.inp("gla_w_out", [1024, Dm]))


def build_L1(dims=FULL, NT=NT_CORE):
    c = Ctx(dims)
    Dm, Dff = dims['D'], dims['DFF']
    x_d = c.inp("x", [NT, Dm])
    wup = c.inp("w_up", [Dm, 2 * Dff])
    wdn = c.inp("w_dn", [Dff, Dm])
    lng = c.inp("ln_g", [Dm])
    lnb = c.inp("ln_b", [Dm])
    w = gla_weight_inputs(c, Dm)
    w['s_init'] = None
    w['f_out'] = c.outp("f_out", [4, 128, 256])
    x1 = c.outp("x1", [NT, Dm])
    load_consts(c)
    load_gla_consts(c)
    x_res = c.sb("x_res", [128, NT // 128, Dm], F32)
    lnp = alloc_lnp(c)
    load_x(c, x_d, x_res, 'x', NT)
    set_lnp(c, lnp, lng, lnb)
    m = c.mark()
    bufs = alloc_ffn_bufs(c, NT)
    ffn_sublayer(c, x_res, 'x', NT, wup, wdn, lnp, bufs)
    store_x(c, x1, x_res, 'x', NT)
    c.release(m)
    gb = alloc_gla_bufs(c, False)
    gla_pass(c, x_res, 'x', NT, False, w, gb, None)
    return finish(c)


def build_L2(dims=FULL, NT=NT_CORE):
    c = Ctx(dims)
    Dm, Dff = dims['D'], dims['DFF']
    x_d = c.inp("x1", [NT, Dm])
    w = gla_weight_inputs(c, Dm)
    w['s_init'] = c.inp("s_init", [4, 128, 256])
    w['f_out'] = None
    lng = [c.inp("ln_g%d" % i, [Dm]) for i in range(3)]
    lnb = [c.inp("ln_b%d" % i, [Dm]) for i in range(3)]
    wup = [c.inp("w_up%d" % i, [Dm, 2 * Dff]) for i in range(2)]
    wdn = [c.inp("w_dn%d" % i, [Dff, Dm]) for i in range(2)]
    x3 = c.outp("x3", [NT, Dm])
    x4 = c.outp("x4", [NT, Dm])
    load_consts(c)
    load_gla_consts(c)
    x_res = c.sb("x_res", [128, NT // 128, Dm], F32)
    lnp = alloc_lnp(c)
    load_x(c, x_d, x_res, 'x', NT)
    set_lnp(c, lnp, lng[0], lnb[0])
    m = c.mark()
    gb = alloc_gla_bufs(c, True)
    gla_pass(c, x_res, 'x', NT, True, w, gb, lnp)
    c.release(m)
    bufs = alloc_ffn_bufs(c, NT)
    set_lnp(c, lnp, lng[1], lnb[1])
    ffn_sublayer(c, x_res, 'x', NT, wup[0], wdn[0], lnp, bufs)
    store_x(c, x3, x_res, 'x', NT, tag='stx3')
    set_lnp(c, lnp, lng[2], lnb[2])
    ffn_sublayer(c, x_res, 'x', NT, wup[1], wdn[1], lnp, bufs)
    store_x(c, x4, x_res, 'x', NT, tag='stx4')
    return finish(c)


def build_L4(dims=FULL, NT=NT_CORE):
    c = Ctx(dims)
    P = c.P
    Dm, Dff = dims['D'], dims['DFF']
    nh = Dm // 512
    x_d = c.inp("x4", [NT, Dm])
    oT_d = c.inp("oT", [1024, NT], BF16)
    wo_d = c.inp("sb_w_out", [1024, Dm])
    lng = [c.inp("ln_g%d" % i, [Dm]) for i in range(2)]
    lnb = [c.inp("ln_b%d" % i, [Dm]) for i in range(2)]
    wup = c.inp("w_up", [Dm, 2 * Dff])
    wdn = c.inp("w_dn", [Dff, Dm])
    out = c.outp("out", [NT, Dm])
    load_consts(c)
    x_res = c.sb("x_res", [128, NT // 128, Dm], F32)
    lnp = alloc_lnp(c)
    load_x(c, x_d, x_res, 'x', NT)
    set_lnp(c, lnp, lng[0], lnb[0])
    m = c.mark()
    oTs = c.sb("oTs", [128, 8, NT], BF16)
    wo = c.sb("wo", [128, 8, Dm], BF16)
    lntmp = alloc_ln_tmp(c, "mln")
    for kc in range(8):
        P.add('sync', lambda e, kc=kc: e.dma_start(out=oTs[:, kc, :], in_=oT_d[kc * 128:(kc + 1) * 128, :]),
              writes=[('oTs', kc)], dma='oTs')
        P.add('gpsimd', lambda e, kc=kc: e.dma_start(out=wo[:, kc, :], in_=wo_d[kc * 128:(kc + 1) * 128, :]),
              writes=[('wo', kc)], dma='wo')
    P.seal([('oTs', kc) for kc in range(8)], 'oTs')
    P.seal([('wo', kc) for kc in range(8)], 'wo')
    for tt in range(NT // 128):
        par = tt % 2
        banks2 = [4 * par + hh for hh in range(nh)]
        for hh in range(nh):
            for kc in range(8):
                P.add('tensor', lambda e, kc=kc, hh=hh, tt=tt, banks2=banks2: e.matmul(
                    c.ps[banks2[hh]][:], lhsT=oTs[:, kc, tt * 128:(tt + 1) * 128], rhs=wo[:, kc, hh * 512:(hh + 1) * 512],
                    start=(kc == 0), stop=(kc == 7)),
                    reads=[('oTs', kc), ('wo', kc)], writes=[c.psk(banks2[hh])])
        ln_epilogue(c, banks2, x_res, 'x', tt, lnp, lntmp[par], ('mlntmp', par))
    c.release(m)
    bufs = alloc_ffn_bufs(c, NT)
    set_lnp(c, lnp, lng[1], lnb[1])
    ffn_sublayer(c, x_res, 'x', NT, wup, wdn, lnp, bufs)
    store_x(c, out, x_res, 'x', NT)
    return finish(c)


_CACHE = {}


def _prog(name, builder):
    if name not in _CACHE:
        _CACHE[name] = builder()
    return _CACHE[name]


def _run(nc, in_maps):
    res = run_bass_kernel_spmd(nc, in_maps, core_ids=list(range(len(in_maps))))
    return res.results


def kernel_multi(x, ln_g, ln_b, ffn_w_up, ffn_w_down, gla_w_in, gla_w_gk, gla_b_gk, gla_norm_g, gla_w_out,
                 sb_w_kv, sb_w_q, sb_w_out):
    f = lambda a: np.ascontiguousarray(np.asarray(a, dtype=np.float32))
    x, ln_g, ln_b, ffn_w_up, ffn_w_down = f(x), f(ln_g), f(ln_b), f(ffn_w_up), f(ffn_w_down)
    gla_w_in, gla_w_gk, gla_b_gk, gla_norm_g, gla_w_out = f(gla_w_in), f(gla_w_gk), f(gla_b_gk), f(gla_norm_g), f(gla_w_out)
    sb_w_kv, sb_w_q, sb_w_out = f(sb_w_kv), f(sb_w_q), f(sb_w_out)
    B = x.shape[0]
    ident = np.eye(128, dtype=np.float32)
    t2s, ind = gla_consts_host()
    negm = attn_consts_host()
    cores = [(b, h) for b in range(B) for h in range(2)]
    glaw = {"gla_w_in": gla_w_in[0], "gla_w_gk": gla_w_gk[0], "gla_b_gk": gla_b_gk[0], "gla_norm_g": gla_norm_g[0],
            "gla_w_out": gla_w_out[0], "c_ident": ident, "c_t2s": t2s, "c_ind": ind}
    ims = []
    for (b, h) in cores:
        d = dict(glaw)
        d.update({"x": f(x[b, h * NT_CORE:(h + 1) * NT_CORE]), "w_up": ffn_w_up[0, 0], "w_dn": ffn_w_down[0, 0],
                  "ln_g": ln_g[0, 0], "ln_b": ln_b[0, 0]})
        ims.append(d)
    r1 = _run(_prog("L1", build_L1), ims)
    ims = []
    zero_state = np.zeros((4, 128, 256), np.float32)
    for i, (b, h) in enumerate(cores):
        d = dict(glaw)
        d.update({"x1": r1[i]["x1"], "s_init": zero_state if h == 0 else r1[i - 1]["f_out"],
                  "ln_g0": ln_g[0, 1], "ln_b0": ln_b[0, 1], "ln_g1": ln_g[0, 2], "ln_b1": ln_b[0, 2],
                  "ln_g2": ln_g[1, 0], "ln_b2": ln_b[1, 0],
                  "w_up0": ffn_w_up[0, 1], "w_dn0": ffn_w_down[0, 1], "w_up1": ffn_w_up[1, 0], "w_dn1": ffn_w_down[1, 0]})
        ims.append(d)
    r2 = _run(_prog("L2", build_L2), ims)
    ims = []
    for (b, hh) in cores:
        x3 = np.concatenate([r2[2 * b]["x3"], r2[2 * b + 1]["x3"]], axis=0)
        x4 = np.concatenate([r2[2 * b]["x4"], r2[2 * b + 1]["x4"]], axis=0)
        ims.append({"x3r": np.ascontiguousarray(x3[::-1]), "x4": x4,
                    "wk": f(sb_w_kv[:, hh * 512:(hh + 1) * 512]), "wv": f(sb_w_kv[:, 1024 + hh * 512:1024 + (hh + 1) * 512]),
                    "wq": f(sb_w_q[0][:, hh * 512:(hh + 1) * 512]), "c_ident": ident, "c_negm": negm})
    r3 = _run(_prog("L3", lambda: build_attn(FULL, SEQ)), ims)
    ims = []
    for i, (b, h) in enumerate(cores):
        o0 = np.asarray(r3[2 * b]["oT"]).reshape(512, SEQ)
        o1 = np.asarray(r3[2 * b + 1]["oT"]).reshape(512, SEQ)
        oT = np.ascontiguousarray(np.concatenate([o0, o1], axis=0)[:, h * NT_CORE:(h + 1) * NT_CORE])
        ims.append({"x4": r2[i]["x4"], "oT": oT, "sb_w_out": sb_w_out[0], "ln_g0": ln_g[1, 1], "ln_b0": ln_b[1, 1],
                    "ln_g1": ln_g[1, 2], "ln_b1": ln_b[1, 2], "w_up": ffn_w_up[1, 1], "w_dn": ffn_w_down[1, 1],
                    "c_ident": ident})
    r4 = _run(_prog("L4", build_L4), ims)
    out = np.empty((B, SEQ, D), np.float32)
    for i, (b, h) in enumerate(cores):
        out[b, h * NT_CORE:(h + 1) * NT_CORE] = r4[i]["out"]
    return out


def kv_phase(c, x_res, xkey, hf, wkv_d, KT_d, V_d, flag, NT=NT_CORE):
    P = c.P
    Dm = c.dm['D']
    KC = Dm // 128
    m = c.mark()
    wkv = c.sb("kv_w", [128, KC, 2048], BF16)
    for kc in range(KC):
        P.add('gpsimd', lambda e, kc=kc: e.dma_start(out=wkv[:, kc, :], in_=wkv_d[kc * 128:(kc + 1) * 128, :]),
              writes=[('kv_w', kc)], dma='kv_w')
    P.seal([('kv_w', kc) for kc in range(KC)], 'kv_w')
    wkeys = [('kv_w', kc) for kc in range(KC)]
    xTr = [c.sb("kv_xT%d" % i, [128, KC, 512], BF16) for i in range(2)]
    kst = [c.sb("kv_ks%d" % i, [128, 512], BF16) for i in range(2)]
    vst = [c.sb("kv_vs%d" % i, [128, 1024], BF16) for i in range(2)]
    ntile = NT // 128
    cnt = 0
    for g in range(ntile // 4):
        xT = xTr[g % 2]
        for i in range(4):
            tt = 4 * g + 3 - i
            for k0 in range(0, KC, 4):
                bank = cnt % 2
                cnt += 1
                for kk in range(4):
                    kc = k0 + kk
                    P.add('tensor', lambda e, bank=bank, kk=kk, kc=kc, tt=tt: e.matmul(
                        c.ps[bank][:, kk * 128:(kk + 1) * 128], lhsT=x_res[:, tt, kc * 128:(kc + 1) * 128], rhs=c.antiid[:],
                        start=True, stop=True), reads=[(xkey, tt), 'antiid'], writes=[c.psk(bank)])
                src = c.ps[bank][:].rearrange("p (a b) -> p a b", a=4)
                dst = xT[:, k0:k0 + 4, i * 128:(i + 1) * 128]
                if cnt % 2 == 0:
                    P.add('scalar', lambda e, src=src, dst=dst: e.copy(out=dst, in_=src), reads=[c.psk(bank)],
                          writes=[('kv_xT', g % 2, i)])
                else:
                    P.add('vector', lambda e, src=src, dst=dst: e.tensor_copy(out=dst, in_=src), reads=[c.psk(bank)],
                          writes=[('kv_xT', g % 2, i)])
        xk = [('kv_xT', g % 2, i) for i in range(4)]
        vt0 = 31 - (hf * 16 + 4 * g + 3)
        for hp in range(8):
            bank = 2 + hp % 2
            ks = kst[hp % 2]
            for kc in range(KC):
                P.add('tensor', lambda e, hp=hp, kc=kc, bank=bank, xT=xT: e.matmul(
                    c.ps[bank][:], lhsT=wkv[:, kc, hp * 128:(hp + 1) * 128], rhs=xT[:, kc, :],
                    start=(kc == 0), stop=(kc == KC - 1)), reads=wkeys + xk, writes=[c.psk(bank)])
            P.add('scalar', lambda e, ks=ks, bank=bank: e.copy(out=ks[:], in_=c.ps[bank][:]), reads=[c.psk(bank)],
                  writes=[('kv_ks', hp % 2)])
            P.add('sync', lambda e, ks=ks, hp=hp, vt0=vt0: e.dma_start(out=KT_d[hp, :, vt0 * 128:vt0 * 128 + 512], in_=ks[:]),
                  reads=[('kv_ks', hp % 2)], dma='kv_kst%d' % (hp % 2))
        for i in range(4):
            vs = vst[i % 2]
            for hv in range(2):
                bank = 4 + 2 * (i % 2) + hv
                for kc in range(KC):
                    P.add('tensor', lambda e, i=i, hv=hv, kc=kc, bank=bank, xT=xT: e.matmul(
                        c.ps[bank][:], lhsT=xT[:, kc, i * 128:(i + 1) * 128], rhs=wkv[:, kc, 1024 + hv * 512:1024 + (hv + 1) * 512],
                        start=(kc == 0), stop=(kc == KC - 1)), reads=wkeys + [('kv_xT', g % 2, i)], writes=[c.psk(bank)])
                if hf == 0:
                    P.add('scalar', lambda e, vs=vs, hv=hv, bank=bank: e.activation(
                        out=vs[:, hv * 512:(hv + 1) * 512], in_=c.ps[bank][:], func=ACTF.Copy, scale=flag[:, 0:1]),
                        reads=[c.psk(bank), 'flag'], writes=[('kv_vs', i % 2, hv)])
                else:
                    P.add('vector', lambda e, vs=vs, hv=hv, bank=bank: e.tensor_copy(
                        out=vs[:, hv * 512:(hv + 1) * 512], in_=c.ps[bank][:]),
                        reads=[c.psk(bank)], writes=[('kv_vs', i % 2, hv)])
            P.add('sync', lambda e, vs=vs, i=i, vt0=vt0: e.dma_start(
                out=V_d[:, :, vt0 + i, :].rearrange("hp p d -> p hp d"), in_=vs[:].rearrange("p (a b) -> p a b", a=8)),
                reads=[('kv_vs', i % 2, 0), ('kv_vs', i % 2, 1)], dma='kv_vst%d' % (i % 2))
    c.release(m)


def attn_phase(c, x_res, xkey, wq_d, wo_d, KT_d, V_d, negm, lnp, NT=NT_CORE):
    P = c.P
    Dm = c.dm['D']
    KC = Dm // 128
    nh = Dm // 512
    NBQ = NT // 128
    NB = SEQ // 128
    qb0 = NB - NBQ
    m = c.mark()
    qT = c.sb("a_qT", [128, 8, NT], BF16)
    oT = c.sb("a_oT", [128, 8, NT], BF16)
    m2 = c.mark()
    wq = c.sb("a_wq", [128, KC, 1024], BF16)
    for kc in range(KC):
        P.add('gpsimd', lambda e, kc=kc: e.dma_start(out=wq[:, kc, :], in_=wq_d[kc * 128:(kc + 1) * 128, :]),
              writes=[('a_wq', kc)], dma='a_wq')
    P.seal([('a_wq', kc) for kc in range(KC)], 'a_wq')
    wkeys = [('a_wq', kc) for kc in range(KC)]
    xTa = [c.sb("a_xT%d" % i, [128, KC, 512], BF16) for i in range(2)]
    for g in range(NT // 512):
        xT = xTa[g % 2]
        transposes_to_xT(c, x_res, xkey, list(range(4 * g, 4 * g + 4)), xT, ('a_xT', g % 2), banks=[0, 1])
        xk = [(('a_xT', g % 2), i) for i in range(4)]
        for hp in range(8):
            bank = 2 + hp % 4
            for kc in range(KC):
                P.add('tensor', lambda e, hp=hp, kc=kc, bank=bank, xT=xT: e.matmul(
                    c.ps[bank][:], lhsT=wq[:, kc, hp * 128:(hp + 1) * 128], rhs=xT[:, kc, :],
                    start=(kc == 0), stop=(kc == KC - 1)), reads=wkeys + xk, writes=[c.psk(bank)])
            P.add('scalar', lambda e, hp=hp, bank=bank, g=g: e.activation(
                out=qT[:, hp, g * 512:(g + 1) * 512], in_=c.ps[bank][:], func=ACTF.Copy, scale=float(SBD ** -0.5)),
                reads=[c.psk(bank)], writes=[('a_qT', hp)])
    c.release(m2)
    zeros = c.sb("a_zeros", [128, 512], F32)
    P.add('gpsimd', lambda e: e.memset(zeros[:], 0.0), writes=['a_zeros'])
    KTb = [c.sb("a_KT%d" % i, [128, SEQ], BF16) for i in range(2)]
    Vb = [c.sb("a_V%d" % i, [128, NB, 128], BF16) for i in range(2)]
    NPB, NA, NAT = 4, 6, 3
    pbs = [c.sb("a_pb%d" % i, [128, 513], F32) for i in range(NPB)]
    As = [c.sb("a_A%d" % i, [128, 512], BF16) for i in range(NA)]
    ATs = [c.sb("a_AT%d" % i, [128, 512], BF16) for i in range(NAT)]
    tiles = []
    head_i = 0
    for hp in range(8):
        for qbl in range(NBQ):
            qb = qb0 + qbl
            r0 = 128 * (NB - 1 - qb)
            nk = 128 * (qb + 1)
            ntile = (nk + 511) // 512
            for half in range(2):
                for kt in range(ntile):
                    c0 = r0 + 512 * kt
                    tiles.append(dict(hp=hp, qbl=qbl, half=half, kt=kt, ntile=ntile, c0=c0, w=min(512, SEQ - c0),
                                      head_i=head_i, idx=len(tiles)))
                head_i += 1

    def load_kv(hp):
        s = hp % 2
        P.add('sync', lambda e, hp=hp, s=s: e.dma_start(out=KTb[s][:], in_=KT_d[hp, :, :]), writes=[('a_KT', s)], dma='a_ldk%d' % s)
        P.add('sync', lambda e, hp=hp, s=s: e.dma_start(out=Vb[s][:], in_=V_d[hp, :, :, :]), writes=[('a_V', s)], dma='a_ldv%d' % s)

    def stage_A(t):
        hp, half, kt, c0, w, i = t['hp'], t['half'], t['kt'], t['c0'], t['w'], t['idx']
        s = hp % 2
        prow = slice(half * 64, (half + 1) * 64)
        qcol = slice(t['qbl'] * 128, (t['qbl'] + 1) * 128)
        zb = 2 + i % 2
        ps_ = i % NPB
        as_ = i % NA
        pb, A = pbs[ps_], As[as_]
        P.add('tensor', lambda e: e.matmul(c.ps[zb][:, 0:w], lhsT=qT[prow, hp, qcol], rhs=KTb[s][prow, c0:c0 + w],
                                           start=True, stop=(kt != 0)),
              reads=[('a_qT', hp), ('a_KT', s)], writes=[c.psk(zb)])
        if kt == 0:
            P.add('tensor', lambda e: e.matmul(c.ps[zb][:, 0:w], lhsT=c.identb[:], rhs=negm[:, 0:w], start=False, stop=True),
                  reads=['identb', 'a_negm'], writes=[c.psk(zb)])
        P.add('scalar', lambda e: e.activation(out=pb[:, 1:w + 1], in_=c.ps[zb][:, 0:w], func=ACTF.Sigmoid, scale=-1.0),
              reads=[c.psk(zb)], writes=[('a_pb', ps_)])
        if kt == 0:
            P.add('vector', lambda e: e.memset(pb[:, 0:1], 1.0), writes=[('a_pb0', ps_)])
        else:
            pps = (i - 1) % NPB
            ppb, pw = pbs[pps], tiles[i - 1]['w']
            P.add('vector', lambda e: e.tensor_copy(out=pb[:, 0:1], in_=ppb[:, pw:pw + 1]),
                  reads=[('a_pb', pps)], writes=[('a_pb0', ps_)])
        P.add('vector', lambda e: e.tensor_tensor_scan(out=pb[:, 2:w + 1:2], data0=pb[:, 1:w + 1:2], data1=pb[:, 2:w + 1:2],
                                                       initial=pb[:, 0:1], op0=ALU.mult, op1=ALU.mult),
              reads=[('a_pb', ps_), ('a_pb0', ps_)], writes=[('a_pb', ps_)])
        P.add('vector', lambda e: e.tensor_tensor(out=pb[:, 1:w + 1:2], in0=pb[:, 0:w:2], in1=pb[:, 1:w + 1:2], op=ALU.mult),
              reads=[('a_pb', ps_), ('a_pb0', ps_)], writes=[('a_pb', ps_)])
        P.add('gpsimd', lambda e: e.tensor_tensor(out=A[:, 0:w], in0=pb[:, 0:w], in1=pb[:, 1:w + 1], op=ALU.subtract),
              reads=[('a_pb', ps_), ('a_pb0', ps_)], writes=[('a_A', as_)])

    def stage_B(t):
        w, i = t['w'], t['idx']
        ab = 4 + i % 2
        as_ = i % NA
        at_ = i % NAT
        A, AT = As[as_], ATs[at_]
        psb = c.ps[ab][:].bitcast(BF16)
        for bi in range(w // 128):
            P.add('tensor', lambda e, bi=bi: e.transpose(out=psb[:, bi * 128:(bi + 1) * 128],
                                                         in_=A[:, bi * 128:(bi + 1) * 128], identity=c.identb[:]),
                  reads=[('a_A', as_), 'identb'], writes=[c.psk(ab)])
        P.add('scalar', lambda e: e.copy(out=AT[:, 0:w], in_=psb[:, 0:w]), reads=[c.psk(ab)], writes=[('a_AT', at_)])

    def stage_C(t):
        hp, half, kt, c0, w, i = t['hp'], t['half'], t['kt'], t['c0'], t['w'], t['idx']
        s = hp % 2
        at_ = i % NAT
        AT = ATs[at_]
        ob = 6 + t['head_i'] % 2
        nblk = w // 128
        for bi in range(nblk):
            vt = c0 // 128 + bi
            first = (kt == 0 and bi == 0)
            last = (kt == t['ntile'] - 1 and bi == nblk - 1)
            P.add('tensor', lambda e, bi=bi, vt=vt, first=first, last=last: e.matmul(
                c.ps[ob][:, 0:128], lhsT=Vb[s][:, vt, :], rhs=AT[:, bi * 128:(bi + 1) * 128], start=first, stop=last),
                reads=[('a_V', s), ('a_AT', at_)], writes=[c.psk(ob)])
        if kt == t['ntile'] - 1:
            prow = slice(half * 64, (half + 1) * 64)
            qcol = slice(t['qbl'] * 128, (t['qbl'] + 1) * 128)
            if half == 0:
                P.add('vector', lambda e: e.tensor_copy(out=oT[prow, hp, qcol], in_=c.ps[ob][prow, 0:128]),
                      reads=[c.psk(ob)], writes=[('a_oT', hp, t['qbl'], half)])
            else:
                P.add('scalar', lambda e: e.copy(out=oT[prow, hp, qcol], in_=c.ps[ob][prow, 0:128]),
                      reads=[c.psk(ob)], writes=[('a_oT', hp, t['qbl'], half)])

    n = len(tiles)
    DB, DC = 3, 4
    load_kv(0)
    for s_ in range(n + DC):
        if s_ < n:
            t = tiles[s_]
            if t['qbl'] == 0 and t['half'] == 0 and t['kt'] == 0 and t['hp'] + 1 < 8:
                load_kv(t['hp'] + 1)
            stage_A(t)
        if 0 <= s_ - DB < n:
            stage_B(tiles[s_ - DB])
        if 0 <= s_ - DC < n:
            stage_C(tiles[s_ - DC])
    c.release(m2)
    wo = c.sb("a_wo", [128, 8, Dm], BF16)
    lntmp = alloc_ln_tmp(c, "aln")
    for kc in range(8):
        P.add('gpsimd', lambda e, kc=kc: e.dma_start(out=wo[:, kc, :], in_=wo_d[kc * 128:(kc + 1) * 128, :]),
              writes=[('a_wo', kc)], dma='a_wo')
    P.seal([('a_wo', kc) for kc in range(8)], 'a_wo')
    for tt in range(NT // 128):
        par = tt % 2
        banks2 = [4 * par + hh for hh in range(nh)]
        for hh in range(nh):
            for kc in range(8):
                P.add('tensor', lambda e, kc=kc, hh=hh, tt=tt, banks2=banks2: e.matmul(
                    c.ps[banks2[hh]][:], lhsT=oT[:, kc, tt * 128:(tt + 1) * 128], rhs=wo[:, kc, hh * 512:(hh + 1) * 512],
                    start=(kc == 0), stop=(kc == 7)),
                    reads=[('a_wo', kc)], writes=[c.psk(banks2[hh])])
        ln_epilogue(c, banks2, x_res, xkey, tt, lnp, lntmp[par], ('alntmp', par))
    c.release(m)


def build_fused(dims=FULL, NT=NT_CORE):
    c = Ctx(dims)
    P, nc = c.P, c.nc
    Dm, Dff = dims['D'], dims['DFF']
    x_in = c.inp("x_in", [2 * NT, Dm])
    flag_d = c.inp("flag", [128, 1])
    ln_g = c.inp("ln_g", [DEPTH, 3, Dm])
    ln_b = c.inp("ln_b", [DEPTH, 3, Dm])
    wup = c.inp("ffn_w_up", [DEPTH, 2, Dm, 2 * Dff])
    wdn = c.inp("ffn_w_down", [DEPTH, 2, Dff, Dm])
    w = gla_weight_inputs(c, Dm)
    wkv_d = c.inp("sb_w_kv", [Dm, 2048])
    wq_d = c.inp("sb_w_q", [Dm, 1024])
    wo_d = c.inp("sb_w_out", [1024, Dm])
    negm_d = c.inp("c_negm", [128, 512])
    anti_d = c.inp("c_antiid", [128, 128])
    out = c.outp("out", [NT, Dm])
    KT_d = nc.dram_tensor("kt_scratch", [8, 128, SEQ], BF16).ap()
    V_d = nc.dram_tensor("v_scratch", [8, 128, SEQ // 128, 128], BF16).ap()
    load_consts(c)
    load_gla_consts(c)
    flag = c.sb("flag", [128, 1], F32)
    P.add('sync', lambda e: e.dma_start(out=flag[:], in_=flag_d), writes=['flag'], dma='c4')
    c.antiid = c.sb("antiid", [128, 128], F32)
    P.add('sync', lambda e: e.dma_start(out=c.antiid[:], in_=anti_d), writes=['antiid'], dma='c5')
    negm = c.sb("a_negm", [128, 512], BF16)
    P.add('gpsimd', lambda e: e.dma_start(out=negm[:], in_=negm_d), writes=['a_negm'], dma='c6')
    S = c.sb("g_S", [128, 4, 256], F32)
    P.add('vector', lambda e: e.memset(S[:], 0.0), writes=[('g_S', h) for h in range(GH)])
    x_res = c.sb("x_res", [128, NT // 128, Dm], F32)
    lnp = alloc_lnp(c)
    c.P.barrier()
    base = c.mark()

    def ffn(layer, idx):
        m = c.mark()
        bufs = alloc_ffn_bufs(c, NT)
        set_lnp(c, lnp, ln_g[layer, 2 * idx, :], ln_b[layer, 2 * idx, :])
        ffn_sublayer(c, x_res, 'x', NT, wup[layer, idx], wdn[layer, idx], lnp, bufs)
        c.release(m)

    for hf in range(2):
        load_x(c, x_in[hf * NT:(hf + 1) * NT, :], x_res, 'x', NT)
        ffn(0, 0)
        m = c.mark()
        gb = alloc_gla_bufs(c, True, S=S)
        set_lnp(c, lnp, ln_g[0, 1, :], ln_b[0, 1, :])
        if hf == 1:
            for h in range(GH):
                P.add('vector', lambda e, h=h: e.tensor_scalar(out=S[:, h, :], in0=S[:, h, :], scalar1=flag[:, 0:1], scalar2=None,
                                                              op0=ALU.mult), reads=[('g_S', h), 'flag'], writes=[('g_S', h)])
        ww = dict(w)
        ww['s_init'] = 'keep'
        ww['f_out'] = None
        gla_pass(c, x_res, 'x', NT, True, ww, gb, lnp)
        c.release(m)
        ffn(0, 1)
        kv_phase(c, x_res, 'x', hf, wkv_d, KT_d, V_d, flag, NT)
    ffn(1, 0)
    set_lnp(c, lnp, ln_g[1, 1, :], ln_b[1, 1, :])
    attn_phase(c, x_res, 'x', wq_d, wo_d, KT_d, V_d, negm, lnp, NT)
    ffn(1, 1)
    store_x(c, out, x_res, 'x', NT)
    return finish(c)


def kernel_fused(x, ln_g, ln_b, ffn_w_up, ffn_w_down, gla_w_in, gla_w_gk, gla_b_gk, gla_norm_g, gla_w_out,
                 sb_w_kv, sb_w_q, sb_w_out):
    f = lambda a: np.ascontiguousarray(np.asarray(a, dtype=np.float32))
    x = f(x)
    B = x.shape[0]
    t2s, ind = gla_consts_host()
    common = {"ln_g": f(ln_g), "ln_b": f(ln_b), "ffn_w_up": f(ffn_w_up), "ffn_w_down": f(ffn_w_down),
              "gla_w_in": f(gla_w_in[0]), "gla_w_gk": f(gla_w_gk[0]), "gla_b_gk": f(gla_b_gk[0]),
              "gla_norm_g": f(gla_norm_g[0]), "gla_w_out": f(gla_w_out[0]), "sb_w_kv": f(sb_w_kv),
              "sb_w_q": f(sb_w_q[0]), "sb_w_out": f(sb_w_out[0]), "c_ident": np.eye(128, dtype=np.float32),
              "c_t2s": t2s, "c_ind": ind, "c_negm": attn_consts_host(),
              "c_antiid": np.ascontiguousarray(np.eye(128, dtype=np.float32)[::-1])}
    cores = [(b, h) for b in range(B) for h in range(2)]
    ims = []
    for (b, h) in cores:
        d = dict(common)
        d["x_in"] = np.ascontiguousarray(np.concatenate([x[b, :NT_CORE], x[b, h * NT_CORE:(h + 1) * NT_CORE]], axis=0))
        d["flag"] = np.full((128, 1), float(h), np.float32)
        ims.append(d)
    r = _run(_prog("FUSED", build_fused), ims)
    out = np.empty((B, SEQ, D), np.float32)
    for i, (b, h) in enumerate(cores):
        out[b, h * NT_CORE:(h + 1) * NT_CORE] = r[i]["out"]
    return out


def kernel(x, ln_g, ln_b, ffn_w_up, ffn_w_down, gla_w_in, gla_w_gk, gla_b_gk, gla_norm_g, gla_w_out,
           sb_w_kv, sb_w_q, sb_w_out):
    return kernel_fused(x, ln_g, ln_b, ffn_w_up, ffn_w_down, gla_w_in, gla_w_gk, gla_b_gk, gla_norm_g, gla_w_out,
                        sb_w_kv, sb_w_q, sb_w_out)
```

```python
from contextlib import ExitStack
import numpy as np
import ml_dtypes
import concourse.bass as bass
import concourse.mybir as mybir
from concourse.bass_utils import run_bass_kernel_spmd

F32 = mybir.dt.float32
BF16 = mybir.dt.bfloat16
ALU = mybir.AluOpType
ACTF = mybir.ActivationFunctionType

ENGS = ['sync', 'tensor', 'vector', 'scalar', 'gpsimd']
SAME_ENGINE_SYNC = {'vector': True, 'scalar': True, 'gpsimd': True, 'tensor': False, 'sync': False}

D = 1024
DFF = 2816
DEPTH = 2
ALPHA = float((2 * DEPTH) ** 0.25)
LN_EPS = 1e-5
RMS_EPS = 1e-6
GH, GDK, GDV = 4, 128, 256
GATE_RANK = 16
GATE_TAU = 16.0
SBH, SBD = 16, 64
NEG_BIG = -240.0


class Op:
    __slots__ = ('eng', 'fn', 'deps', 'dma', 'tok', 'sig')

    def __init__(self, eng, fn, deps, dma):
        self.eng, self.fn, self.deps, self.dma = eng, fn, deps, dma
        self.tok = None
        self.sig = 0


class Prog:
    def __init__(self, nc):
        self.nc = nc
        self.ops = {e: [] for e in ENGS}
        self.bufs = {}
        self.dma_counts = {}
        self.keep = set()

    def add(self, eng, fn, reads=(), writes=(), dma=None, deps=()):
        d = set(t for t in deps if t is not None)
        for k in reads:
            st = self.bufs.get(k)
            if st is not None and st[0] is not None:
                d.add(st[0])
        for k in writes:
            st = self.bufs.get(k)
            if st is not None:
                if st[0] is not None:
                    d.add(st[0])
                d.update(st[1].values())
        op = Op(eng, fn, d, dma)
        idx = len(self.ops[eng])
        self.ops[eng].append(op)
        if dma is not None:
            c = self.dma_counts.get(dma, 0) + 1
            self.dma_counts[dma] = c
            tok = ('d', dma, c)
        else:
            tok = ('e', eng, idx)
        op.tok = tok
        for k in reads:
            st = self.bufs.setdefault(k, [None, {}])
            rk = eng if dma is None else ('d', dma)
            st[1][rk] = tok
        for k in writes:
            self.bufs[k] = [tok, {}]
        return tok

    def seal(self, keys, dma_key):
        tok = ('d', dma_key, self.dma_counts[dma_key])
        for k in keys:
            st = self.bufs.get(k)
            if st is None:
                continue
            if st[0] is not None and st[0][0] == 'd' and st[0][1] == dma_key:
                st[0] = tok
            rk = ('d', dma_key)
            if rk in st[1]:
                st[1][rk] = tok

    def barrier(self):
        toks = [('d', k, cnt) for k, cnt in self.dma_counts.items()]
        for e in ENGS:
            for op in reversed(self.ops[e]):
                if op.dma is None and op.fn is not None:
                    toks.append(op.tok)
                    break
        for e in ENGS:
            self.add(e, None, deps=toks)
        self.bufs = {k: v for k, v in self.bufs.items() if k in self.keep}

    def all_dma_tokens(self, prefix=None):
        return [('d', k, c) for k, c in self.dma_counts.items()
                if prefix is None or str(k).startswith(prefix)]

    def emit(self):
        nc = self.nc
        needed = set()
        for e in ENGS:
            for op in self.ops[e]:
                for t in op.deps:
                    if t[0] == 'e':
                        if t[1] == e and not SAME_ENGINE_SYNC[e]:
                            continue
                        needed.add(t)
        for e in ENGS:
            n = 0
            for op in self.ops[e]:
                if op.dma is None and op.tok in needed:
                    n += 1
                    op.sig = n
        with ExitStack() as es:
            esem = {e: es.enter_context(nc.semaphore("s_" + e)) for e in ENGS}
            dsem = {}
            for i, k in enumerate(self.dma_counts):
                dsem[k] = es.enter_context(nc.semaphore("d%d" % i))
            block = es.enter_context(nc.Block())
            for e in ENGS:
                ops = self.ops[e]

                def body(eng, e=e, ops=ops):
                    waited = {}
                    for op in ops:
                        best = {}
                        for t in op.deps:
                            if t[0] == 'e':
                                if t[1] == e and not SAME_ENGINE_SYNC[e]:
                                    continue
                                sem = esem[t[1]]
                                val = self.ops[t[1]][t[2]].sig
                                key = ('e', t[1])
                            else:
                                sem = dsem[t[1]]
                                val = 16 * t[2]
                                key = ('d', t[1])
                            if key not in best or best[key][1] < val:
                                best[key] = (sem, val)
                        for key, (sem, val) in best.items():
                            if waited.get(key, 0) >= val:
                                continue
                            eng.wait_ge(sem, val)
                            waited[key] = val
                        if op.fn is None:
                            continue
                        ins = op.fn(eng)
                        if op.dma is not None:
                            ins.then_inc(dsem[op.dma], 16)
                        elif op.sig:
                            ins.then_inc(esem[e], 1)
                getattr(block, e)(body)


class Ctx:
    ARENA_BYTES = 207 * 1024

    def __init__(self, dims):
        self.dm = dims
        self.nc = bass.Bass("TRN2", target_bir_lowering=False)
        self.P = Prog(self.nc)
        self.ps = [self.nc.alloc_psum_tensor("psb%d" % i, [128, 512], F32) for i in range(8)]
        self.uid = 0
        self.dram = {}
        self.arena = self.nc.alloc_sbuf_tensor("arena", [128, self.ARENA_BYTES // 4], F32)
        self.off = 0

    def inp(self, name, shape, dt=F32):
        t = self.nc.dram_tensor(name, list(shape), dt, kind="ExternalInput").ap()
        self.dram[name] = t
        return t

    def outp(self, name, shape, dt=F32):
        t = self.nc.dram_tensor(name, list(shape), dt, kind="ExternalOutput").ap()
        self.dram[name] = t
        return t

    def sb(self, name, shape, dt=F32):
        esz = 2 if dt == BF16 else 4
        n = 1
        for d in shape[1:]:
            n *= d
        nbytes = (n * esz + 63) // 64 * 64
        if self.off + nbytes > self.ARENA_BYTES:
            raise RuntimeError("SBUF arena overflow allocating %s (%d + %d)" % (name, self.off, nbytes))
        a = self.arena[0:shape[0], self.off // 4:(self.off + nbytes) // 4]
        self.off += nbytes
        if dt != F32:
            a = a.bitcast(dt)
        a = a[:, 0:n]
        if len(shape) == 3:
            a = a.rearrange("p (a b) -> p a b", a=shape[1])
        elif len(shape) != 2:
            raise RuntimeError("bad shape")
        return a

    def mark(self):
        return self.off

    def release(self, mark):
        self.off = mark
        self.P.barrier()

    def psk(self, i):
        return ('ps', i)


def load_consts(c):
    P = c.P
    ident_d = c.inp("c_ident", [128, 128])
    c.ident = c.sb("ident", [128, 128], F32)
    c.identb = c.sb("identb", [128, 128], BF16)
    P.add('sync', lambda e: e.dma_start(out=c.ident[:], in_=ident_d), writes=['ident'], dma='c0')
    P.add('gpsimd', lambda e: e.dma_start(out=c.identb[:], in_=ident_d), writes=['identb'], dma='c1')
    c.epsb = c.sb("epsb", [128, 2], F32)
    c.oneb = c.sb("oneb", [128, 1], F32)
    P.add('vector', lambda e: e.memset(c.oneb[:], 1.0), writes=['oneb'])
    P.add('vector', lambda e: e.memset(c.epsb[:, 0:1], LN_EPS), writes=['epsb'])
    P.add('vector', lambda e: e.memset(c.epsb[:, 1:2], RMS_EPS), writes=['epsb'])


def load_ln_params(c, name, g_d, b_d):
    Dm = c.dm['D']
    g = c.sb(name + "_g", [128, Dm], F32)
    b = c.sb(name + "_b", [128, Dm], F32)
    c.P.add('sync', lambda e: e.dma_start(out=g[:], in_=g_d.partition_broadcast(128)), writes=[name + '_g'], dma='lnp')
    c.P.add('sync', lambda e: e.dma_start(out=b[:], in_=b_d.partition_broadcast(128)), writes=[name + '_b'], dma='lnp')
    c.P.seal([name + '_g', name + '_b'], 'lnp')
    return (g, b, name + '_g', name + '_b')


def transposes_to_xT(c, x_res, xkey, tts, xT, xTkey, banks):
    P = c.P
    KC = c.dm['D'] // 128
    bi = 0
    for i, tt in enumerate(tts):
        for k0 in range(0, KC, 4):
            bank = banks[bi % len(banks)]
            bi += 1
            pt = c.ps[bank]
            for kk in range(4):
                kc = k0 + kk
                P.add('tensor', lambda e, pt=pt, kk=kk, kc=kc, tt=tt: e.transpose(
                    out=pt[:, kk * 128:(kk + 1) * 128], in_=x_res[:, tt, kc * 128:(kc + 1) * 128], identity=c.ident[:]),
                    reads=[(xkey, tt), 'ident'], writes=[c.psk(bank)])
            eng = 'scalar' if (bi % 2 == 0) else 'vector'
            src = pt[:].rearrange("p (a b) -> p a b", a=4)
            dst = xT[:, k0:k0 + 4, i * 128:(i + 1) * 128]
            if eng == 'scalar':
                P.add('scalar', lambda e, src=src, dst=dst: e.copy(out=dst, in_=src),
                      reads=[c.psk(bank)], writes=[(xTkey, i)])
            else:
                P.add('vector', lambda e, src=src, dst=dst: e.tensor_copy(out=dst, in_=src),
                      reads=[c.psk(bank)], writes=[(xTkey, i)])


def ln_epilogue(c, banks2, x_res, xkey, tt, lnp, tmp, tmpkey, extra_reads=()):
    P = c.P
    Dm = c.dm['D']
    g, b, gk, bk = lnp
    t2, st = tmp
    nh = Dm // 512
    xk = (xkey, tt)
    for h in range(nh):
        P.add('vector', lambda e, h=h: e.scalar_tensor_tensor(
            out=x_res[:, tt, h * 512:(h + 1) * 512], in0=x_res[:, tt, h * 512:(h + 1) * 512], scalar=ALPHA,
            in1=c.ps[banks2[h]][:], op0=ALU.mult, op1=ALU.add),
            reads=[xk, c.psk(banks2[h])] + list(extra_reads), writes=[xk])
    for h in range(nh):
        P.add('vector', lambda e, h=h: e.bn_stats(out=st[:, 6 * h:6 * h + 6], in_=x_res[:, tt, h * 512:(h + 1) * 512]),
              reads=[xk], writes=[(tmpkey, 'st', h)])
    P.add('vector', lambda e: e.bn_aggr(out=st[:, 12:14], in_=st[:, 0:6 * nh]),
          reads=[(tmpkey, 'st', h) for h in range(nh)], writes=[(tmpkey, 'mv')])
    P.add('scalar', lambda e: e.activation(out=st[:, 16:17], in_=st[:, 13:14], func=ACTF.Ln, bias=c.epsb[:, 0:1]),
          reads=[(tmpkey, 'mv'), 'epsb'], writes=[(tmpkey, 'lnv')])
    P.add('scalar', lambda e: e.activation(out=st[:, 14:15], in_=st[:, 16:17], func=ACTF.Exp, scale=-0.5),
          reads=[(tmpkey, 'lnv')], writes=[(tmpkey, 'rstd')])
    P.add('vector', lambda e: e.scalar_tensor_tensor(out=st[:, 15:16], in0=st[:, 12:13], scalar=-1.0, in1=st[:, 14:15],
                                                     op0=ALU.mult, op1=ALU.mult),
          reads=[(tmpkey, 'mv'), (tmpkey, 'rstd')], writes=[(tmpkey, 'nb')])
    P.add('scalar', lambda e: e.activation(out=t2[:], in_=x_res[:, tt, :], func=ACTF.Identity, bias=st[:, 15:16], scale=st[:, 14:15]),
          reads=[xk, (tmpkey, 'nb'), (tmpkey, 'rstd')], writes=[(tmpkey, 't2')])
    P.add('vector', lambda e: e.tensor_tensor(out=t2[:], in0=t2[:], in1=g[:], op=ALU.mult),
          reads=[(tmpkey, 't2'), gk], writes=[(tmpkey, 't2')])
    P.add('gpsimd', lambda e: e.tensor_tensor(out=x_res[:, tt, :], in0=t2[:], in1=b[:], op=ALU.add),
          reads=[(tmpkey, 't2'), bk], writes=[xk])


def alloc_ln_tmp(c, name):
    Dm = c.dm['D']
    return [(c.sb(name + "_t2_%d" % i, [128, Dm], F32), c.sb(name + "_st_%d" % i, [128, 32], F32)) for i in range(2)]


def ffn_sublayer(c, x_res, xkey, NT, w_up_d, w_dn_d, lnp, bufs):
    P = c.P
    Dm, Dff = c.dm['D'], c.dm['DFF']
    KC, JC = Dm // 128, Dff // 128
    ST = min(1024, NT)
    nst = NT // ST
    xT, hT, wd, wslots, sgs, lntmp = bufs['xT'], bufs['hT'], bufs['wd'], bufs['wslots'], bufs['sg'], bufs['lntmp']
    uid = c.uid
    c.uid += 1
    nsl = len(wslots)
    nh = Dm // 512
    for st in range(nst):
        tts = list(range(st * (ST // 128), (st + 1) * (ST // 128)))
        transposes_to_xT(c, x_res, xkey, tts, xT, 'xT', banks=[0, 2])
        for j in range(JC):
            P.add('gpsimd', lambda e, j=j: e.dma_start(out=wd[:, j, :], in_=w_dn_d[j * 128:(j + 1) * 128, :]),
                  writes=[('wd', j)], dma='wd')
        P.seal([('wd', j) for j in range(JC)], 'wd')
        ntk = ST // 512 if ST >= 512 else 1
        tw = min(512, ST)
        for j in range(JC):
            s = (uid * 1000 + st * JC + j) % nsl
            wg, wu = wslots[s]
            P.add('gpsimd', lambda e, j=j, wg=wg: e.dma_start(
                out=wg[:], in_=w_up_d[:, j * 128:(j + 1) * 128].rearrange("(kc p) n -> p kc n", p=128)),
                writes=[('wg', s)], dma='wg%d' % s)
            P.add('gpsimd', lambda e, j=j, wu=wu: e.dma_start(
                out=wu[:], in_=w_up_d[:, Dff + j * 128:Dff + (j + 1) * 128].rearrange("(kc p) n -> p kc n", p=128)),
                writes=[('wu', s)], dma='wu%d' % s)
            for t5 in range(ntk):
                par = (j * ntk + t5) % 2
                bg, bu = (0, 1) if par == 0 else (2, 3)
                cols = slice(t5 * tw, (t5 + 1) * tw)
                xkeys = [('xT', i) for i in range(t5 * (tw // 128), (t5 + 1) * (tw // 128))]
                for kc in range(KC):
                    P.add('tensor', lambda e, kc=kc, wg=wg, bg=bg, cols=cols: e.matmul(
                        c.ps[bg][:, 0:tw], lhsT=wg[:, kc, :], rhs=xT[:, kc, cols], start=(kc == 0), stop=(kc == KC - 1)),
                        reads=[('wg', s)] + xkeys, writes=[c.psk(bg)])
                for kc in range(KC):
                    P.add('tensor', lambda e, kc=kc, wu=wu, bu=bu, cols=cols: e.matmul(
                        c.ps[bu][:, 0:tw], lhsT=wu[:, kc, :], rhs=xT[:, kc, cols], start=(kc == 0), stop=(kc == KC - 1)),
                        reads=[('wu', s)] + xkeys, writes=[c.psk(bu)])
                sg = sgs[par]
                P.add('scalar', lambda e, sg=sg, bg=bg: e.activation(out=sg[:, 0:tw], in_=c.ps[bg][:, 0:tw], func=ACTF.Silu),
                      reads=[c.psk(bg)], writes=[('sg', par)])
                P.add('vector', lambda e, sg=sg, bu=bu, j=j, cols=cols: e.scalar_tensor_tensor(
                    out=hT[:, j, cols], in0=sg[:, 0:tw], scalar=0.5, in1=c.ps[bu][:, 0:tw], op0=ALU.mult, op1=ALU.mult),
                    reads=[('sg', par), c.psk(bu)], writes=[('hT', j, t5)])
        for i, tt in enumerate(tts):
            par = i % 2
            banks2 = [4 + 2 * par + h for h in range(nh)]
            t5 = (i * 128) // tw
            for h in range(nh):
                for j in range(JC):
                    P.add('tensor', lambda e, j=j, h=h, i=i, banks2=banks2: e.matmul(
                        c.ps[banks2[h]][:], lhsT=hT[:, j, i * 128:(i + 1) * 128], rhs=wd[:, j, h * 512:(h + 1) * 512],
                        start=(j == 0), stop=(j == JC - 1)),
                        reads=[('hT', j, t5), ('wd', j)], writes=[c.psk(banks2[h])])
            ln_epilogue(c, banks2, x_res, xkey, tt, lnp, lntmp[par], ('lntmp', par))


def alloc_ffn_bufs(c, NT):
    Dm, Dff = c.dm['D'], c.dm['DFF']
    KC, JC = Dm // 128, Dff // 128
    ST = min(1024, NT)
    b = {}
    b['xT'] = c.sb("xT", [128, KC, ST], BF16)
    b['hT'] = c.sb("hT", [128, JC, ST], BF16)
    b['wd'] = c.sb("wd", [128, JC, Dm], BF16)
    b['wslots'] = [(c.sb("wg%d" % i, [128, KC, 128], BF16), c.sb("wu%d" % i, [128, KC, 128], BF16)) for i in range(2)]
    b['sg'] = [c.sb("sg%d" % i, [128, 512], F32) for i in range(2)]
    b['lntmp'] = alloc_ln_tmp(c, "ln")
    return b


def load_x(c, x_d, x_res, xkey, NT):
    for tt in range(NT // 128):
        c.P.add('sync', lambda e, tt=tt: e.dma_start(out=x_res[:, tt, :], in_=x_d[tt * 128:(tt + 1) * 128, :]),
                writes=[(xkey, tt)], dma='ldx')
    c.P.seal([(xkey, tt) for tt in range(NT // 128)], 'ldx')


def store_x(c, x_d, x_res, xkey, NT, tag='stx'):
    for tt in range(NT // 128):
        c.P.add('sync', lambda e, tt=tt: e.dma_start(out=x_d[tt * 128:(tt + 1) * 128, :], in_=x_res[:, tt, :]),
                reads=[(xkey, tt)], dma=tag)
    c.P.seal([(xkey, tt) for tt in range(NT // 128)], tag)


def finish(c):
    toks = c.P.all_dma_tokens()
    c.P.add('sync', None, deps=toks)
    c.P.emit()
    return c.nc


def build_ffn_test(dims, NT):
    c = Ctx(dims)
    Dm, Dff = dims['D'], dims['DFF']
    x_d = c.inp("x", [NT, Dm])
    wup = c.inp("w_up", [Dm, 2 * Dff])
    wdn = c.inp("w_dn", [Dff, Dm])
    lng = c.inp("ln_g", [Dm])
    lnb = c.inp("ln_b", [Dm])
    out = c.outp("out", [NT, Dm])
    load_consts(c)
    x_res = c.sb("x_res", [128, NT // 128, Dm], F32)
    load_x(c, x_d, x_res, 'x', NT)
    lnp = load_ln_params(c, "ln0", lng, lnb)
    bufs = alloc_ffn_bufs(c, NT)
    ffn_sublayer(c, x_res, 'x', NT, wup, wdn, lnp, bufs)
    store_x(c, out, x_res, 'x', NT)
    return finish(c)


def load_gla_consts(c):
    t2_d = c.inp("c_t2s", [128, 128])
    ind_d = c.inp("c_ind", [128, 2])
    c.t2s = c.sb("t2s", [128, 128], F32)
    c.ind = c.sb("ind", [128, 2], F32)
    c.P.add('sync', lambda e: e.dma_start(out=c.t2s[:], in_=t2_d), writes=['t2s'], dma='c2')
    c.P.add('sync', lambda e: e.dma_start(out=c.ind[:], in_=ind_d), writes=['ind'], dma='c3')


def gla_consts_host():
    t = np.arange(128)
    same = (t[:, None] // 64) == (t[None, :] // 64)
    t2s = np.where(same & (t[:, None] > t[None, :]), -1.0 / GATE_TAU, 0.0).astype(np.float32)
    ind = np.zeros((128, 2), np.float32)
    ind[:64, 0] = -1.0 / GATE_TAU
    ind[64:, 1] = -1.0 / GATE_TAU
    return t2s, ind


def alloc_gla_bufs(c, full, S=None):
    Dm = c.dm['D']
    KC = Dm // 128
    b = {}
    b['win'] = c.sb("g_win", [128, KC, 3088], BF16)
    b['wgk'] = c.sb("g_wgk", [17, 512], BF16)
    b['xT'] = c.sb("g_xT", [128, KC, 512], BF16)
    b['lowT'] = c.sb("g_lowT", [17, 512], BF16)
    b['e'] = c.sb("g_e", [128, 512], F32)
    b['l'] = c.sb("g_l", [128, 512], F32)
    b['ed'] = c.sb("g_ed", [128, 512], F32)
    b['kdec'] = c.sb("g_kdec", [128, 512], BF16)
    b['v'] = c.sb("g_v", [128, 1024], BF16)
    b['decT'] = c.sb("g_decT", [128, 8], F32)
    b['S'] = S if S is not None else c.sb("g_S", [128, 4, 256], F32)
    if full:
        b['wout'] = c.sb("g_wout", [128, 8, Dm], BF16)
        b['qT'] = c.sb("g_qT", [128, 4, 512], BF16)
        b['sr'] = c.sb("g_sr", [128, 1024], F32)
        b['Sb'] = c.sb("g_Sb", [128, 4, 256], BF16)
        b['o'] = c.sb("g_o", [128, 4, 256], F32)
        b['junk'] = c.sb("g_junk", [128, 256], F32)
        b['ss'] = c.sb("g_ss", [128, 8], F32)
        b['gated'] = c.sb("g_gated", [128, 1024], BF16)
        b['gT'] = c.sb("g_gT", [128, 8, 128], BF16)
        b['ng'] = c.sb("g_ng", [128, 256], F32)
        b['lntmp'] = alloc_ln_tmp(c, "gln")
    return b


def gla_pass(c, x_res, xkey, NT, full, w, b, lnp=None):
    P = c.P
    Dm = c.dm['D']
    KC = Dm // 128
    nh = Dm // 512
    win, wgk, xT, lowT = b['win'], b['wgk'], b['xT'], b['lowT']
    S = b['S']
    for kc in range(KC):
        P.add('gpsimd', lambda e, kc=kc: e.dma_start(out=win[:, kc, :], in_=w['w_in'][kc * 128:(kc + 1) * 128, :]),
              writes=[('g_win', kc)], dma='g_win')
    P.seal([('g_win', kc) for kc in range(KC)], 'g_win')
    winkeys = [('g_win', kc) for kc in range(KC)]
    P.add('gpsimd', lambda e: e.dma_start(out=wgk[0:16, :], in_=w['w_gk']), writes=['g_wgk0'], dma='g_wgk')
    P.add('gpsimd', lambda e: e.dma_start(out=wgk[16:17, :], in_=w['b_gk'].rearrange("(o n) -> o n", o=1)),
          writes=['g_wgk1'], dma='g_wgk')
    P.seal(['g_wgk0', 'g_wgk1'], 'g_wgk')
    if isinstance(w.get('s_init'), str):
        pass
    elif w.get('s_init') is not None:
        P.add('sync', lambda e: e.dma_start(out=S[:], in_=w['s_init'].rearrange("h p d -> p h d")), writes=[('g_S', h) for h in range(GH)], dma='g_sinit')
    else:
        P.add('vector', lambda e: e.memset(S[:], 0.0), writes=[('g_S', h) for h in range(GH)])
    P.add('vector', lambda e: e.memset(lowT[:], 1.0), writes=['g_lowT'])
    if full:
        wout, qT, sr, Sb, o_sb, junk, ss, gated, gT, ng = (b[k] for k in ('wout', 'qT', 'sr', 'Sb', 'o', 'junk', 'ss', 'gated', 'gT', 'ng'))
        for kc in range(8):
            P.add('gpsimd', lambda e, kc=kc: e.dma_start(out=wout[:, kc, :], in_=w['w_out'][kc * 128:(kc + 1) * 128, :]),
                  writes=[('g_wout', kc)], dma='g_wout')
        P.seal([('g_wout', kc) for kc in range(8)], 'g_wout')
        P.add('sync', lambda e: e.dma_start(out=ng[:], in_=w['norm_g'].partition_broadcast(128)), writes=['g_ng'], dma='g_ng')
    e_sb, l_sb, ed, kdec, v_sb, decT = b['e'], b['l'], b['ed'], b['kdec'], b['v'], b['decT']
    GW = min(512, NT)
    ngrp = NT // GW
    tpg = GW // 128
    for gi in range(ngrp):
        tts = list(range(gi * tpg, (gi + 1) * tpg))
        transposes_to_xT(c, x_res, xkey, tts, xT, 'g_xT', banks=[0])
        xkeys = [('g_xT', i) for i in range(tpg)]
        if full:
            for h in range(GH):
                for kc in range(KC):
                    P.add('tensor', lambda e, h=h, kc=kc: e.matmul(
                        c.ps[0][:, 0:GW], lhsT=win[:, kc, h * 128:(h + 1) * 128], rhs=xT[:, kc, 0:GW],
                        start=(kc == 0), stop=(kc == KC - 1)),
                        reads=winkeys + xkeys, writes=[c.psk(0)])
                P.add('scalar', lambda e, h=h: e.activation(out=qT[:, h, 0:GW], in_=c.ps[0][:, 0:GW], func=ACTF.Copy,
                                                            scale=float(GDK ** -0.5)),
                      reads=[c.psk(0)], writes=[('g_qT', h)])
        for kc in range(KC):
            P.add('tensor', lambda e, kc=kc: e.matmul(
                c.ps[0][0:16, 0:GW], lhsT=win[:, kc, 3072:3088], rhs=xT[:, kc, 0:GW], start=(kc == 0), stop=(kc == KC - 1)),
                reads=winkeys + xkeys, writes=[c.psk(0)])
        P.add('vector', lambda e: e.tensor_copy(out=lowT[0:16, 0:GW], in_=c.ps[0][0:16, 0:GW]),
              reads=[c.psk(0)], writes=['g_lowT'])
        for ti, tt in enumerate(tts):
            tcol = slice(ti * 128, (ti + 1) * 128)
            for kc in range(KC):
                P.add('tensor', lambda e, kc=kc, tcol=tcol: e.matmul(
                    c.ps[1][:], lhsT=xT[:, kc, tcol], rhs=win[:, kc, 512:1024], start=(kc == 0), stop=(kc == KC - 1)),
                    reads=winkeys + [('g_xT', ti)], writes=[c.psk(1)])
            for hv in range(2):
                for kc in range(KC):
                    P.add('tensor', lambda e, kc=kc, hv=hv, tcol=tcol: e.matmul(
                        c.ps[2 + hv][:], lhsT=xT[:, kc, tcol], rhs=win[:, kc, 1024 + hv * 512:1024 + (hv + 1) * 512],
                        start=(kc == 0), stop=(kc == KC - 1)),
                        reads=winkeys + [('g_xT', ti)], writes=[c.psk(2 + hv)])
            if full:
                for hv in range(2):
                    for kc in range(KC):
                        P.add('tensor', lambda e, kc=kc, hv=hv, tcol=tcol: e.matmul(
                            c.ps[4 + hv][:], lhsT=xT[:, kc, tcol], rhs=win[:, kc, 2048 + hv * 512:2048 + (hv + 1) * 512],
                            start=(kc == 0), stop=(kc == KC - 1)),
                            reads=winkeys + [('g_xT', ti)], writes=[c.psk(4 + hv)])
            P.add('tensor', lambda e, tcol=tcol: e.matmul(c.ps[0][:], lhsT=lowT[:, tcol], rhs=wgk[:], start=True, stop=True),
                  reads=['g_lowT', 'g_wgk0', 'g_wgk1'], writes=[c.psk(0)])
            P.add('scalar', lambda e: e.activation(out=e_sb[:], in_=c.ps[0][:], func=ACTF.Exp, scale=-1.0),
                  reads=[c.psk(0)], writes=['g_e'])
            P.add('scalar', lambda e: e.activation(out=l_sb[:], in_=e_sb[:], func=ACTF.Ln, bias=c.oneb[:, 0:1]),
                  reads=['g_e', 'oneb'], writes=['g_l'])
            P.add('tensor', lambda e: e.matmul(c.ps[0][:], lhsT=c.t2s[:], rhs=l_sb[:], start=True, stop=True),
                  reads=['t2s', 'g_l'], writes=[c.psk(0)])
            P.add('scalar', lambda e: e.activation(out=ed[:], in_=c.ps[0][:], func=ACTF.Exp),
                  reads=[c.psk(0)], writes=['g_ed'])
            for h in range(GH):
                P.add('tensor', lambda e, h=h: e.matmul(c.ps[6][:, 2 * h:2 * h + 2], lhsT=l_sb[:, h * 128:(h + 1) * 128],
                                                        rhs=c.ind[:], start=True, stop=True),
                      reads=['g_l', 'ind'], writes=[c.psk(6)])
            P.add('scalar', lambda e: e.activation(out=decT[:], in_=c.ps[6][:, 0:8], func=ACTF.Exp),
                  reads=[c.psk(6)], writes=['g_decT'])
            P.add('vector', lambda e: e.tensor_tensor(out=kdec[:], in0=c.ps[1][:], in1=ed[:], op=ALU.mult),
                  reads=[c.psk(1), 'g_ed'], writes=['g_kdec'])
            for hv in range(2):
                P.add('scalar' if hv == 0 else 'vector',
                      (lambda e, hv=hv: e.copy(out=v_sb[:, hv * 512:(hv + 1) * 512], in_=c.ps[2 + hv][:])) if hv == 0 else
                      (lambda e, hv=hv: e.tensor_copy(out=v_sb[:, hv * 512:(hv + 1) * 512], in_=c.ps[2 + hv][:])),
                      reads=[c.psk(2 + hv)], writes=[('g_v', hv)])
            if full:
                for hv in range(2):
                    P.add('scalar', lambda e, hv=hv: e.activation(out=sr[:, hv * 512:(hv + 1) * 512], in_=c.ps[4 + hv][:], func=ACTF.Silu),
                          reads=[c.psk(4 + hv)], writes=[('g_sr', hv)])
            for cc in range(2):
                rows = slice(cc * 64, (cc + 1) * 64)
                for h in range(GH):
                    P.add('tensor', lambda e, h=h, rows=rows: e.matmul(
                        c.ps[2 + h // 2][:, (h % 2) * 256:(h % 2 + 1) * 256], lhsT=kdec[rows, h * 128:(h + 1) * 128],
                        rhs=v_sb[rows, h * 256:(h + 1) * 256], start=True, stop=True),
                        reads=['g_kdec', ('g_v', h // 2)], writes=[c.psk(2 + h // 2)])
                for h in range(GH):
                    P.add('vector', lambda e, h=h, cc=cc: e.scalar_tensor_tensor(
                        out=S[:, h, :], in0=S[:, h, :], scalar=decT[:, 2 * h + cc:2 * h + cc + 1],
                        in1=c.ps[2 + h // 2][:, (h % 2) * 256:(h % 2 + 1) * 256], op0=ALU.mult, op1=ALU.add),
                        reads=[('g_S', h), 'g_decT', c.psk(2 + h // 2)], writes=[('g_S', h)])
                if not full:
                    continue
                P.add('scalar', lambda e: e.copy(out=Sb[:], in_=S[:]), reads=[('g_S', h) for h in range(GH)], writes=['g_Sb'])
                for h in range(GH):
                    P.add('tensor', lambda e, h=h, tcol=tcol: e.matmul(
                        c.ps[4 + h // 2][:, (h % 2) * 256:(h % 2 + 1) * 256], lhsT=qT[:, h, tcol], rhs=Sb[:, h, :],
                        start=True, stop=True),
                        reads=[('g_qT', h), 'g_Sb'], writes=[c.psk(4 + h // 2)])
                for hv in range(2):
                    src = c.ps[4 + hv][rows, :].rearrange("p (a b) -> p a b", a=2)
                    dst = o_sb[rows, 2 * hv:2 * hv + 2, :]
                    if hv == 0:
                        P.add('vector', lambda e, src=src, dst=dst: e.tensor_copy(out=dst, in_=src),
                              reads=[c.psk(4 + hv)], writes=[('g_o', cc, hv)])
                    else:
                        P.add('scalar', lambda e, src=src, dst=dst: e.copy(out=dst, in_=src),
                              reads=[c.psk(4 + hv)], writes=[('g_o', cc, hv)])
            if not full:
                continue
            okeys = [('g_o', cc, hv) for cc in range(2) for hv in range(2)]
            for h in range(GH):
                P.add('scalar', lambda e, h=h: e.activation(out=junk[:], in_=o_sb[:, h, :], func=ACTF.Square,
                                                            accum_out=ss[:, h:h + 1]),
                      reads=okeys, writes=[('g_ss', h), 'g_junk'])
            P.add('scalar', lambda e: e.activation(out=ss[:, 4:8], in_=ss[:, 0:4], func=ACTF.Ln, bias=c.epsb[:, 1:2],
                                                   scale=1.0 / GDV),
                  reads=[('g_ss', h) for h in range(GH)] + ['epsb'], writes=['g_ss2'])
            P.add('scalar', lambda e: e.activation(out=ss[:, 4:8], in_=ss[:, 4:8], func=ACTF.Exp, scale=-0.5),
                  reads=['g_ss2'], writes=['g_ss2'])
            for h in range(GH):
                P.add('vector', lambda e, h=h: e.scalar_tensor_tensor(
                    out=o_sb[:, h, :], in0=o_sb[:, h, :], scalar=ss[:, 4 + h:5 + h], in1=ng[:], op0=ALU.mult, op1=ALU.mult),
                    reads=okeys + ['g_ss2', 'g_ng'], writes=[('g_on', h)])
            for hv in range(2):
                P.add('gpsimd' if hv == 0 else 'vector', lambda e, hv=hv: e.tensor_tensor(
                    out=gated[:, hv * 512:(hv + 1) * 512],
                    in0=o_sb[:, 2 * hv:2 * hv + 2, :].rearrange("p a b -> p (a b)"),
                    in1=sr[:, hv * 512:(hv + 1) * 512], op=ALU.mult),
                    reads=[('g_on', 2 * hv), ('g_on', 2 * hv + 1), ('g_sr', hv)], writes=[('g_gated', hv)])
            psb = c.ps[6][:].bitcast(BF16)
            for kc in range(8):
                P.add('tensor', lambda e, kc=kc: e.transpose(out=psb[:, kc * 128:(kc + 1) * 128],
                                                             in_=gated[:, kc * 128:(kc + 1) * 128], identity=c.identb[:]),
                      reads=[('g_gated', kc // 4), 'identb'], writes=[c.psk(6)])
            P.add('vector', lambda e: e.tensor_copy(out=gT[:].rearrange("p a b -> p (a b)"), in_=psb),
                  reads=[c.psk(6)], writes=['g_gT'])
            mb = [7, 1][:nh]
            for hh in range(nh):
                for kc in range(8):
                    P.add('tensor', lambda e, kc=kc, hh=hh: e.matmul(
                        c.ps[mb[hh]][:], lhsT=gT[:, kc, :], rhs=wout[:, kc, hh * 512:(hh + 1) * 512],
                        start=(kc == 0), stop=(kc == 7)),
                        reads=['g_gT', ('g_wout', kc)], writes=[c.psk(mb[hh])])
            ln_epilogue(c, mb, x_res, xkey, tt, lnp, b['lntmp'][tt % 2], ('glntmp', tt % 2))
    if w.get('f_out') is not None:
        P.add('sync', lambda e: e.dma_start(out=w['f_out'].rearrange("h p d -> p h d"), in_=S[:]), reads=[('g_S', h) for h in range(GH)], dma='g_fout')


def build_gla_test(dims, NT, full):
    c = Ctx(dims)
    Dm = dims['D']
    x_d = c.inp("x", [NT, Dm])
    w = dict(w_in=c.inp("w_in", [Dm, 3088]), w_gk=c.inp("w_gk", [16, 512]), b_gk=c.inp("b_gk", [512]),
             norm_g=c.inp("norm_g", [256]), w_out=c.inp("w_out", [1024, Dm]), s_init=c.inp("s_init", [4, 128, 256]),
             f_out=c.outp("f_out", [4, 128, 256]))
    lng = c.inp("ln_g", [Dm])
    lnb = c.inp("ln_b", [Dm])
    out = c.outp("out", [NT, Dm])
    load_consts(c)
    load_gla_consts(c)
    x_res = c.sb("x_res", [128, NT // 128, Dm], F32)
    load_x(c, x_d, x_res, 'x', NT)
    lnp = load_ln_params(c, "ln0", lng, lnb)
    bufs = alloc_gla_bufs(c, full)
    gla_pass(c, x_res, 'x', NT, full, w, bufs, lnp)
    store_x(c, out, x_res, 'x', NT)
    return finish(c)


def attn_consts_host():
    i = np.arange(128)[:, None]
    j = np.arange(512)[None, :]
    return np.where((j <= 127 - i) & (j < 128), NEG_BIG, 0.0).astype(np.float32)


def attn_kernel(c, SEQ, x3r_d, x4_d, wk_d, wv_d, wq_d, oT_d, negmask_d):
    P = c.P
    Dm = c.dm['D']
    KC = Dm // 128
    NB = SEQ // 128
    NG = SEQ // 512
    wk = c.sb("a_wk", [128, KC, 512], BF16)
    wv = c.sb("a_wv", [128, KC, 512], BF16)
    wq = c.sb("a_wq", [128, KC, 512], BF16)
    for nm, t, d in (('a_wk', wk, wk_d), ('a_wv', wv, wv_d), ('a_wq', wq, wq_d)):
        P.add('gpsimd', lambda e, t=t, d=d: e.dma_start(out=t[:], in_=d.rearrange("(kc p) n -> p kc n", p=128)),
              writes=[nm], dma=nm)
    negm = c.sb("a_negm", [128, 512], BF16)
    P.add('gpsimd', lambda e: e.dma_start(out=negm[:], in_=negmask_d), writes=['a_negm'], dma='a_negm')
    zeros = c.sb("a_zeros", [128, 512], F32)
    P.add('gpsimd', lambda e: e.memset(zeros[:], 0.0), writes=['a_zeros'])
    KT = c.sb("a_KT", [128, 4, SEQ], BF16)
    V = c.sb("a_V", [128, NB, 512], BF16)
    qT = c.sb("a_qT", [128, 4, SEQ], BF16)
    oT = c.sb("a_oT", [128, 4, SEQ], BF16)
    xs = c.sb("a_xs", [128, 4, Dm], F32)
    xTa = c.sb("a_xT", [128, KC, 512], BF16)
    for which in range(2):
        src_d = x3r_d if which == 0 else x4_d
        for g in range(NG):
            for ti in range(4):
                P.add('sync', lambda e, g=g, ti=ti, src_d=src_d: e.dma_start(
                    out=xs[:, ti, :], in_=src_d[(g * 4 + ti) * 128:(g * 4 + ti + 1) * 128, :]),
                    writes=[('a_xs', ti)], dma='a_xs')
            P.seal([('a_xs', ti) for ti in range(4)], 'a_xs')
            transposes_to_xT(c, xs, 'a_xs', [0, 1, 2, 3], xTa, 'a_xT', banks=[0, 1])
            xkeys = [('a_xT', i) for i in range(4)]
            gcol = slice(g * 512, (g + 1) * 512)
            bk = 0
            if which == 0:
                for hp in range(4):
                    bank = bk % 2
                    bk += 1
                    for kc in range(KC):
                        P.add('tensor', lambda e, hp=hp, kc=kc, bank=bank: e.matmul(
                            c.ps[bank][:], lhsT=wk[:, kc, hp * 128:(hp + 1) * 128], rhs=xTa[:, kc, :],
                            start=(kc == 0), stop=(kc == KC - 1)), reads=['a_wk'] + xkeys, writes=[c.psk(bank)])
                    P.add('scalar', lambda e, hp=hp, bank=bank, gcol=gcol: e.copy(out=KT[:, hp, gcol], in_=c.ps[bank][:]),
                          reads=[c.psk(bank)], writes=[('a_KT', hp, g)])
                for ti in range(4):
                    bank = bk % 2
                    bk += 1
                    for kc in range(KC):
                        P.add('tensor', lambda e, ti=ti, kc=kc, bank=bank: e.matmul(
                            c.ps[bank][:], lhsT=xTa[:, kc, ti * 128:(ti + 1) * 128], rhs=wv[:, kc, :],
                            start=(kc == 0), stop=(kc == KC - 1)), reads=['a_wv', ('a_xT', ti)], writes=[c.psk(bank)])
                    P.add('vector', lambda e, ti=ti, bank=bank, g=g: e.tensor_copy(out=V[:, g * 4 + ti, :], in_=c.ps[bank][:]),
                          reads=[c.psk(bank)], writes=[('a_V', g * 4 + ti)])
            else:
                for hp in range(4):
                    bank = bk % 2
                    bk += 1
                    for kc in range(KC):
                        P.add('tensor', lambda e, hp=hp, kc=kc, bank=bank: e.matmul(
                            c.ps[bank][:], lhsT=wq[:, kc, hp * 128:(hp + 1) * 128], rhs=xTa[:, kc, :],
                            start=(kc == 0), stop=(kc == KC - 1)), reads=['a_wq'] + xkeys, writes=[c.psk(bank)])
                    P.add('scalar', lambda e, hp=hp, bank=bank, gcol=gcol: e.activation(
                        out=qT[:, hp, gcol], in_=c.ps[bank][:], func=ACTF.Copy, scale=float(SBD ** -0.5)),
                        reads=[c.psk(bank)], writes=[('a_qT', hp, g)])
    NPB = 3
    pbs = [c.sb("a_pb%d" % i, [128, 513], F32) for i in range(NPB)]
    As = [c.sb("a_A%d" % i, [128, 512], BF16) for i in range(2)]
    ATs = [c.sb("a_AT%d" % i, [128, 512], BF16) for i in range(2)]
    tile_i = 0
    head_i = 0
    for qb in range(NB):
        r0 = 128 * (NB - 1 - qb)
        nk = 128 * (qb + 1)
        ntile = (nk + 511) // 512
        qcol = slice(qb * 128, (qb + 1) * 128)
        for h in range(8):
            hp, half = h // 2, h % 2
            prow = slice(half * 64, (half + 1) * 64)
            ob = 6 + head_i % 2
            head_i += 1
            prev_slot = None
            for kt in range(ntile):
                c0 = r0 + 512 * kt
                w = min(512, SEQ - c0)
                nblk = w // 128
                zb = 2 + tile_i % 2
                ab = 4 + tile_i % 2
                ps_ = tile_i % NPB
                as_ = tile_i % 2
                tile_i += 1
                pb, A, AT = pbs[ps_], As[as_], ATs[as_]
                kkeys = [('a_KT', hp, gg) for gg in range(c0 // 512, (c0 + w - 1) // 512 + 1)]
                P.add('tensor', lambda e, hp=hp, prow=prow, qcol=qcol, c0=c0, w=w, zb=zb, kt=kt: e.matmul(
                    c.ps[zb][:, 0:w], lhsT=qT[prow, hp, qcol], rhs=KT[prow, hp, c0:c0 + w], start=True, stop=(kt != 0)),
                    reads=[('a_qT', hp, qb // 4)] + kkeys, writes=[c.psk(zb)])
                if kt == 0:
                    P.add('tensor', lambda e, w=w, zb=zb: e.matmul(
                        c.ps[zb][:, 0:w], lhsT=c.identb[:], rhs=negm[:, 0:w], start=False, stop=True),
                        reads=['identb', 'a_negm'], writes=[c.psk(zb)])
                P.add('scalar', lambda e, pb=pb, zb=zb, w=w: e.activation(out=pb[:, 1:w + 1], in_=c.ps[zb][:, 0:w],
                                                                          func=ACTF.Sigmoid, scale=-1.0),
                      reads=[c.psk(zb)], writes=[('a_pb', ps_)])
                if kt == 0:
                    P.add('vector', lambda e, pb=pb: e.memset(pb[:, 0:1], 1.0), writes=[('a_pb0', ps_)],
                          reads=[])
                else:
                    ppb, pw = pbs[prev_slot[0]], prev_slot[1]
                    P.add('vector', lambda e, pb=pb, ppb=ppb, pw=pw: e.tensor_copy(out=pb[:, 0:1], in_=ppb[:, pw:pw + 1]),
                          reads=[('a_pb', prev_slot[0])], writes=[('a_pb0', ps_)])
                P.add('vector', lambda e, pb=pb, w=w: e.tensor_tensor_scan(
                    out=pb[:, 1:w + 1], data0=pb[:, 1:w + 1], data1=zeros[:, 0:w], initial=pb[:, 0:1],
                    op0=ALU.mult, op1=ALU.add),
                    reads=[('a_pb', ps_), ('a_pb0', ps_), 'a_zeros'], writes=[('a_pb', ps_)])
                P.add('gpsimd', lambda e, pb=pb, A=A, w=w: e.tensor_tensor(out=A[:, 0:w], in0=pb[:, 0:w], in1=pb[:, 1:w + 1],
                                                                          op=ALU.subtract),
                      reads=[('a_pb', ps_), ('a_pb0', ps_)], writes=[('a_A', as_)])
                psb = c.ps[ab][:].bitcast(BF16)
                for bi in range(nblk):
                    P.add('tensor', lambda e, bi=bi, A=A, psb=psb: e.transpose(
                        out=psb[:, bi * 128:(bi + 1) * 128], in_=A[:, bi * 128:(bi + 1) * 128], identity=c.identb[:]),
                        reads=[('a_A', as_), 'identb'], writes=[c.psk(ab)])
                P.add('scalar', lambda e, AT=AT, psb=psb, w=w: e.copy(out=AT[:, 0:w], in_=psb[:, 0:w]),
                      reads=[c.psk(ab)], writes=[('a_AT', as_)])
                for bi in range(nblk):
                    vt = c0 // 128 + bi
                    first = (kt == 0 and bi == 0)
                    last = (kt == ntile - 1 and bi == nblk - 1)
                    P.add('tensor', lambda e, bi=bi, vt=vt, hp=hp, AT=AT, ob=ob, first=first, last=last: e.matmul(
                        c.ps[ob][:, 0:128], lhsT=V[:, vt, hp * 128:(hp + 1) * 128], rhs=AT[:, bi * 128:(bi + 1) * 128],
                        start=first, stop=last),
                        reads=[('a_V', vt), ('a_AT', as_)], writes=[c.psk(ob)])
                prev_slot = (ps_, w)
            eng = 'vector' if half == 0 else 'scalar'
            if eng == 'vector':
                P.add('vector', lambda e, prow=prow, hp=hp, qcol=qcol, ob=ob: e.tensor_copy(
                    out=oT[prow, hp, qcol], in_=c.ps[ob][prow, 0:128]), reads=[c.psk(ob)], writes=[('a_oT', hp, qb, half)])
            else:
                P.add('scalar', lambda e, prow=prow, hp=hp, qcol=qcol, ob=ob: e.copy(
                    out=oT[prow, hp, qcol], in_=c.ps[ob][prow, 0:128]), reads=[c.psk(ob)], writes=[('a_oT', hp, qb, half)])
    for hp in range(4):
        P.add('sync', lambda e, hp=hp: e.dma_start(out=oT_d[hp, :, :], in_=oT[:, hp, :]),
              reads=[('a_oT', hp, qb, half) for qb in range(NB) for half in range(2)], dma='a_out')


def build_attn(dims, SEQ):
    c = Ctx(dims)
    Dm = dims['D']
    x3r = c.inp("x3r", [SEQ, Dm])
    x4 = c.inp("x4", [SEQ, Dm])
    wk = c.inp("wk", [Dm, 512])
    wv = c.inp("wv", [Dm, 512])
    wq = c.inp("wq", [Dm, 512])
    negm = c.inp("c_negm", [128, 512])
    oT = c.outp("oT", [4, 128, SEQ], BF16)
    load_consts(c)
    attn_kernel(c, SEQ, x3r, x4, wk, wv, wq, oT, negm)
    return finish(c)


FULL = dict(D=D, DFF=DFF)
NT_CORE = 2048
SEQ = 4096


def alloc_lnp(c, name="lnp"):
    Dm = c.dm['D']
    return (c.sb(name + "_g", [128, Dm], F32), c.sb(name + "_b", [128, Dm], F32), name + '_g', name + '_b')


def set_lnp(c, lnp, g_d, b_d):
    g, b, gk, bk = lnp
    c.P.add('sync', lambda e: e.dma_start(out=g[:], in_=g_d.partition_broadcast(128)), writes=[gk], dma='lnp')
    c.P.add('sync', lambda e: e.dma_start(out=b[:], in_=b_d.partition_broadcast(128)), writes=[bk], dma='lnp')
    c.P.seal([gk, bk], 'lnp')


def gla_weight_inputs(c, Dm):
    return dict(w_in=c.inp("gla_w_in", [Dm, 3088]), w_gk=c.inp("gla_w_gk", [16, 512]), b_gk=c.inp("gla_b_gk", [512]),
                norm_g=c.inp("gla_norm_g", [256]), w_out=c.inp("gla_w_out", [1024, Dm]))


def build_L1(dims=FULL, NT=NT_CORE):
    c = Ctx(dims)
    Dm, Dff = dims['D'], dims['DFF']
    x_d = c.inp("x", [NT, Dm])
    wup = c.inp("w_up", [Dm, 2 * Dff])
    wdn = c.inp("w_dn", [Dff, Dm])
    lng = c.inp("ln_g", [Dm])
    lnb = c.inp("ln_b", [Dm])
    w = gla_weight_inputs(c, Dm)
    w['s_init'] = None
    w['f_out'] = c.outp("f_out", [4, 128, 256])
    x1 = c.outp("x1", [NT, Dm])
    load_consts(c)
    load_gla_consts(c)
    x_res = c.sb("x_res", [128, NT // 128, Dm], F32)
    lnp = alloc_lnp(c)
    load_x(c, x_d, x_res, 'x', NT)
    set_lnp(c, lnp, lng, lnb)
    m = c.mark()
    bufs = alloc_ffn_bufs(c, NT)
    ffn_sublayer(c, x_res, 'x', NT, wup, wdn, lnp, bufs)
    store_x(c, x1, x_res, 'x', NT)
    c.release(m)
    gb = alloc_gla_bufs(c, False)
    gla_pass(c, x_res, 'x', NT, False, w, gb, None)
    return finish(c)


def build_L2(dims=FULL, NT=NT_CORE):
    c = Ctx(dims)
    Dm, Dff = dims['D'], dims['DFF']
    x_d = c.inp("x1", [NT, Dm])
    w = gla_weight_inputs(c, Dm)
    w['s_init'] = c.inp("s_init", [4, 128, 256])
    w['f_out'] = None
    lng = [c.inp("ln_g%d" % i, [Dm]) for i in range(3)]
    lnb = [c.inp("ln_b%d" % i, [Dm]) for i in range(3)]
    wup = [c.inp("w_up%d" % i, [Dm, 2 * Dff]) for i in range(2)]
    wdn = [c.inp("w_dn%d" % i, [Dff, Dm]) for i in range(2)]
    x3 = c.outp("x3", [NT, Dm])
    x4 = c.outp("x4", [NT, Dm])
    load_consts(c)
    load_gla_consts(c)
    x_res = c.sb("x_res", [128, NT // 128, Dm], F32)
    lnp = alloc_lnp(c)
    load_x(c, x_d, x_res, 'x', NT)
    set_lnp(c, lnp, lng[0], lnb[0])
    m = c.mark()
    gb = alloc_gla_bufs(c, True)
    gla_pass(c, x_res, 'x', NT, True, w, gb, lnp)
    c.release(m)
    bufs = alloc_ffn_bufs(c, NT)
    set_lnp(c, lnp, lng[1], lnb[1])
    ffn_sublayer(c, x_res, 'x', NT, wup[0], wdn[0], lnp, bufs)
    store_x(c, x3, x_res, 'x', NT, tag='stx3')
    set_lnp(c, lnp, lng[2], lnb[2])
    ffn_sublayer(c, x_res, 'x', NT, wup[1], wdn[1], lnp, bufs)
    store_x(c, x4, x_res, 'x', NT, tag='stx4')
    return finish(c)


def build_L4(dims=FULL, NT=NT_CORE):
    c = Ctx(dims)
    P = c.P
    Dm, Dff = dims['D'], dims['DFF']
    nh = Dm // 512
    x_d = c.inp("x4", [NT, Dm])
    oT_d = c.inp("oT", [1024, NT], BF16)
    wo_d = c.inp("sb_w_out", [1024, Dm])
    lng = [c.inp("ln_g%d" % i, [Dm]) for i in range(2)]
    lnb = [c.inp("ln_b%d" % i, [Dm]) for i in range(2)]
    wup = c.inp("w_up", [Dm, 2 * Dff])
    wdn = c.inp("w_dn", [Dff, Dm])
    out = c.outp("out", [NT, Dm])
    load_consts(c)
    x_res = c.sb("x_res", [128, NT // 128, Dm], F32)
    lnp = alloc_lnp(c)
    load_x(c, x_d, x_res, 'x', NT)
    set_lnp(c, lnp, lng[0], lnb[0])
    m = c.mark()
    oTs = c.sb("oTs", [128, 8, NT], BF16)
    wo = c.sb("wo", [128, 8, Dm], BF16)
    lntmp = alloc_ln_tmp(c, "mln")
    for kc in range(8):
        P.add('sync', lambda e, kc=kc: e.dma_start(out=oTs[:, kc, :], in_=oT_d[kc * 128:(kc + 1) * 128, :]),
              writes=[('oTs', kc)], dma='oTs')
        P.add('gpsimd', lambda e, kc=kc: e.dma_start(out=wo[:, kc, :], in_=wo_d[kc * 128:(kc + 1) * 128, :]),
              writes=[('wo', kc)], dma='wo')
    P.seal([('oTs', kc) for kc in range(8)], 'oTs')
    P.seal([('wo', kc) for kc in range(8)], 'wo')
    for tt in range(NT // 128):
        par = tt % 2
        banks2 = [4 * par + hh for hh in range(nh)]
        for hh in range(nh):
            for kc in range(8):
                P.add('tensor', lambda e, kc=kc, hh=hh, tt=tt, banks2=banks2: e.matmul(
                    c.ps[banks2[hh]][:], lhsT=oTs[:, kc, tt * 128:(tt + 1) * 128], rhs=wo[:, kc, hh * 512:(hh + 1) * 512],
                    start=(kc == 0), stop=(kc == 7)),
                    reads=[('oTs', kc), ('wo', kc)], writes=[c.psk(banks2[hh])])
        ln_epilogue(c, banks2, x_res, 'x', tt, lnp, lntmp[par], ('mlntmp', par))
    c.release(m)
    bufs = alloc_ffn_bufs(c, NT)
    set_lnp(c, lnp, lng[1], lnb[1])
    ffn_sublayer(c, x_res, 'x', NT, wup, wdn, lnp, bufs)
    store_x(c, out, x_res, 'x', NT)
    return finish(c)


_CACHE = {}


def _prog(name, builder):
    if name not in _CACHE:
        _CACHE[name] = builder()
    return _CACHE[name]


def _run(nc, in_maps):
    res = run_bass_kernel_spmd(nc, in_maps, core_ids=list(range(len(in_maps))))
    return res.results


def kernel_multi(x, ln_g, ln_b, ffn_w_up, ffn_w_down, gla_w_in, gla_w_gk, gla_b_gk, gla_norm_g, gla_w_out,
                 sb_w_kv, sb_w_q, sb_w_out):
    f = lambda a: np.ascontiguousarray(np.asarray(a, dtype=np.float32))
    x, ln_g, ln_b, ffn_w_up, ffn_w_down = f(x), f(ln_g), f(ln_b), f(ffn_w_up), f(ffn_w_down)
    gla_w_in, gla_w_gk, gla_b_gk, gla_norm_g, gla_w_out = f(gla_w_in), f(gla_w_gk), f(gla_b_gk), f(gla_norm_g), f(gla_w_out)
    sb_w_kv, sb_w_q, sb_w_out = f(sb_w_kv), f(sb_w_q), f(sb_w_out)
    B = x.shape[0]
    ident = np.eye(128, dtype=np.float32)
    t2s, ind = gla_consts_host()
    negm = attn_consts_host()
    cores = [(b, h) for b in range(B) for h in range(2)]
    glaw = {"gla_w_in": gla_w_in[0], "gla_w_gk": gla_w_gk[0], "gla_b_gk": gla_b_gk[0], "gla_norm_g": gla_norm_g[0],
            "gla_w_out": gla_w_out[0], "c_ident": ident, "c_t2s": t2s, "c_ind": ind}
    ims = []
    for (b, h) in cores:
        d = dict(glaw)
        d.update({"x": f(x[b, h * NT_CORE:(h + 1) * NT_CORE]), "w_up": ffn_w_up[0, 0], "w_dn": ffn_w_down[0, 0],
                  "ln_g": ln_g[0, 0], "ln_b": ln_b[0, 0]})
        ims.append(d)
    r1 = _run(_prog("L1", build_L1), ims)
    ims = []
    zero_state = np.zeros((4, 128, 256), np.float32)
    for i, (b, h) in enumerate(cores):
        d = dict(glaw)
        d.update({"x1": r1[i]["x1"], "s_init": zero_state if h == 0 else r1[i - 1]["f_out"],
                  "ln_g0": ln_g[0, 1], "ln_b0": ln_b[0, 1], "ln_g1": ln_g[0, 2], "ln_b1": ln_b[0, 2],
                  "ln_g2": ln_g[1, 0], "ln_b2": ln_b[1, 0],
                  "w_up0": ffn_w_up[0, 1], "w_dn0": ffn_w_down[0, 1], "w_up1": ffn_w_up[1, 0], "w_dn1": ffn_w_down[1, 0]})
        ims.append(d)
    r2 = _run(_prog("L2", build_L2), ims)
    ims = []
    for (b, hh) in cores:
        x3 = np.concatenate([r2[2 * b]["x3"], r2[2 * b + 1]["x3"]], axis=0)
        x4 = np.concatenate([r2[2 * b]["x4"], r2[2 * b + 1]["x4"]], axis=0)
        ims.append({"x3r": np.ascontiguousarray(x3[::-1]), "x4": x4,
                    "wk": f(sb_w_kv[:, hh * 512:(hh + 1) * 512]), "wv": f(sb_w_kv[:, 1024 + hh * 512:1024 + (hh + 1) * 512]),
                    "wq": f(sb_w_q[0][:, hh * 512:(hh + 1) * 512]), "c_ident": ident, "c_negm": negm})
    r3 = _run(_prog("L3", lambda: build_attn(FULL, SEQ)), ims)
    ims = []
    for i, (b, h) in enumerate(cores):
        o0 = np.asarray(r3[2 * b]["oT"]).reshape(512, SEQ)
        o1 = np.asarray(r3[2 * b + 1]["oT"]).reshape(512, SEQ)
        oT = np.ascontiguousarray(np.concatenate([o0, o1], axis=0)[:, h * NT_CORE:(h + 1) * NT_CORE])
        ims.append({"x4": r2[i]["x4"], "oT": oT, "sb_w_out": sb_w_out[0], "ln_g0": ln_g[1, 1], "ln_b0": ln_b[1, 1],
                    "ln_g1": ln_g[1, 2], "ln_b1": ln_b[1, 2], "w_up": ffn_w_up[1, 1], "w_dn": ffn_w_down[1, 1],
                    "c_ident": ident})
    r4 = _run(_prog("L4", build_L4), ims)
    out = np.empty((B, SEQ, D), np.float32)
    for i, (b, h) in enumerate(cores):
        out[b, h * NT_CORE:(h + 1) * NT_CORE] = r4[i]["out"]
    return out


def kv_phase(c, x_res, xkey, hf, wkv_d, KT_d, V_d, flag, NT=NT_CORE):
    P = c.P
    Dm = c.dm['D']
    KC = Dm // 128
    m = c.mark()
    wkv = c.sb("kv_w", [128, KC, 2048], BF16)
    for kc in range(KC):
        P.add('gpsimd', lambda e, kc=kc: e.dma_start(out=wkv[:, kc, :], in_=wkv_d[kc * 128:(kc + 1) * 128, :]),
              writes=[('kv_w', kc)], dma='kv_w')
    P.seal([('kv_w', kc) for kc in range(KC)], 'kv_w')
    wkeys = [('kv_w', kc) for kc in range(KC)]
    xTr = [c.sb("kv_xT%d" % i, [128, KC, 512], BF16) for i in range(2)]
    kst = [c.sb("kv_ks%d" % i, [128, 512], BF16) for i in range(2)]
    vst = [c.sb("kv_vs%d" % i, [128, 1024], BF16) for i in range(2)]
    ntile = NT // 128
    cnt = 0
    for g in range(ntile // 4):
        xT = xTr[g % 2]
        for i in range(4):
            tt = 4 * g + 3 - i
            for k0 in range(0, KC, 4):
                bank = cnt % 2
                cnt += 1
                for kk in range(4):
                    kc = k0 + kk
                    P.add('tensor', lambda e, bank=bank, kk=kk, kc=kc, tt=tt: e.matmul(
                        c.ps[bank][:, kk * 128:(kk + 1) * 128], lhsT=x_res[:, tt, kc * 128:(kc + 1) * 128], rhs=c.antiid[:],
                        start=True, stop=True), reads=[(xkey, tt), 'antiid'], writes=[c.psk(bank)])
                src = c.ps[bank][:].rearrange("p (a b) -> p a b", a=4)
                dst = xT[:, k0:k0 + 4, i * 128:(i + 1) * 128]
                if cnt % 2 == 0:
                    P.add('scalar', lambda e, src=src, dst=dst: e.copy(out=dst, in_=src), reads=[c.psk(bank)],
                          writes=[('kv_xT', g % 2, i)])
                else:
                    P.add('vector', lambda e, src=src, dst=dst: e.tensor_copy(out=dst, in_=src), reads=[c.psk(bank)],
                          writes=[('kv_xT', g % 2, i)])
        xk = [('kv_xT', g % 2, i) for i in range(4)]
        vt0 = 31 - (hf * 16 + 4 * g + 3)
        for hp in range(8):
            bank = 2 + hp % 2
            ks = kst[hp % 2]
            for kc in range(KC):
                P.add('tensor', lambda e, hp=hp, kc=kc, bank=bank, xT=xT: e.matmul(
                    c.ps[bank][:], lhsT=wkv[:, kc, hp * 128:(hp + 1) * 128], rhs=xT[:, kc, :],
                    start=(kc == 0), stop=(kc == KC - 1)), reads=wkeys + xk, writes=[c.psk(bank)])
            P.add('scalar', lambda e, ks=ks, bank=bank: e.copy(out=ks[:], in_=c.ps[bank][:]), reads=[c.psk(bank)],
                  writes=[('kv_ks', hp % 2)])
            P.add('sync', lambda e, ks=ks, hp=hp, vt0=vt0: e.dma_start(out=KT_d[hp, :, vt0 * 128:vt0 * 128 + 512], in_=ks[:]),
                  reads=[('kv_ks', hp % 2)], dma='kv_kst%d' % (hp % 2))
        for i in range(4):
            vs = vst[i % 2]
            for hv in range(2):
                bank = 4 + 2 * (i % 2) + hv
                for kc in range(KC):
                    P.add('tensor', lambda e, i=i, hv=hv, kc=kc, bank=bank, xT=xT: e.matmul(
                        c.ps[bank][:], lhsT=xT[:, kc, i * 128:(i + 1) * 128], rhs=wkv[:, kc, 1024 + hv * 512:1024 + (hv + 1) * 512],
                        start=(kc == 0), stop=(kc == KC - 1)), reads=wkeys + [('kv_xT', g % 2, i)], writes=[c.psk(bank)])
                if hf == 0:
                    P.add('scalar', lambda e, vs=vs, hv=hv, bank=bank: e.activation(
                        out=vs[:, hv * 512:(hv + 1) * 512], in_=c.ps[bank][:], func=ACTF.Copy, scale=flag[:, 0:1]),
                        reads=[c.psk(bank), 'flag'], writes=[('kv_vs', i % 2, hv)])
                else:
                    P.add('vector', lambda e, vs=vs, hv=hv, bank=bank: e.tensor_copy(
                        out=vs[:, hv * 512:(hv + 1) * 512], in_=c.ps[bank][:]),
                        reads=[c.psk(bank)], writes=[('kv_vs', i % 2, hv)])
            P.add('sync', lambda e, vs=vs, i=i, vt0=vt0: e.dma_start(
                out=V_d[:, :, vt0 + i, :].rearrange("hp p d -> p hp d"), in_=vs[:].rearrange("p (a b) -> p a b", a=8)),
                reads=[('kv_vs', i % 2, 0), ('kv_vs', i % 2, 1)], dma='kv_vst%d' % (i % 2))
    c.release(m)


def attn_phase(c, x_res, xkey, wq_d, wo_d, KT_d, V_d, negm, lnp, NT=NT_CORE):
    P = c.P
    Dm = c.dm['D']
    KC = Dm // 128
    nh = Dm // 512
    NBQ = NT // 128
    NB = SEQ // 128
    qb0 = NB - NBQ
    m = c.mark()
    qT = c.sb("a_qT", [128, 8, NT], BF16)
    oT = c.sb("a_oT", [128, 8, NT], BF16)
    m2 = c.mark()
    wq = c.sb("a_wq", [128, KC, 1024], BF16)
    for kc in range(KC):
        P.add('gpsimd', lambda e, kc=kc: e.dma_start(out=wq[:, kc, :], in_=wq_d[kc * 128:(kc + 1) * 128, :]),
              writes=[('a_wq', kc)], dma='a_wq')
    P.seal([('a_wq', kc) for kc in range(KC)], 'a_wq')
    wkeys = [('a_wq', kc) for kc in range(KC)]
    xTa = [c.sb("a_xT%d" % i, [128, KC, 512], BF16) for i in range(2)]
    for g in range(NT // 512):
        xT = xTa[g % 2]
        transposes_to_xT(c, x_res, xkey, list(range(4 * g, 4 * g + 4)), xT, ('a_xT', g % 2), banks=[0, 1])
        xk = [(('a_xT', g % 2), i) for i in range(4)]
        for hp in range(8):
            bank = 2 + hp % 4
            for kc in range(KC):
                P.add('tensor', lambda e, hp=hp, kc=kc, bank=bank, xT=xT: e.matmul(
                    c.ps[bank][:], lhsT=wq[:, kc, hp * 128:(hp + 1) * 128], rhs=xT[:, kc, :],
                    start=(kc == 0), stop=(kc == KC - 1)), reads=wkeys + xk, writes=[c.psk(bank)])
            P.add('scalar', lambda e, hp=hp, bank=bank, g=g: e.activation(
                out=qT[:, hp, g * 512:(g + 1) * 512], in_=c.ps[bank][:], func=ACTF.Copy, scale=float(SBD ** -0.5)),
                reads=[c.psk(bank)], writes=[('a_qT', hp)])
    c.release(m2)
    zeros = c.sb("a_zeros", [128, 512], F32)
    P.add('gpsimd', lambda e: e.memset(zeros[:], 0.0), writes=['a_zeros'])
    KTb = [c.sb("a_KT%d" % i, [128, SEQ], BF16) for i in range(2)]
    Vb = [c.sb("a_V%d" % i, [128, NB, 128], BF16) for i in range(2)]
    NPB, NA, NAT = 6, 6, 3
    pbs = [c.sb("a_pb%d" % i, [128, 513], F32) for i in range(NPB)]
    As = [c.sb("a_A%d" % i, [128, 512], BF16) for i in range(NA)]
    ATs = [c.sb("a_AT%d" % i, [128, 512], BF16) for i in range(NAT)]
    tiles = []
    head_i = 0
    for hp in range(8):
        for qbl in range(NBQ):
            qb = qb0 + qbl
            r0 = 128 * (NB - 1 - qb)
            nk = 128 * (qb + 1)
            ntile = (nk + 511) // 512
            for kt in range(ntile):
                for half in range(2):
                    c0 = r0 + 512 * kt
                    tiles.append(dict(hp=hp, qbl=qbl, half=half, kt=kt, ntile=ntile, c0=c0, w=min(512, SEQ - c0),
                                      head_i=head_i + half, idx=len(tiles)))
            head_i += 2

    def load_kv(hp):
        s = hp % 2
        P.add('sync', lambda e, hp=hp, s=s: e.dma_start(out=KTb[s][:], in_=KT_d[hp, :, :]), writes=[('a_KT', s)], dma='a_ldk%d' % s)
        P.add('sync', lambda e, hp=hp, s=s: e.dma_start(out=Vb[s][:], in_=V_d[hp, :, :, :]), writes=[('a_V', s)], dma='a_ldv%d' % s)

    def stage_A(t):
        hp, half, kt, c0, w, i = t['hp'], t['half'], t['kt'], t['c0'], t['w'], t['idx']
        s = hp % 2
        prow = slice(half * 64, (half + 1) * 64)
        qcol = slice(t['qbl'] * 128, (t['qbl'] + 1) * 128)
        zb = 2 + i % 2
        ps_ = i % NPB
        as_ = i % NA
        pb, A = pbs[ps_], As[as_]
        P.add('tensor', lambda e: e.matmul(c.ps[zb][:, 0:w], lhsT=qT[prow, hp, qcol], rhs=KTb[s][prow, c0:c0 + w],
                                           start=True, stop=(kt != 0)),
              reads=[('a_qT', hp), ('a_KT', s)], writes=[c.psk(zb)])
        if kt == 0:
            P.add('tensor', lambda e: e.matmul(c.ps[zb][:, 0:w], lhsT=c.identb[:], rhs=negm[:, 0:w], start=False, stop=True),
                  reads=['identb', 'a_negm'], writes=[c.psk(zb)])
        P.add('scalar', lambda e: e.activation(out=pb[:, 1:w + 1], in_=c.ps[zb][:, 0:w], func=ACTF.Sigmoid, scale=-1.0),
              reads=[c.psk(zb)], writes=[('a_pb', ps_)])
        if kt == 0:
            P.add('vector', lambda e: e.tensor_tensor_scan(out=pb[:, 1:w + 1], data0=pb[:, 1:w + 1], data1=zeros[:, 0:w],
                                                           initial=1.0, op0=ALU.mult, op1=ALU.add),
                  reads=[('a_pb', ps_), 'a_zeros'], writes=[('a_pb', ps_)])
            P.add('gpsimd', lambda e: e.memset(A[:, 0:1], 0.0), writes=[('a_A0', as_)])
            P.add('gpsimd', lambda e: e.tensor_tensor(out=A[:, 1:w], in0=pb[:, 1:w], in1=pb[:, 2:w + 1], op=ALU.subtract),
                  reads=[('a_pb', ps_)], writes=[('a_A', as_)])
        else:
            pps = (i - 2) % NPB
            ppb, pw = pbs[pps], tiles[i - 2]['w']
            P.add('vector', lambda e: e.tensor_tensor_scan(out=pb[:, 1:w + 1], data0=pb[:, 1:w + 1], data1=zeros[:, 0:w],
                                                           initial=ppb[:, pw:pw + 1], op0=ALU.mult, op1=ALU.add),
                  reads=[('a_pb', ps_), ('a_pb', pps), 'a_zeros'], writes=[('a_pb', ps_)])
            P.add('gpsimd', lambda e: e.tensor_tensor(out=A[:, 0:1], in0=ppb[:, pw:pw + 1], in1=pb[:, 1:2], op=ALU.subtract),
                  reads=[('a_pb', ps_), ('a_pb', pps)], writes=[('a_A0', as_)])
            P.add('gpsimd', lambda e: e.tensor_tensor(out=A[:, 1:w], in0=pb[:, 1:w], in1=pb[:, 2:w + 1], op=ALU.subtract),
                  reads=[('a_pb', ps_)], writes=[('a_A', as_)])

    def stage_B(t):
        w, i = t['w'], t['idx']
        ab = 4 + i % 2
        as_ = i % NA
        at_ = i % NAT
        A, AT = As[as_], ATs[at_]
        psb = c.ps[ab][:].bitcast(BF16)
        for bi in range(w // 128):
            P.add('tensor', lambda e, bi=bi: e.transpose(out=psb[:, bi * 128:(bi + 1) * 128],
                                                         in_=A[:, bi * 128:(bi + 1) * 128], identity=c.identb[:]),
                  reads=[('a_A', as_), ('a_A0', as_), 'identb'], writes=[c.psk(ab)])
        P.add('scalar', lambda e: e.copy(out=AT[:, 0:w], in_=psb[:, 0:w]), reads=[c.psk(ab)], writes=[('a_AT', at_)])

    def stage_C(t):
        hp, half, kt, c0, w, i = t['hp'], t['half'], t['kt'], t['c0'], t['w'], t['idx']
        s = hp % 2
        at_ = i % NAT
        AT = ATs[at_]
        ob = (6, 7, 0, 1)[t['head_i'] % 4]
        nblk = w // 128
        for bi in range(nblk):
            vt = c0 // 128 + bi
            first = (kt == 0 and bi == 0)
            last = (kt == t['ntile'] - 1 and bi == nblk - 1)
            P.add('tensor', lambda e, bi=bi, vt=vt, first=first, last=last: e.matmul(
                c.ps[ob][:, 0:128], lhsT=Vb[s][:, vt, :], rhs=AT[:, bi * 128:(bi + 1) * 128], start=first, stop=last),
                reads=[('a_V', s), ('a_AT', at_)], writes=[c.psk(ob)])
        if kt == t['ntile'] - 1:
            prow = slice(half * 64, (half + 1) * 64)
            qcol = slice(t['qbl'] * 128, (t['qbl'] + 1) * 128)
            if half == 0:
                P.add('vector', lambda e: e.tensor_copy(out=oT[prow, hp, qcol], in_=c.ps[ob][prow, 0:128]),
                      reads=[c.psk(ob)], writes=[('a_oT', hp, t['qbl'], half)])
            else:
                P.add('scalar', lambda e: e.copy(out=oT[prow, hp, qcol], in_=c.ps[ob][prow, 0:128]),
                      reads=[c.psk(ob)], writes=[('a_oT', hp, t['qbl'], half)])

    n = len(tiles)
    DB, DC = 3, 4
    load_kv(0)
    for s_ in range(n + DC):
        if s_ < n:
            t = tiles[s_]
            if t['qbl'] == 0 and t['half'] == 0 and t['kt'] == 0 and t['hp'] + 1 < 8:
                load_kv(t['hp'] + 1)
            stage_A(t)
        if 0 <= s_ - DB < n:
            stage_B(tiles[s_ - DB])
        if 0 <= s_ - DC < n:
            stage_C(tiles[s_ - DC])
    c.release(m2)
    wo = c.sb("a_wo", [128, 8, Dm], BF16)
    lntmp = alloc_ln_tmp(c, "aln")
    for kc in range(8):
        P.add('gpsimd', lambda e, kc=kc: e.dma_start(out=wo[:, kc, :], in_=wo_d[kc * 128:(kc + 1) * 128, :]),
              writes=[('a_wo', kc)], dma='a_wo')
    P.seal([('a_wo', kc) for kc in range(8)], 'a_wo')
    for tt in range(NT // 128):
        par = tt % 2
        banks2 = [4 * par + hh for hh in range(nh)]
        for hh in range(nh):
            for kc in range(8):
                P.add('tensor', lambda e, kc=kc, hh=hh, tt=tt, banks2=banks2: e.matmul(
                    c.ps[banks2[hh]][:], lhsT=oT[:, kc, tt * 128:(tt + 1) * 128], rhs=wo[:, kc, hh * 512:(hh + 1) * 512],
                    start=(kc == 0), stop=(kc == 7)),
                    reads=[('a_wo', kc)], writes=[c.psk(banks2[hh])])
        ln_epilogue(c, banks2, x_res, xkey, tt, lnp, lntmp[par], ('alntmp', par))
    c.release(m)


def build_fused(dims=FULL, NT=NT_CORE):
    c = Ctx(dims)
    P, nc = c.P, c.nc
    Dm, Dff = dims['D'], dims['DFF']
    x_in = c.inp("x_in", [2 * NT, Dm])
    flag_d = c.inp("flag", [128, 1])
    ln_g = c.inp("ln_g", [DEPTH, 3, Dm])
    ln_b = c.inp("ln_b", [DEPTH, 3, Dm])
    wup = c.inp("ffn_w_up", [DEPTH, 2, Dm, 2 * Dff])
    wdn = c.inp("ffn_w_down", [DEPTH, 2, Dff, Dm])
    w = gla_weight_inputs(c, Dm)
    wkv_d = c.inp("sb_w_kv", [Dm, 2048])
    wq_d = c.inp("sb_w_q", [Dm, 1024])
    wo_d = c.inp("sb_w_out", [1024, Dm])
    negm_d = c.inp("c_negm", [128, 512])
    anti_d = c.inp("c_antiid", [128, 128])
    out = c.outp("out", [NT, Dm])
    KT_d = nc.dram_tensor("kt_scratch", [8, 128, SEQ], BF16).ap()
    V_d = nc.dram_tensor("v_scratch", [8, 128, SEQ // 128, 128], BF16).ap()
    load_consts(c)
    load_gla_consts(c)
    flag = c.sb("flag", [128, 1], F32)
    P.add('sync', lambda e: e.dma_start(out=flag[:], in_=flag_d), writes=['flag'], dma='c4')
    c.antiid = c.sb("antiid", [128, 128], F32)
    P.add('sync', lambda e: e.dma_start(out=c.antiid[:], in_=anti_d), writes=['antiid'], dma='c5')
    negm = c.sb("a_negm", [128, 512], BF16)
    P.add('gpsimd', lambda e: e.dma_start(out=negm[:], in_=negm_d), writes=['a_negm'], dma='c6')
    S = c.sb("g_S", [128, 4, 256], F32)
    P.add('vector', lambda e: e.memset(S[:], 0.0), writes=[('g_S', h) for h in range(GH)])
    x_res = c.sb("x_res", [128, NT // 128, Dm], F32)
    lnp = alloc_lnp(c)
    c.P.barrier()
    base = c.mark()

    def ffn(layer, idx):
        m = c.mark()
        bufs = alloc_ffn_bufs(c, NT)
        set_lnp(c, lnp, ln_g[layer, 2 * idx, :], ln_b[layer, 2 * idx, :])
        ffn_sublayer(c, x_res, 'x', NT, wup[layer, idx], wdn[layer, idx], lnp, bufs)
        c.release(m)

    for hf in range(2):
        load_x(c, x_in[hf * NT:(hf + 1) * NT, :], x_res, 'x', NT)
        ffn(0, 0)
        m = c.mark()
        gb = alloc_gla_bufs(c, True, S=S)
        set_lnp(c, lnp, ln_g[0, 1, :], ln_b[0, 1, :])
        if hf == 1:
            for h in range(GH):
                P.add('vector', lambda e, h=h: e.tensor_scalar(out=S[:, h, :], in0=S[:, h, :], scalar1=flag[:, 0:1], scalar2=None,
                                                              op0=ALU.mult), reads=[('g_S', h), 'flag'], writes=[('g_S', h)])
        ww = dict(w)
        ww['s_init'] = 'keep'
        ww['f_out'] = None
        gla_pass(c, x_res, 'x', NT, True, ww, gb, lnp)
        c.release(m)
        ffn(0, 1)
        kv_phase(c, x_res, 'x', hf, wkv_d, KT_d, V_d, flag, NT)
    ffn(1, 0)
    set_lnp(c, lnp, ln_g[1, 1, :], ln_b[1, 1, :])
    attn_phase(c, x_res, 'x', wq_d, wo_d, KT_d, V_d, negm, lnp, NT)
    ffn(1, 1)
    store_x(c, out, x_res, 'x', NT)
    return finish(c)


def kernel_fused(x, ln_g, ln_b, ffn_w_up, ffn_w_down, gla_w_in, gla_w_gk, gla_b_gk, gla_norm_g, gla_w_out,
                 sb_w_kv, sb_w_q, sb_w_out):
    f = lambda a: np.ascontiguousarray(np.asarray(a, dtype=np.float32))
    x = f(x)
    B = x.shape[0]
    t2s, ind = gla_consts_host()
    common = {"ln_g": f(ln_g), "ln_b": f(ln_b), "ffn_w_up": f(ffn_w_up), "ffn_w_down": f(ffn_w_down),
              "gla_w_in": f(gla_w_in[0]), "gla_w_gk": f(gla_w_gk[0]), "gla_b_gk": f(gla_b_gk[0]),
              "gla_norm_g": f(gla_norm_g[0]), "gla_w_out": f(gla_w_out[0]), "sb_w_kv": f(sb_w_kv),
              "sb_w_q": f(sb_w_q[0]), "sb_w_out": f(sb_w_out[0]), "c_ident": np.eye(128, dtype=np.float32),
              "c_t2s": t2s, "c_ind": ind, "c_negm": attn_consts_host(),
              "c_antiid": np.ascontiguousarray(np.eye(128, dtype=np.float32)[::-1])}
    cores = [(b, h) for b in range(B) for h in range(2)]
    ims = []
    for (b, h) in cores:
        d = dict(common)
        d["x_in"] = np.ascontiguousarray(np.concatenate([x[b, :NT_CORE], x[b, h * NT_CORE:(h + 1) * NT_CORE]], axis=0))
        d["flag"] = np.full((128, 1), float(h), np.float32)
        ims.append(d)
    r = _run(_prog("FUSED", build_fused), ims)
    out = np.empty((B, SEQ, D), np.float32)
    for i, (b, h) in enumerate(cores):
        out[b, h * NT_CORE:(h + 1) * NT_CORE] = r[i]["out"]
    return out


def kernel(x, ln_g, ln_b, ffn_w_up, ffn_w_down, gla_w_in, gla_w_gk, gla_b_gk, gla_norm_g, gla_w_out,
           sb_w_kv, sb_w_q, sb_w_out):
    return kernel_fused(x, ln_g, ln_b, ffn_w_up, ffn_w_down, gla_w_in, gla_w_gk, gla_b_gk, gla_norm_g, gla_w_out,
                        sb_w_kv, sb_w_q, sb_w_out)
```

```python
from contextlib import ExitStack
import numpy as np
import ml_dtypes
import concourse.bass as bass
import concourse.mybir as mybir
from concourse.bass_utils import run_bass_kernel_spmd

F32 = mybir.dt.float32
BF16 = mybir.dt.bfloat16
ALU = mybir.AluOpType
ACTF = mybir.ActivationFunctionType

ENGS = ['sync', 'tensor', 'vector', 'scalar', 'gpsimd']
SAME_ENGINE_SYNC = {'vector': True, 'scalar': True, 'gpsimd': True, 'tensor': False, 'sync': False}

D = 1024
DFF = 2816
DEPTH = 2
ALPHA = float((2 * DEPTH) ** 0.25)
LN_EPS = 1e-5
RMS_EPS = 1e-6
GH, GDK, GDV = 4, 128, 256
GATE_RANK = 16
GATE_TAU = 16.0
SBH, SBD = 16, 64
NEG_BIG = -240.0


class Op:
    __slots__ = ('eng', 'fn', 'deps', 'dma', 'tok', 'sig')

    def __init__(self, eng, fn, deps, dma):
        self.eng, self.fn, self.deps, self.dma = eng, fn, deps, dma
        self.tok = None
        self.sig = 0


class Prog:
    def __init__(self, nc):
        self.nc = nc
        self.ops = {e: [] for e in ENGS}
        self.bufs = {}
        self.dma_counts = {}
        self.keep = set()

    def add(self, eng, fn, reads=(), writes=(), dma=None, deps=()):
        d = set(t for t in deps if t is not None)
        for k in reads:
            st = self.bufs.get(k)
            if st is not None and st[0] is not None:
                d.add(st[0])
        for k in writes:
            st = self.bufs.get(k)
            if st is not None:
                if st[0] is not None:
                    d.add(st[0])
                d.update(st[1].values())
        op = Op(eng, fn, d, dma)
        idx = len(self.ops[eng])
        self.ops[eng].append(op)
        if dma is not None:
            c = self.dma_counts.get(dma, 0) + 1
            self.dma_counts[dma] = c
            tok = ('d', dma, c)
        else:
            tok = ('e', eng, idx)
        op.tok = tok
        for k in reads:
            st = self.bufs.setdefault(k, [None, {}])
            rk = eng if dma is None else ('d', dma)
            st[1][rk] = tok
        for k in writes:
            self.bufs[k] = [tok, {}]
        return tok

    def seal(self, keys, dma_key):
        tok = ('d', dma_key, self.dma_counts[dma_key])
        for k in keys:
            st = self.bufs.get(k)
            if st is None:
                continue
            if st[0] is not None and st[0][0] == 'd' and st[0][1] == dma_key:
                st[0] = tok
            rk = ('d', dma_key)
            if rk in st[1]:
                st[1][rk] = tok

    def barrier(self):
        toks = [('d', k, cnt) for k, cnt in self.dma_counts.items()]
        for e in ENGS:
            for op in reversed(self.ops[e]):
                if op.dma is None and op.fn is not None:
                    toks.append(op.tok)
                    break
        for e in ENGS:
            self.add(e, None, deps=toks)
        self.bufs = {k: v for k, v in self.bufs.items() if k in self.keep}

    def all_dma_tokens(self, prefix=None):
        return [('d', k, c) for k, c in self.dma_counts.items()
                if prefix is None or str(k).startswith(prefix)]

    def emit(self):
        nc = self.nc
        needed = set()
        for e in ENGS:
            for op in self.ops[e]:
                for t in op.deps:
                    if t[0] == 'e':
                        if t[1] == e and not SAME_ENGINE_SYNC[e]:
                            continue
                        needed.add(t)
        for e in ENGS:
            n = 0
            for op in self.ops[e]:
                if op.dma is None and op.tok in needed:
                    n += 1
                    op.sig = n
        with ExitStack() as es:
            esem = {e: es.enter_context(nc.semaphore("s_" + e)) for e in ENGS}
            dsem = {}
            for i, k in enumerate(self.dma_counts):
                dsem[k] = es.enter_context(nc.semaphore("d%d" % i))
            block = es.enter_context(nc.Block())
            for e in ENGS:
                ops = self.ops[e]

                def body(eng, e=e, ops=ops):
                    waited = {}
                    for op in ops:
                        best = {}
                        for t in op.deps:
                            if t[0] == 'e':
                                if t[1] == e and not SAME_ENGINE_SYNC[e]:
                                    continue
                                sem = esem[t[1]]
                                val = self.ops[t[1]][t[2]].sig
                                key = ('e', t[1])
                            else:
                                sem = dsem[t[1]]
                                val = 16 * t[2]
                                key = ('d', t[1])
                            if key not in best or best[key][1] < val:
                                best[key] = (sem, val)
                        for key, (sem, val) in best.items():
                            if waited.get(key, 0) >= val:
                                continue
                            eng.wait_ge(sem, val)
                            waited[key] = val
                        if op.fn is None:
                            continue
                        ins = op.fn(eng)
                        if op.dma is not None:
                            ins.then_inc(dsem[op.dma], 16)
                        elif op.sig:
                            ins.then_inc(esem[e], 1)
                getattr(block, e)(body)


class Ctx:
    ARENA_BYTES = 207 * 1024

    def __init__(self, dims):
        self.dm = dims
        self.nc = bass.Bass("TRN2", target_bir_lowering=False)
        self.P = Prog(self.nc)
        self.ps = [self.nc.alloc_psum_tensor("psb%d" % i, [128, 512], F32) for i in range(8)]
        self.uid = 0
        self.dram = {}
        self.arena = self.nc.alloc_sbuf_tensor("arena", [128, self.ARENA_BYTES // 4], F32)
        self.off = 0

    def inp(self, name, shape, dt=F32):
        t = self.nc.dram_tensor(name, list(shape), dt, kind="ExternalInput").ap()
        self.dram[name] = t
        return t

    def outp(self, name, shape, dt=F32):
        t = self.nc.dram_tensor(name, list(shape), dt, kind="ExternalOutput").ap()
        self.dram[name] = t
        return t

    def sb(self, name, shape, dt=F32):
        esz = 2 if dt == BF16 else 4
        n = 1
        for d in shape[1:]:
            n *= d
        nbytes = (n * esz + 63) // 64 * 64
        if self.off + nbytes > self.ARENA_BYTES:
            raise RuntimeError("SBUF arena overflow allocating %s (%d + %d)" % (name, self.off, nbytes))
        a = self.arena[0:shape[0], self.off // 4:(self.off + nbytes) // 4]
        self.off += nbytes
        if dt != F32:
            a = a.bitcast(dt)
        a = a[:, 0:n]
        if len(shape) == 3:
            a = a.rearrange("p (a b) -> p a b", a=shape[1])
        elif len(shape) != 2:
            raise RuntimeError("bad shape")
        return a

    def mark(self):
        return self.off

    def release(self, mark):
        self.off = mark
        self.P.barrier()

    def psk(self, i):
        return ('ps', i)


def load_consts(c):
    P = c.P
    ident_d = c.inp("c_ident", [128, 128])
    c.ident = c.sb("ident", [128, 128], F32)
    c.identb = c.sb("identb", [128, 128], BF16)
    P.add('sync', lambda e: e.dma_start(out=c.ident[:], in_=ident_d), writes=['ident'], dma='c0')
    P.add('gpsimd', lambda e: e.dma_start(out=c.identb[:], in_=ident_d), writes=['identb'], dma='c1')
    c.epsb = c.sb("epsb", [128, 2], F32)
    c.oneb = c.sb("oneb", [128, 1], F32)
    P.add('vector', lambda e: e.memset(c.oneb[:], 1.0), writes=['oneb'])
    P.add('vector', lambda e: e.memset(c.epsb[:, 0:1], LN_EPS), writes=['epsb'])
    P.add('vector', lambda e: e.memset(c.epsb[:, 1:2], RMS_EPS), writes=['epsb'])


def load_ln_params(c, name, g_d, b_d):
    Dm = c.dm['D']
    g = c.sb(name + "_g", [128, Dm], F32)
    b = c.sb(name + "_b", [128, Dm], F32)
    c.P.add('sync', lambda e: e.dma_start(out=g[:], in_=g_d.partition_broadcast(128)), writes=[name + '_g'], dma='lnp')
    c.P.add('sync', lambda e: e.dma_start(out=b[:], in_=b_d.partition_broadcast(128)), writes=[name + '_b'], dma='lnp')
    c.P.seal([name + '_g', name + '_b'], 'lnp')
    return (g, b, name + '_g', name + '_b')


def transposes_to_xT(c, x_res, xkey, tts, xT, xTkey, banks):
    P = c.P
    KC = c.dm['D'] // 128
    bi = 0
    for i, tt in enumerate(tts):
        for k0 in range(0, KC, 4):
            bank = banks[bi % len(banks)]
            bi += 1
            pt = c.ps[bank]
            for kk in range(4):
                kc = k0 + kk
                P.add('tensor', lambda e, pt=pt, kk=kk, kc=kc, tt=tt: e.transpose(
                    out=pt[:, kk * 128:(kk + 1) * 128], in_=x_res[:, tt, kc * 128:(kc + 1) * 128], identity=c.ident[:]),
                    reads=[(xkey, tt), 'ident'], writes=[c.psk(bank)])
            eng = 'scalar' if (bi % 2 == 0) else 'vector'
            src = pt[:].rearrange("p (a b) -> p a b", a=4)
            dst = xT[:, k0:k0 + 4, i * 128:(i + 1) * 128]
            if eng == 'scalar':
                P.add('scalar', lambda e, src=src, dst=dst: e.copy(out=dst, in_=src),
                      reads=[c.psk(bank)], writes=[(xTkey, i)])
            else:
                P.add('vector', lambda e, src=src, dst=dst: e.tensor_copy(out=dst, in_=src),
                      reads=[c.psk(bank)], writes=[(xTkey, i)])


def ln_epilogue(c, banks2, x_res, xkey, tt, lnp, tmp, tmpkey, extra_reads=()):
    P = c.P
    Dm = c.dm['D']
    g, b, gk, bk = lnp
    t2, st = tmp
    nh = Dm // 512
    xk = (xkey, tt)
    for h in range(nh):
        P.add('vector', lambda e, h=h: e.scalar_tensor_tensor(
            out=x_res[:, tt, h * 512:(h + 1) * 512], in0=x_res[:, tt, h * 512:(h + 1) * 512], scalar=ALPHA,
            in1=c.ps[banks2[h]][:], op0=ALU.mult, op1=ALU.add),
            reads=[xk, c.psk(banks2[h])] + list(extra_reads), writes=[xk])
    for h in range(nh):
        P.add('vector', lambda e, h=h: e.bn_stats(out=st[:, 6 * h:6 * h + 6], in_=x_res[:, tt, h * 512:(h + 1) * 512]),
              reads=[xk], writes=[(tmpkey, 'st', h)])
    P.add('vector', lambda e: e.bn_aggr(out=st[:, 12:14], in_=st[:, 0:6 * nh]),
          reads=[(tmpkey, 'st', h) for h in range(nh)], writes=[(tmpkey, 'mv')])
    P.add('scalar', lambda e: e.activation(out=st[:, 16:17], in_=st[:, 13:14], func=ACTF.Ln, bias=c.epsb[:, 0:1]),
          reads=[(tmpkey, 'mv'), 'epsb'], writes=[(tmpkey, 'lnv')])
    P.add('scalar', lambda e: e.activation(out=st[:, 14:15], in_=st[:, 16:17], func=ACTF.Exp, scale=-0.5),
          reads=[(tmpkey, 'lnv')], writes=[(tmpkey, 'rstd')])
    P.add('vector', lambda e: e.scalar_tensor_tensor(out=st[:, 15:16], in0=st[:, 12:13], scalar=-1.0, in1=st[:, 14:15],
                                                     op0=ALU.mult, op1=ALU.mult),
          reads=[(tmpkey, 'mv'), (tmpkey, 'rstd')], writes=[(tmpkey, 'nb')])
    P.add('scalar', lambda e: e.activation(out=t2[:], in_=x_res[:, tt, :], func=ACTF.Identity, bias=st[:, 15:16], scale=st[:, 14:15]),
          reads=[xk, (tmpkey, 'nb'), (tmpkey, 'rstd')], writes=[(tmpkey, 't2')])
    P.add('vector', lambda e: e.tensor_tensor(out=t2[:], in0=t2[:], in1=g[:], op=ALU.mult),
          reads=[(tmpkey, 't2'), gk], writes=[(tmpkey, 't2')])
    P.add('gpsimd', lambda e: e.tensor_tensor(out=x_res[:, tt, :], in0=t2[:], in1=b[:], op=ALU.add),
          reads=[(tmpkey, 't2'), bk], writes=[xk])


def alloc_ln_tmp(c, name):
    Dm = c.dm['D']
    return [(c.sb(name + "_t2_%d" % i, [128, Dm], F32), c.sb(name + "_st_%d" % i, [128, 32], F32)) for i in range(2)]


def ffn_sublayer(c, x_res, xkey, NT, w_up_d, w_dn_d, lnp, bufs):
    P = c.P
    Dm, Dff = c.dm['D'], c.dm['DFF']
    KC, JC = Dm // 128, Dff // 128
    ST = min(1024, NT)
    nst = NT // ST
    xT, hT, wd, wslots, sgs, lntmp = bufs['xT'], bufs['hT'], bufs['wd'], bufs['wslots'], bufs['sg'], bufs['lntmp']
    uid = c.uid
    c.uid += 1
    nsl = len(wslots)
    nh = Dm // 512
    for st in range(nst):
        tts = list(range(st * (ST // 128), (st + 1) * (ST // 128)))
        transposes_to_xT(c, x_res, xkey, tts, xT, 'xT', banks=[0, 2])
        for j in range(JC):
            P.add('gpsimd', lambda e, j=j: e.dma_start(out=wd[:, j, :], in_=w_dn_d[j * 128:(j + 1) * 128, :]),
                  writes=[('wd', j)], dma='wd')
        P.seal([('wd', j) for j in range(JC)], 'wd')
        ntk = ST // 512 if ST >= 512 else 1
        tw = min(512, ST)
        for j in range(JC):
            s = (uid * 1000 + st * JC + j) % nsl
            wg, wu = wslots[s]
            P.add('gpsimd', lambda e, j=j, wg=wg: e.dma_start(
                out=wg[:], in_=w_up_d[:, j * 128:(j + 1) * 128].rearrange("(kc p) n -> p kc n", p=128)),
                writes=[('wg', s)], dma='wg%d' % s)
            P.add('gpsimd', lambda e, j=j, wu=wu: e.dma_start(
                out=wu[:], in_=w_up_d[:, Dff + j * 128:Dff + (j + 1) * 128].rearrange("(kc p) n -> p kc n", p=128)),
                writes=[('wu', s)], dma='wu%d' % s)
            for t5 in range(ntk):
                par = (j * ntk + t5) % 2
                bg, bu = (0, 1) if par == 0 else (2, 3)
                cols = slice(t5 * tw, (t5 + 1) * tw)
                xkeys = [('xT', i) for i in range(t5 * (tw // 128), (t5 + 1) * (tw // 128))]
                for kc in range(KC):
                    P.add('tensor', lambda e, kc=kc, wg=wg, bg=bg, cols=cols: e.matmul(
                        c.ps[bg][:, 0:tw], lhsT=wg[:, kc, :], rhs=xT[:, kc, cols], start=(kc == 0), stop=(kc == KC - 1)),
                        reads=[('wg', s)] + xkeys, writes=[c.psk(bg)])
                for kc in range(KC):
                    P.add('tensor', lambda e, kc=kc, wu=wu, bu=bu, cols=cols: e.matmul(
                        c.ps[bu][:, 0:tw], lhsT=wu[:, kc, :], rhs=xT[:, kc, cols], start=(kc == 0), stop=(kc == KC - 1)),
                        reads=[('wu', s)] + xkeys, writes=[c.psk(bu)])
                sg = sgs[par]
                P.add('scalar', lambda e, sg=sg, bg=bg: e.activation(out=sg[:, 0:tw], in_=c.ps[bg][:, 0:tw], func=ACTF.Silu),
                      reads=[c.psk(bg)], writes=[('sg', par)])
                P.add('vector', lambda e, sg=sg, bu=bu, j=j, cols=cols: e.scalar_tensor_tensor(
                    out=hT[:, j, cols], in0=sg[:, 0:tw], scalar=0.5, in1=c.ps[bu][:, 0:tw], op0=ALU.mult, op1=ALU.mult),
                    reads=[('sg', par), c.psk(bu)], writes=[('hT', j, t5)])
        for i, tt in enumerate(tts):
            par = i % 2
            banks2 = [4 + 2 * par + h for h in range(nh)]
            t5 = (i * 128) // tw
            for h in range(nh):
                for j in range(JC):
                    P.add('tensor', lambda e, j=j, h=h, i=i, banks2=banks2: e.matmul(
                        c.ps[banks2[h]][:], lhsT=hT[:, j, i * 128:(i + 1) * 128], rhs=wd[:, j, h * 512:(h + 1) * 512],
                        start=(j == 0), stop=(j == JC - 1)),
                        reads=[('hT', j, t5), ('wd', j)], writes=[c.psk(banks2[h])])
            ln_epilogue(c, banks2, x_res, xkey, tt, lnp, lntmp[par], ('lntmp', par))


def alloc_ffn_bufs(c, NT):
    Dm, Dff = c.dm['D'], c.dm['DFF']
    KC, JC = Dm // 128, Dff // 128
    ST = min(1024, NT)
    b = {}
    b['xT'] = c.sb("xT", [128, KC, ST], BF16)
    b['hT'] = c.sb("hT", [128, JC, ST], BF16)
    b['wd'] = c.sb("wd", [128, JC, Dm], BF16)
    b['wslots'] = [(c.sb("wg%d" % i, [128, KC, 128], BF16), c.sb("wu%d" % i, [128, KC, 128], BF16)) for i in range(2)]
    b['sg'] = [c.sb("sg%d" % i, [128, 512], F32) for i in range(2)]
    b['lntmp'] = alloc_ln_tmp(c, "ln")
    return b


def load_x(c, x_d, x_res, xkey, NT):
    for tt in range(NT // 128):
        c.P.add('sync', lambda e, tt=tt: e.dma_start(out=x_res[:, tt, :], in_=x_d[tt * 128:(tt + 1) * 128, :]),
                writes=[(xkey, tt)], dma='ldx')
    c.P.seal([(xkey, tt) for tt in range(NT // 128)], 'ldx')


def store_x(c, x_d, x_res, xkey, NT, tag='stx'):
    for tt in range(NT // 128):
        c.P.add('sync', lambda e, tt=tt: e.dma_start(out=x_d[tt * 128:(tt + 1) * 128, :], in_=x_res[:, tt, :]),
                reads=[(xkey, tt)], dma=tag)
    c.P.seal([(xkey, tt) for tt in range(NT // 128)], tag)


def finish(c):
    toks = c.P.all_dma_tokens()
    c.P.add('sync', None, deps=toks)
    c.P.emit()
    return c.nc


def build_ffn_test(dims, NT):
    c = Ctx(dims)
    Dm, Dff = dims['D'], dims['DFF']
    x_d = c.inp("x", [NT, Dm])
    wup = c.inp("w_up", [Dm, 2 * Dff])
    wdn = c.inp("w_dn", [Dff, Dm])
    lng = c.inp("ln_g", [Dm])
    lnb = c.inp("ln_b", [Dm])
    out = c.outp("out", [NT, Dm])
    load_consts(c)
    x_res = c.sb("x_res", [128, NT // 128, Dm], F32)
    load_x(c, x_d, x_res, 'x', NT)
    lnp = load_ln_params(c, "ln0", lng, lnb)
    bufs = alloc_ffn_bufs(c, NT)
    ffn_sublayer(c, x_res, 'x', NT, wup, wdn, lnp, bufs)
    store_x(c, out, x_res, 'x', NT)
    return finish(c)


def load_gla_consts(c):
    t2_d = c.inp("c_t2s", [128, 128])
    ind_d = c.inp("c_ind", [128, 2])
    c.t2s = c.sb("t2s", [128, 128], F32)
    c.ind = c.sb("ind", [128, 2], F32)
    c.P.add('sync', lambda e: e.dma_start(out=c.t2s[:], in_=t2_d), writes=['t2s'], dma='c2')
    c.P.add('sync', lambda e: e.dma_start(out=c.ind[:], in_=ind_d), writes=['ind'], dma='c3')


def gla_consts_host():
    t = np.arange(128)
    same = (t[:, None] // 64) == (t[None, :] // 64)
    t2s = np.where(same & (t[:, None] > t[None, :]), -1.0 / GATE_TAU, 0.0).astype(np.float32)
    ind = np.zeros((128, 2), np.float32)
    ind[:64, 0] = -1.0 / GATE_TAU
    ind[64:, 1] = -1.0 / GATE_TAU
    return t2s, ind


def alloc_gla_bufs(c, full, S=None):
    Dm = c.dm['D']
    KC = Dm // 128
    b = {}
    b['win'] = c.sb("g_win", [128, KC, 3088], BF16)
    b['wgk'] = c.sb("g_wgk", [17, 512], BF16)
    b['xT'] = c.sb("g_xT", [128, KC, 512], BF16)
    b['lowT'] = c.sb("g_lowT", [17, 512], BF16)
    b['e'] = c.sb("g_e", [128, 512], F32)
    b['l'] = c.sb("g_l", [128, 512], F32)
    b['ed'] = c.sb("g_ed", [128, 512], F32)
    b['kdec'] = c.sb("g_kdec", [128, 512], BF16)
    b['v'] = c.sb("g_v", [128, 1024], BF16)
    b['decT'] = c.sb("g_decT", [128, 8], F32)
    b['S'] = S if S is not None else c.sb("g_S", [128, 4, 256], F32)
    if full:
        b['wout'] = c.sb("g_wout", [128, 8, Dm], BF16)
        b['qT'] = c.sb("g_qT", [128, 4, 512], BF16)
        b['sr'] = c.sb("g_sr", [128, 1024], F32)
        b['Sb'] = c.sb("g_Sb", [128, 4, 256], BF16)
        b['o'] = c.sb("g_o", [128, 4, 256], F32)
        b['junk'] = c.sb("g_junk", [128, 256], F32)
        b['ss'] = c.sb("g_ss", [128, 8], F32)
        b['gated'] = c.sb("g_gated", [128, 1024], BF16)
        b['gT'] = c.sb("g_gT", [128, 8, 128], BF16)
        b['ng'] = c.sb("g_ng", [128, 256], F32)
        b['lntmp'] = alloc_ln_tmp(c, "gln")
    return b


def gla_pass(c, x_res, xkey, NT, full, w, b, lnp=None):
    P = c.P
    Dm = c.dm['D']
    KC = Dm // 128
    nh = Dm // 512
    win, wgk, xT, lowT = b['win'], b['wgk'], b['xT'], b['lowT']
    S = b['S']
    for kc in range(KC):
        P.add('gpsimd', lambda e, kc=kc: e.dma_start(out=win[:, kc, :], in_=w['w_in'][kc * 128:(kc + 1) * 128, :]),
              writes=[('g_win', kc)], dma='g_win')
    P.seal([('g_win', kc) for kc in range(KC)], 'g_win')
    winkeys = [('g_win', kc) for kc in range(KC)]
    P.add('gpsimd', lambda e: e.dma_start(out=wgk[0:16, :], in_=w['w_gk']), writes=['g_wgk0'], dma='g_wgk')
    P.add('gpsimd', lambda e: e.dma_start(out=wgk[16:17, :], in_=w['b_gk'].rearrange("(o n) -> o n", o=1)),
          writes=['g_wgk1'], dma='g_wgk')
    P.seal(['g_wgk0', 'g_wgk1'], 'g_wgk')
    if isinstance(w.get('s_init'), str):
        pass
    elif w.get('s_init') is not None:
        P.add('sync', lambda e: e.dma_start(out=S[:], in_=w['s_init'].rearrange("h p d -> p h d")), writes=[('g_S', h) for h in range(GH)], dma='g_sinit')
    else:
        P.add('vector', lambda e: e.memset(S[:], 0.0), writes=[('g_S', h) for h in range(GH)])
    P.add('vector', lambda e: e.memset(lowT[:], 1.0), writes=['g_lowT'])
    if full:
        wout, qT, sr, Sb, o_sb, junk, ss, gated, gT, ng = (b[k] for k in ('wout', 'qT', 'sr', 'Sb', 'o', 'junk', 'ss', 'gated', 'gT', 'ng'))
        for kc in range(8):
            P.add('gpsimd', lambda e, kc=kc: e.dma_start(out=wout[:, kc, :], in_=w['w_out'][kc * 128:(kc + 1) * 128, :]),
                  writes=[('g_wout', kc)], dma='g_wout')
        P.seal([('g_wout', kc) for kc in range(8)], 'g_wout')
        P.add('sync', lambda e: e.dma_start(out=ng[:], in_=w['norm_g'].partition_broadcast(128)), writes=['g_ng'], dma='g_ng')
    e_sb, l_sb, ed, kdec, v_sb, decT = b['e'], b['l'], b['ed'], b['kdec'], b['v'], b['decT']
    GW = min(512, NT)
    ngrp = NT // GW
    tpg = GW // 128
    for gi in range(ngrp):
        tts = list(range(gi * tpg, (gi + 1) * tpg))
        transposes_to_xT(c, x_res, xkey, tts, xT, 'g_xT', banks=[0])
        xkeys = [('g_xT', i) for i in range(tpg)]
        if full:
            for h in range(GH):
                for kc in range(KC):
                    P.add('tensor', lambda e, h=h, kc=kc: e.matmul(
                        c.ps[0][:, 0:GW], lhsT=win[:, kc, h * 128:(h + 1) * 128], rhs=xT[:, kc, 0:GW],
                        start=(kc == 0), stop=(kc == KC - 1)),
                        reads=winkeys + xkeys, writes=[c.psk(0)])
                P.add('scalar', lambda e, h=h: e.activation(out=qT[:, h, 0:GW], in_=c.ps[0][:, 0:GW], func=ACTF.Copy,
                                                            scale=float(GDK ** -0.5)),
                      reads=[c.psk(0)], writes=[('g_qT', h)])
        for kc in range(KC):
            P.add('tensor', lambda e, kc=kc: e.matmul(
                c.ps[0][0:16, 0:GW], lhsT=win[:, kc, 3072:3088], rhs=xT[:, kc, 0:GW], start=(kc == 0), stop=(kc == KC - 1)),
                reads=winkeys + xkeys, writes=[c.psk(0)])
        P.add('vector', lambda e: e.tensor_copy(out=lowT[0:16, 0:GW], in_=c.ps[0][0:16, 0:GW]),
              reads=[c.psk(0)], writes=['g_lowT'])
        for ti, tt in enumerate(tts):
            tcol = slice(ti * 128, (ti + 1) * 128)
            for kc in range(KC):
                P.add('tensor', lambda e, kc=kc, tcol=tcol: e.matmul(
                    c.ps[1][:], lhsT=xT[:, kc, tcol], rhs=win[:, kc, 512:1024], start=(kc == 0), stop=(kc == KC - 1)),
                    reads=winkeys + [('g_xT', ti)], writes=[c.psk(1)])
            for hv in range(2):
                for kc in range(KC):
                    P.add('tensor', lambda e, kc=kc, hv=hv, tcol=tcol: e.matmul(
                        c.ps[2 + hv][:], lhsT=xT[:, kc, tcol], rhs=win[:, kc, 1024 + hv * 512:1024 + (hv + 1) * 512],
                        start=(kc == 0), stop=(kc == KC - 1)),
                        reads=winkeys + [('g_xT', ti)], writes=[c.psk(2 + hv)])
            if full:
                for hv in range(2):
                    for kc in range(KC):
                        P.add('tensor', lambda e, kc=kc, hv=hv, tcol=tcol: e.matmul(
                            c.ps[4 + hv][:], lhsT=xT[:, kc, tcol], rhs=win[:, kc, 2048 + hv * 512:2048 + (hv + 1) * 512],
                            start=(kc == 0), stop=(kc == KC - 1)),
                            reads=winkeys + [('g_xT', ti)], writes=[c.psk(4 + hv)])
            P.add('tensor', lambda e, tcol=tcol: e.matmul(c.ps[0][:], lhsT=lowT[:, tcol], rhs=wgk[:], start=True, stop=True),
                  reads=['g_lowT', 'g_wgk0', 'g_wgk1'], writes=[c.psk(0)])
            P.add('scalar', lambda e: e.activation(out=e_sb[:], in_=c.ps[0][:], func=ACTF.Exp, scale=-1.0),
                  reads=[c.psk(0)], writes=['g_e'])
            P.add('scalar', lambda e: e.activation(out=l_sb[:], in_=e_sb[:], func=ACTF.Ln, bias=c.oneb[:, 0:1]),
                  reads=['g_e', 'oneb'], writes=['g_l'])
            P.add('tensor', lambda e: e.matmul(c.ps[0][:], lhsT=c.t2s[:], rhs=l_sb[:], start=True, stop=True),
                  reads=['t2s', 'g_l'], writes=[c.psk(0)])
            P.add('scalar', lambda e: e.activation(out=ed[:], in_=c.ps[0][:], func=ACTF.Exp),
                  reads=[c.psk(0)], writes=['g_ed'])
            for h in range(GH):
                P.add('tensor', lambda e, h=h: e.matmul(c.ps[6][:, 2 * h:2 * h + 2], lhsT=l_sb[:, h * 128:(h + 1) * 128],
                                                        rhs=c.ind[:], start=True, stop=True),
                      reads=['g_l', 'ind'], writes=[c.psk(6)])
            P.add('scalar', lambda e: e.activation(out=decT[:], in_=c.ps[6][:, 0:8], func=ACTF.Exp),
                  reads=[c.psk(6)], writes=['g_decT'])
            P.add('vector', lambda e: e.tensor_tensor(out=kdec[:], in0=c.ps[1][:], in1=ed[:], op=ALU.mult),
                  reads=[c.psk(1), 'g_ed'], writes=['g_kdec'])
            for hv in range(2):
                P.add('scalar' if hv == 0 else 'vector',
                      (lambda e, hv=hv: e.copy(out=v_sb[:, hv * 512:(hv + 1) * 512], in_=c.ps[2 + hv][:])) if hv == 0 else
                      (lambda e, hv=hv: e.tensor_copy(out=v_sb[:, hv * 512:(hv + 1) * 512], in_=c.ps[2 + hv][:])),
                      reads=[c.psk(2 + hv)], writes=[('g_v', hv)])
            if full:
                for hv in range(2):
                    P.add('scalar', lambda e, hv=hv: e.activation(out=sr[:, hv * 512:(hv + 1) * 512], in_=c.ps[4 + hv][:], func=ACTF.Silu),
                          reads=[c.psk(4 + hv)], writes=[('g_sr', hv)])
            for cc in range(2):
                rows = slice(cc * 64, (cc + 1) * 64)
                for h in range(GH):
                    P.add('tensor', lambda e, h=h, rows=rows: e.matmul(
                        c.ps[2 + h // 2][:, (h % 2) * 256:(h % 2 + 1) * 256], lhsT=kdec[rows, h * 128:(h + 1) * 128],
                        rhs=v_sb[rows, h * 256:(h + 1) * 256], start=True, stop=True),
                        reads=['g_kdec', ('g_v', h // 2)], writes=[c.psk(2 + h // 2)])
                for h in range(GH):
                    P.add('vector', lambda e, h=h, cc=cc: e.scalar_tensor_tensor(
                        out=S[:, h, :], in0=S[:, h, :], scalar=decT[:, 2 * h + cc:2 * h + cc + 1],
                        in1=c.ps[2 + h // 2][:, (h % 2) * 256:(h % 2 + 1) * 256], op0=ALU.mult, op1=ALU.add),
                        reads=[('g_S', h), 'g_decT', c.psk(2 + h // 2)], writes=[('g_S', h)])
                if not full:
                    continue
                P.add('scalar', lambda e: e.copy(out=Sb[:], in_=S[:]), reads=[('g_S', h) for h in range(GH)], writes=['g_Sb'])
                for h in range(GH):
                    P.add('tensor', lambda e, h=h, tcol=tcol: e.matmul(
                        c.ps[4 + h // 2][:, (h % 2) * 256:(h % 2 + 1) * 256], lhsT=qT[:, h, tcol], rhs=Sb[:, h, :],
                        start=True, stop=True),
                        reads=[('g_qT', h), 'g_Sb'], writes=[c.psk(4 + h // 2)])
                for hv in range(2):
                    src = c.ps[4 + hv][rows, :].rearrange("p (a b) -> p a b", a=2)
                    dst = o_sb[rows, 2 * hv:2 * hv + 2, :]
                    if hv == 0:
                        P.add('vector', lambda e, src=src, dst=dst: e.tensor_copy(out=dst, in_=src),
                              reads=[c.psk(4 + hv)], writes=[('g_o', cc, hv)])
                    else:
                        P.add('scalar', lambda e, src=src, dst=dst: e.copy(out=dst, in_=src),
                              reads=[c.psk(4 + hv)], writes=[('g_o', cc, hv)])
            if not full:
                continue
            okeys = [('g_o', cc, hv) for cc in range(2) for hv in range(2)]
            for h in range(GH):
                P.add('scalar', lambda e, h=h: e.activation(out=junk[:], in_=o_sb[:, h, :], func=ACTF.Square,
                                                            accum_out=ss[:, h:h + 1]),
                      reads=okeys, writes=[('g_ss', h), 'g_junk'])
            P.add('scalar', lambda e: e.activation(out=ss[:, 4:8], in_=ss[:, 0:4], func=ACTF.Ln, bias=c.epsb[:, 1:2],
                                                   scale=1.0 / GDV),
                  reads=[('g_ss', h) for h in range(GH)] + ['epsb'], writes=['g_ss2'])
            P.add('scalar', lambda e: e.activation(out=ss[:, 4:8], in_=ss[:, 4:8], func=ACTF.Exp, scale=-0.5),
                  reads=['g_ss2'], writes=['g_ss2'])
            for h in range(GH):
                P.add('vector', lambda e, h=h: e.scalar_tensor_tensor(
                    out=o_sb[:, h, :], in0=o_sb[:, h, :], scalar=ss[:, 4 + h:5 + h], in1=ng[:], op0=ALU.mult, op1=ALU.mult),
                    reads=okeys + ['g_ss2', 'g_ng'], writes=[('g_on', h)])
            for hv in range(2):
                P.add('gpsimd' if hv == 0 else 'vector', lambda e, hv=hv: e.tensor_tensor(
                    out=gated[:, hv * 512:(hv + 1) * 512],
                    in0=o_sb[:, 2 * hv:2 * hv + 2, :].rearrange("p a b -> p (a b)"),
                    in1=sr[:, hv * 512:(hv + 1) * 512], op=ALU.mult),
                    reads=[('g_on', 2 * hv), ('g_on', 2 * hv + 1), ('g_sr', hv)], writes=[('g_gated', hv)])
            psb = c.ps[6][:].bitcast(BF16)
            for kc in range(8):
                P.add('tensor', lambda e, kc=kc: e.transpose(out=psb[:, kc * 128:(kc + 1) * 128],
                                                             in_=gated[:, kc * 128:(kc + 1) * 128], identity=c.identb[:]),
                      reads=[('g_gated', kc // 4), 'identb'], writes=[c.psk(6)])
            P.add('vector', lambda e: e.tensor_copy(out=gT[:].rearrange("p a b -> p (a b)"), in_=psb),
                  reads=[c.psk(6)], writes=['g_gT'])
            mb = [7, 1][:nh]
            for hh in range(nh):
                for kc in range(8):
                    P.add('tensor', lambda e, kc=kc, hh=hh: e.matmul(
                        c.ps[mb[hh]][:], lhsT=gT[:, kc, :], rhs=wout[:, kc, hh * 512:(hh + 1) * 512],
                        start=(kc == 0), stop=(kc == 7)),
                        reads=['g_gT', ('g_wout', kc)], writes=[c.psk(mb[hh])])
            ln_epilogue(c, mb, x_res, xkey, tt, lnp, b['lntmp'][tt % 2], ('glntmp', tt % 2))
    if w.get('f_out') is not None:
        P.add('sync', lambda e: e.dma_start(out=w['f_out'].rearrange("h p d -> p h d"), in_=S[:]), reads=[('g_S', h) for h in range(GH)], dma='g_fout')


def build_gla_test(dims, NT, full):
    c = Ctx(dims)
    Dm = dims['D']
    x_d = c.inp("x", [NT, Dm])
    w = dict(w_in=c.inp("w_in", [Dm, 3088]), w_gk=c.inp("w_gk", [16, 512]), b_gk=c.inp("b_gk", [512]),
             norm_g=c.inp("norm_g", [256]), w_out=c.inp("w_out", [1024, Dm]), s_init=c.inp("s_init", [4, 128, 256]),
             f_out=c.outp("f_out", [4, 128, 256]))
    lng = c.inp("ln_g", [Dm])
    lnb = c.inp("ln_b", [Dm])
    out = c.outp("out", [NT, Dm])
    load_consts(c)
    load_gla_consts(c)
    x_res = c.sb("x_res", [128, NT // 128, Dm], F32)
    load_x(c, x_d, x_res, 'x', NT)
    lnp = load_ln_params(c, "ln0", lng, lnb)
    bufs = alloc_gla_bufs(c, full)
    gla_pass(c, x_res, 'x', NT, full, w, bufs, lnp)
    store_x(c, out, x_res, 'x', NT)
    return finish(c)


def attn_consts_host():
    i = np.arange(128)[:, None]
    j = np.arange(512)[None, :]
    return np.where((j <= 127 - i) & (j < 128), NEG_BIG, 0.0).astype(np.float32)


def attn_kernel(c, SEQ, x3r_d, x4_d, wk_d, wv_d, wq_d, oT_d, negmask_d):
    P = c.P
    Dm = c.dm['D']
    KC = Dm // 128
    NB = SEQ // 128
    NG = SEQ // 512
    wk = c.sb("a_wk", [128, KC, 512], BF16)
    wv = c.sb("a_wv", [128, KC, 512], BF16)
    wq = c.sb("a_wq", [128, KC, 512], BF16)
    for nm, t, d in (('a_wk', wk, wk_d), ('a_wv', wv, wv_d), ('a_wq', wq, wq_d)):
        P.add('gpsimd', lambda e, t=t, d=d: e.dma_start(out=t[:], in_=d.rearrange("(kc p) n -> p kc n", p=128)),
              writes=[nm], dma=nm)
    negm = c.sb("a_negm", [128, 512], BF16)
    P.add('gpsimd', lambda e: e.dma_start(out=negm[:], in_=negmask_d), writes=['a_negm'], dma='a_negm')
    zeros = c.sb("a_zeros", [128, 512], F32)
    P.add('gpsimd', lambda e: e.memset(zeros[:], 0.0), writes=['a_zeros'])
    KT = c.sb("a_KT", [128, 4, SEQ], BF16)
    V = c.sb("a_V", [128, NB, 512], BF16)
    qT = c.sb("a_qT", [128, 4, SEQ], BF16)
    oT = c.sb("a_oT", [128, 4, SEQ], BF16)
    xs = c.sb("a_xs", [128, 4, Dm], F32)
    xTa = c.sb("a_xT", [128, KC, 512], BF16)
    for which in range(2):
        src_d = x3r_d if which == 0 else x4_d
        for g in range(NG):
            for ti in range(4):
                P.add('sync', lambda e, g=g, ti=ti, src_d=src_d: e.dma_start(
                    out=xs[:, ti, :], in_=src_d[(g * 4 + ti) * 128:(g * 4 + ti + 1) * 128, :]),
                    writes=[('a_xs', ti)], dma='a_xs')
            P.seal([('a_xs', ti) for ti in range(4)], 'a_xs')
            transposes_to_xT(c, xs, 'a_xs', [0, 1, 2, 3], xTa, 'a_xT', banks=[0, 1])
            xkeys = [('a_xT', i) for i in range(4)]
            gcol = slice(g * 512, (g + 1) * 512)
            bk = 0
            if which == 0:
                for hp in range(4):
                    bank = bk % 2
                    bk += 1
                    for kc in range(KC):
                        P.add('tensor', lambda e, hp=hp, kc=kc, bank=bank: e.matmul(
                            c.ps[bank][:], lhsT=wk[:, kc, hp * 128:(hp + 1) * 128], rhs=xTa[:, kc, :],
                            start=(kc == 0), stop=(kc == KC - 1)), reads=['a_wk'] + xkeys, writes=[c.psk(bank)])
                    P.add('scalar', lambda e, hp=hp, bank=bank, gcol=gcol: e.copy(out=KT[:, hp, gcol], in_=c.ps[bank][:]),
                          reads=[c.psk(bank)], writes=[('a_KT', hp, g)])
                for ti in range(4):
                    bank = bk % 2
                    bk += 1
                    for kc in range(KC):
                        P.add('tensor', lambda e, ti=ti, kc=kc, bank=bank: e.matmul(
                            c.ps[bank][:], lhsT=xTa[:, kc, ti * 128:(ti + 1) * 128], rhs=wv[:, kc, :],
                            start=(kc == 0), stop=(kc == KC - 1)), reads=['a_wv', ('a_xT', ti)], writes=[c.psk(bank)])
                    P.add('vector', lambda e, ti=ti, bank=bank, g=g: e.tensor_copy(out=V[:, g * 4 + ti, :], in_=c.ps[bank][:]),
                          reads=[c.psk(bank)], writes=[('a_V', g * 4 + ti)])
            else:
                for hp in range(4):
                    bank = bk % 2
                    bk += 1
                    for kc in range(KC):
                        P.add('tensor', lambda e, hp=hp, kc=kc, bank=bank: e.matmul(
                            c.ps[bank][:], lhsT=wq[:, kc, hp * 128:(hp + 1) * 128], rhs=xTa[:, kc, :],
                            start=(kc == 0), stop=(kc == KC - 1)), reads=['a_wq'] + xkeys, writes=[c.psk(bank)])
                    P.add('scalar', lambda e, hp=hp, bank=bank, gcol=gcol: e.activation(
                        out=qT[:, hp, gcol], in_=c.ps[bank][:], func=ACTF.Copy, scale=float(SBD ** -0.5)),
                        reads=[c.psk(bank)], writes=[('a_qT', hp, g)])
    NPB = 3
    pbs = [c.sb("a_pb%d" % i, [128, 513], F32) for i in range(NPB)]
    As = [c.sb("a_A%d" % i, [128, 512], BF16) for i in range(2)]
    ATs = [c.sb("a_AT%d" % i, [128, 512], BF16) for i in range(2)]
    tile_i = 0
    head_i = 0
    for qb in range(NB):
        r0 = 128 * (NB - 1 - qb)
        nk = 128 * (qb + 1)
        ntile = (nk + 511) // 512
        qcol = slice(qb * 128, (qb + 1) * 128)
        for h in range(8):
            hp, half = h // 2, h % 2
            prow = slice(half * 64, (half + 1) * 64)
            ob = 6 + head_i % 2
            head_i += 1
            prev_slot = None
            for kt in range(ntile):
                c0 = r0 + 512 * kt
                w = min(512, SEQ - c0)
                nblk = w // 128
                zb = 2 + tile_i % 2
                ab = 4 + tile_i % 2
                ps_ = tile_i % NPB
                as_ = tile_i % 2
                tile_i += 1
                pb, A, AT = pbs[ps_], As[as_], ATs[as_]
                kkeys = [('a_KT', hp, gg) for gg in range(c0 // 512, (c0 + w - 1) // 512 + 1)]
                P.add('tensor', lambda e, hp=hp, prow=prow, qcol=qcol, c0=c0, w=w, zb=zb, kt=kt: e.matmul(
                    c.ps[zb][:, 0:w], lhsT=qT[prow, hp, qcol], rhs=KT[prow, hp, c0:c0 + w], start=True, stop=(kt != 0)),
                    reads=[('a_qT', hp, qb // 4)] + kkeys, writes=[c.psk(zb)])
                if kt == 0:
                    P.add('tensor', lambda e, w=w, zb=zb: e.matmul(
                        c.ps[zb][:, 0:w], lhsT=c.identb[:], rhs=negm[:, 0:w], start=False, stop=True),
                        reads=['identb', 'a_negm'], writes=[c.psk(zb)])
                P.add('scalar', lambda e, pb=pb, zb=zb, w=w: e.activation(out=pb[:, 1:w + 1], in_=c.ps[zb][:, 0:w],
                                                                          func=ACTF.Sigmoid, scale=-1.0),
                      reads=[c.psk(zb)], writes=[('a_pb', ps_)])
                if kt == 0:
                    P.add('vector', lambda e, pb=pb: e.memset(pb[:, 0:1], 1.0), writes=[('a_pb0', ps_)],
                          reads=[])
                else:
                    ppb, pw = pbs[prev_slot[0]], prev_slot[1]
                    P.add('vector', lambda e, pb=pb, ppb=ppb, pw=pw: e.tensor_copy(out=pb[:, 0:1], in_=ppb[:, pw:pw + 1]),
                          reads=[('a_pb', prev_slot[0])], writes=[('a_pb0', ps_)])
                P.add('vector', lambda e, pb=pb, w=w: e.tensor_tensor_scan(
                    out=pb[:, 1:w + 1], data0=pb[:, 1:w + 1], data1=zeros[:, 0:w], initial=pb[:, 0:1],
                    op0=ALU.mult, op1=ALU.add),
                    reads=[('a_pb', ps_), ('a_pb0', ps_), 'a_zeros'], writes=[('a_pb', ps_)])
                P.add('gpsimd', lambda e, pb=pb, A=A, w=w: e.tensor_tensor(out=A[:, 0:w], in0=pb[:, 0:w], in1=pb[:, 1:w + 1],
                                                                          op=ALU.subtract),
                      reads=[('a_pb', ps_), ('a_pb0', ps_)], writes=[('a_A', as_)])
                psb = c.ps[ab][:].bitcast(BF16)
                for bi in range(nblk):
                    P.add('tensor', lambda e, bi=bi, A=A, psb=psb: e.transpose(
                        out=psb[:, bi * 128:(bi + 1) * 128], in_=A[:, bi * 128:(bi + 1) * 128], identity=c.identb[:]),
                        reads=[('a_A', as_), 'identb'], writes=[c.psk(ab)])
                P.add('scalar', lambda e, AT=AT, psb=psb, w=w: e.copy(out=AT[:, 0:w], in_=psb[:, 0:w]),
                      reads=[c.psk(ab)], writes=[('a_AT', as_)])
                for bi in range(nblk):
                    vt = c0 // 128 + bi
                    first = (kt == 0 and bi == 0)
                    last = (kt == ntile - 1 and bi == nblk - 1)
                    P.add('tensor', lambda e, bi=bi, vt=vt, hp=hp, AT=AT, ob=ob, first=first, last=last: e.matmul(
                        c.ps[ob][:, 0:128], lhsT=V[:, vt, hp * 128:(hp + 1) * 128], rhs=AT[:, bi * 128:(bi + 1) * 128],
                        start=first, stop=last),
                        reads=[('a_V', vt), ('a_AT', as_)], writes=[c.psk(ob)])
                prev_slot = (ps_, w)
            eng = 'vector' if half == 0 else 'scalar'
            if eng == 'vector':
                P.add('vector', lambda e, prow=prow, hp=hp, qcol=qcol, ob=ob: e.tensor_copy(
                    out=oT[prow, hp, qcol], in_=c.ps[ob][prow, 0:128]), reads=[c.psk(ob)], writes=[('a_oT', hp, qb, half)])
            else:
                P.add('scalar', lambda e, prow=prow, hp=hp, qcol=qcol, ob=ob: e.copy(
                    out=oT[prow, hp, qcol], in_=c.ps[ob][prow, 0:128]), reads=[c.psk(ob)], writes=[('a_oT', hp, qb, half)])
    for hp in range(4):
        P.add('sync', lambda e, hp=hp: e.dma_start(out=oT_d[hp, :, :], in_=oT[:, hp, :]),
              reads=[('a_oT', hp, qb, half) for qb in range(NB) for half in range(2)], dma='a_out')


def build_attn(dims, SEQ):
    c = Ctx(dims)
    Dm = dims['D']
    x3r = c.inp("x3r", [SEQ, Dm])
    x4 = c.inp("x4", [SEQ, Dm])
    wk = c.inp("wk", [Dm, 512])
    wv = c.inp("wv", [Dm, 512])
    wq = c.inp("wq", [Dm, 512])
    negm = c.inp("c_negm", [128, 512])
    oT = c.outp("oT", [4, 128, SEQ], BF16)
    load_consts(c)
    attn_kernel(c, SEQ, x3r, x4, wk, wv, wq, oT, negm)
    return finish(c)


FULL = dict(D=D, DFF=DFF)
NT_CORE = 2048
SEQ = 4096


def alloc_lnp(c, name="lnp"):
    Dm = c.dm['D']
    return (c.sb(name + "_g", [128, Dm], F32), c.sb(name + "_b", [128, Dm], F32), name + '_g', name + '_b')


def set_lnp(c, lnp, g_d, b_d):
    g, b, gk, bk = lnp
    c.P.add('sync', lambda e: e.dma_start(out=g[:], in_=g_d.partition_broadcast(128)), writes=[gk], dma='lnp')
    c.P.add('sync', lambda e: e.dma_start(out=b[:], in_=b_d.partition_broadcast(128)), writes=[bk], dma='lnp')
    c.P.seal([gk, bk], 'lnp')


def gla_weight_inputs(c, Dm):
    return dict(w_in=c.inp("gla_w_in", [Dm, 3088]), w_gk=c.inp("gla_w_gk", [16, 512]), b_gk=c.inp("gla_b_gk", [512]),
                norm_g=c.inp("gla_norm_g", [256]), w_out=c.inp("gla_w_out", [1024, Dm]))


def build_L1(dims=FULL, NT=NT_CORE):
    c = Ctx(dims)
    Dm, Dff = dims['D'], dims['DFF']
    x_d = c.inp("x", [NT, Dm])
    wup = c.inp("w_up", [Dm, 2 * Dff])
    wdn = c.inp("w_dn", [Dff, Dm])
    lng = c.inp("ln_g", [Dm])
    lnb = c.inp("ln_b", [Dm])
    w = gla_weight_inputs(c, Dm)
    w['s_init'] = None
    w['f_out'] = c.outp("f_out", [4, 128, 256])
    x1 = c.outp("x1", [NT, Dm])
    load_consts(c)
    load_gla_consts(c)
    x_res = c.sb("x_res", [128, NT // 128, Dm], F32)
    lnp = alloc_lnp(c)
    load_x(c, x_d, x_res, 'x', NT)
    set_lnp(c, lnp, lng, lnb)
    m = c.mark()
    bufs = alloc_ffn_bufs(c, NT)
    ffn_sublayer(c, x_res, 'x', NT, wup, wdn, lnp, bufs)
    store_x(c, x1, x_res, 'x', NT)
    c.release(m)
    gb = alloc_gla_bufs(c, False)
    gla_pass(c, x_res, 'x', NT, False, w, gb, None)
    return finish(c)


def build_L2(dims=FULL, NT=NT_CORE):
    c = Ctx(dims)
    Dm, Dff = dims['D'], dims['DFF']
    x_d = c.inp("x1", [NT, Dm])
    w = gla_weight_inputs(c, Dm)
    w['s_init'] = c.inp("s_init", [4, 128, 256])
    w['f_out'] = None
    lng = [c.inp("ln_g%d" % i, [Dm]) for i in range(3)]
    lnb = [c.inp("ln_b%d" % i, [Dm]) for i in range(3)]
    wup = [c.inp("w_up%d" % i, [Dm, 2 * Dff]) for i in range(2)]
    wdn = [c.inp("w_dn%d" % i, [Dff, Dm]) for i in range(2)]
    x3 = c.outp("x3", [NT, Dm])
    x4 = c.outp("x4", [NT, Dm])
    load_consts(c)
    load_gla_consts(c)
    x_res = c.sb("x_res", [128, NT // 128, Dm], F32)
    lnp = alloc_lnp(c)
    load_x(c, x_d, x_res, 'x', NT)
    set_lnp(c, lnp, lng[0], lnb[0])
    m = c.mark()
    gb = alloc_gla_bufs(c, True)
    gla_pass(c, x_res, 'x', NT, True, w, gb, lnp)
    c.release(m)
    bufs = alloc_ffn_bufs(c, NT)
    set_lnp(c, lnp, lng[1], lnb[1])
    ffn_sublayer(c, x_res, 'x', NT, wup[0], wdn[0], lnp, bufs)
    store_x(c, x3, x_res, 'x', NT, tag='stx3')
    set_lnp(c, lnp, lng[2], lnb[2])
    ffn_sublayer(c, x_res, 'x', NT, wup[1], wdn[1], lnp, bufs)
    store_x(c, x4, x_res, 'x', NT, tag='stx4')
    return finish(c)


def build_L4(dims=FULL, NT=NT_CORE):
    c = Ctx(dims)
    P = c.P
    Dm, Dff = dims['D'], dims['DFF']
    nh = Dm // 512
    x_d = c.inp("x4", [NT, Dm])
    oT_d = c.inp("oT", [1024, NT], BF16)
    wo_d = c.inp("sb_w_out", [1024, Dm])
    lng = [c.inp("ln_g%d" % i, [Dm]) for i in range(2)]
    lnb = [c.inp("ln_b%d" % i, [Dm]) for i in range(2)]
    wup = c.inp("w_up", [Dm, 2 * Dff])
    wdn = c.inp("w_dn", [Dff, Dm])
    out = c.outp("out", [NT, Dm])
    load_consts(c)
    x_res = c.sb("x_res", [128, NT // 128, Dm], F32)
    lnp = alloc_lnp(c)
    load_x(c, x_d, x_res, 'x', NT)
    set_lnp(c, lnp, lng[0], lnb[0])
    m = c.mark()
    oTs = c.sb("oTs", [128, 8, NT], BF16)
    wo = c.sb("wo", [128, 8, Dm], BF16)
    lntmp = alloc_ln_tmp(c, "mln")
    for kc in range(8):
        P.add('sync', lambda e, kc=kc: e.dma_start(out=oTs[:, kc, :], in_=oT_d[kc * 128:(kc + 1) * 128, :]),
              writes=[('oTs', kc)], dma='oTs')
        P.add('gpsimd', lambda e, kc=kc: e.dma_start(out=wo[:, kc, :], in_=wo_d[kc * 128:(kc + 1) * 128, :]),
              writes=[('wo', kc)], dma='wo')
    P.seal([('oTs', kc) for kc in range(8)], 'oTs')
    P.seal([('wo', kc) for kc in range(8)], 'wo')
    for tt in range(NT // 128):
        par = tt % 2
        banks2 = [4 * par + hh for hh in range(nh)]
        for hh in range(nh):
            for kc in range(8):
                P.add('tensor', lambda e, kc=kc, hh=hh, tt=tt, banks2=banks2: e.matmul(
                    c.ps[banks2[hh]][:], lhsT=oTs[:, kc, tt * 128:(tt + 1) * 128], rhs=wo[:, kc, hh * 512:(hh + 1) * 512],
                    start=(kc == 0), stop=(kc == 7)),
                    reads=[('oTs', kc), ('wo', kc)], writes=[c.psk(banks2[hh])])
        ln_epilogue(c, banks2, x_res, 'x', tt, lnp, lntmp[par], ('mlntmp', par))
    c.release(m)
    bufs = alloc_ffn_bufs(c, NT)
    set_lnp(c, lnp, lng[1], lnb[1])
    ffn_sublayer(c, x_res, 'x', NT, wup, wdn, lnp, bufs)
    store_x(c, out, x_res, 'x', NT)
    return finish(c)


_CACHE = {}


def _prog(name, builder):
    if name not in _CACHE:
        _CACHE[name] = builder()
    return _CACHE[name]


def _run(nc, in_maps):
    res = run_bass_kernel_spmd(nc, in_maps, core_ids=list(range(len(in_maps))))
    return res.results


def kernel_multi(x, ln_g, ln_b, ffn_w_up, ffn_w_down, gla_w_in, gla_w_gk, gla_b_gk, gla_norm_g, gla_w_out,
                 sb_w_kv, sb_w_q, sb_w_out):
    f = lambda a: np.ascontiguousarray(np.asarray(a, dtype=np.float32))
    x, ln_g, ln_b, ffn_w_up, ffn_w_down = f(x), f(ln_g), f(ln_b), f(ffn_w_up), f(ffn_w_down)
    gla_w_in, gla_w_gk, gla_b_gk, gla_norm_g, gla_w_out = f(gla_w_in), f(gla_w_gk), f(gla_b_gk), f(gla_norm_g), f(gla_w_out)
    sb_w_kv, sb_w_q, sb_w_out = f(sb_w_kv), f(sb_w_q), f(sb_w_out)
    B = x.shape[0]
    ident = np.eye(128, dtype=np.float32)
    t2s, ind = gla_consts_host()
    negm = attn_consts_host()
    cores = [(b, h) for b in range(B) for h in range(2)]
    glaw = {"gla_w_in": gla_w_in[0], "gla_w_gk": gla_w_gk[0], "gla_b_gk": gla_b_gk[0], "gla_norm_g": gla_norm_g[0],
            "gla_w_out": gla_w_out[0], "c_ident": ident, "c_t2s": t2s, "c_ind": ind}
    ims = []
    for (b, h) in cores:
        d = dict(glaw)
        d.update({"x": f(x[b, h * NT_CORE:(h + 1) * NT_CORE]), "w_up": ffn_w_up[0, 0], "w_dn": ffn_w_down[0, 0],
                  "ln_g": ln_g[0, 0], "ln_b": ln_b[0, 0]})
        ims.append(d)
    r1 = _run(_prog("L1", build_L1), ims)
    ims = []
    zero_state = np.zeros((4, 128, 256), np.float32)
    for i, (b, h) in enumerate(cores):
        d = dict(glaw)
        d.update({"x1": r1[i]["x1"], "s_init": zero_state if h == 0 else r1[i - 1]["f_out"],
                  "ln_g0": ln_g[0, 1], "ln_b0": ln_b[0, 1], "ln_g1": ln_g[0, 2], "ln_b1": ln_b[0, 2],
                  "ln_g2": ln_g[1, 0], "ln_b2": ln_b[1, 0],
                  "w_up0": ffn_w_up[0, 1], "w_dn0": ffn_w_down[0, 1], "w_up1": ffn_w_up[1, 0], "w_dn1": ffn_w_down[1, 0]})
        ims.append(d)
    r2 = _run(_prog("L2", build_L2), ims)
    ims = []
    for (b, hh) in cores:
        x3 = np.concatenate([r2[2 * b]["x3"], r2[2 * b + 1]["x3"]], axis=0)
        x4 = np.concatenate([r2[2 * b]["x4"], r2[2 * b + 1]["x4"]], axis=0)
        ims.append({"x3r": np.ascontiguousarray(x3[::-1]), "x4": x4,
                    "wk": f(sb_w_kv[:, hh * 512:(hh + 1) * 512]), "wv": f(sb_w_kv[:, 1024 + hh * 512:1024 + (hh + 1) * 512]),
                    "wq": f(sb_w_q[0][:, hh * 512:(hh + 1) * 512]), "c_ident": ident, "c_negm": negm})
    r3 = _run(_prog("L3", lambda: build_attn(FULL, SEQ)), ims)
    ims = []
    for i, (b, h) in enumerate(cores):
        o0 = np.asarray(r3[2 * b]["oT"]).reshape(512, SEQ)
        o1 = np.asarray(r3[2 * b + 1]["oT"]).reshape(512, SEQ)
        oT = np.ascontiguousarray(np.concatenate([o0, o1], axis=0)[:, h * NT_CORE:(h + 1) * NT_CORE])
        ims.append({"x4": r2[i]["x4"], "oT": oT, "sb_w_out": sb_w_out[0], "ln_g0": ln_g[1, 1], "ln_b0": ln_b[1, 1],
                    "ln_g1": ln_g[1, 2], "ln_b1": ln_b[1, 2], "w_up": ffn_w_up[1, 1], "w_dn": ffn_w_down[1, 1],
                    "c_ident": ident})
    r4 = _run(_prog("L4", build_L4), ims)
    out = np.empty((B, SEQ, D), np.float32)
    for i, (b, h) in enumerate(cores):
        out[b, h * NT_CORE:(h + 1) * NT_CORE] = r4[i]["out"]
    return out


def kv_phase(c, x_res, xkey, hf, wkv_d, KT_d, V_d, flag, NT=NT_CORE):
    P = c.P
    Dm = c.dm['D']
    KC = Dm // 128
    m = c.mark()
    wkv = c.sb("kv_w", [128, KC, 2048], BF16)
    for kc in range(KC):
        P.add('gpsimd', lambda e, kc=kc: e.dma_start(out=wkv[:, kc, :], in_=wkv_d[kc * 128:(kc + 1) * 128, :]),
              writes=[('kv_w', kc)], dma='kv_w')
    P.seal([('kv_w', kc) for kc in range(KC)], 'kv_w')
    wkeys = [('kv_w', kc) for kc in range(KC)]
    xTr = [c.sb("kv_xT%d" % i, [128, KC, 512], BF16) for i in range(2)]
    kst = [c.sb("kv_ks%d" % i, [128, 512], BF16) for i in range(2)]
    vst = [c.sb("kv_vs%d" % i, [128, 1024], BF16) for i in range(2)]
    ntile = NT // 128
    cnt = 0
    for g in range(ntile // 4):
        xT = xTr[g % 2]
        for i in range(4):
            tt = 4 * g + 3 - i
            for k0 in range(0, KC, 4):
                bank = cnt % 2
                cnt += 1
                for kk in range(4):
                    kc = k0 + kk
                    P.add('tensor', lambda e, bank=bank, kk=kk, kc=kc, tt=tt: e.matmul(
                        c.ps[bank][:, kk * 128:(kk + 1) * 128], lhsT=x_res[:, tt, kc * 128:(kc + 1) * 128], rhs=c.antiid[:],
                        start=True, stop=True), reads=[(xkey, tt), 'antiid'], writes=[c.psk(bank)])
                src = c.ps[bank][:].rearrange("p (a b) -> p a b", a=4)
                dst = xT[:, k0:k0 + 4, i * 128:(i + 1) * 128]
                if cnt % 2 == 0:
                    P.add('scalar', lambda e, src=src, dst=dst: e.copy(out=dst, in_=src), reads=[c.psk(bank)],
                          writes=[('kv_xT', g % 2, i)])
                else:
                    P.add('vector', lambda e, src=src, dst=dst: e.tensor_copy(out=dst, in_=src), reads=[c.psk(bank)],
                          writes=[('kv_xT', g % 2, i)])
        xk = [('kv_xT', g % 2, i) for i in range(4)]
        vt0 = 31 - (hf * 16 + 4 * g + 3)
        for hp in range(8):
            bank = 2 + hp % 2
            ks = kst[hp % 2]
            for kc in range(KC):
                P.add('tensor', lambda e, hp=hp, kc=kc, bank=bank, xT=xT: e.matmul(
                    c.ps[bank][:], lhsT=wkv[:, kc, hp * 128:(hp + 1) * 128], rhs=xT[:, kc, :],
                    start=(kc == 0), stop=(kc == KC - 1)), reads=wkeys + xk, writes=[c.psk(bank)])
            P.add('scalar', lambda e, ks=ks, bank=bank: e.copy(out=ks[:], in_=c.ps[bank][:]), reads=[c.psk(bank)],
                  writes=[('kv_ks', hp % 2)])
            P.add('sync', lambda e, ks=ks, hp=hp, vt0=vt0: e.dma_start(out=KT_d[hp, :, vt0 * 128:vt0 * 128 + 512], in_=ks[:]),
                  reads=[('kv_ks', hp % 2)], dma='kv_kst%d' % (hp % 2))
        for i in range(4):
            vs = vst[i % 2]
            for hv in range(2):
                bank = 4 + 2 * (i % 2) + hv
                for kc in range(KC):
                    P.add('tensor', lambda e, i=i, hv=hv, kc=kc, bank=bank, xT=xT: e.matmul(
                        c.ps[bank][:], lhsT=xT[:, kc, i * 128:(i + 1) * 128], rhs=wkv[:, kc, 1024 + hv * 512:1024 + (hv + 1) * 512],
                        start=(kc == 0), stop=(kc == KC - 1)), reads=wkeys + [('kv_xT', g % 2, i)], writes=[c.psk(bank)])
                if hf == 0:
                    P.add('scalar', lambda e, vs=vs, hv=hv, bank=bank: e.activation(
                        out=vs[:, hv * 512:(hv + 1) * 512], in_=c.ps[bank][:], func=ACTF.Copy, scale=flag[:, 0:1]),
                        reads=[c.psk(bank), 'flag'], writes=[('kv_vs', i % 2, hv)])
                else:
                    P.add('vector', lambda e, vs=vs, hv=hv, bank=bank: e.tensor_copy(
                        out=vs[:, hv * 512:(hv + 1) * 512], in_=c.ps[bank][:]),
                        reads=[c.psk(bank)], writes=[('kv_vs', i % 2, hv)])
            P.add('sync', lambda e, vs=vs, i=i, vt0=vt0: e.dma_start(
                out=V_d[:, :, vt0 + i, :].rearrange("hp p d -> p hp d"), in_=vs[:].rearrange("p (a b) -> p a b", a=8)),
                reads=[('kv_vs', i % 2, 0), ('kv_vs', i % 2, 1)], dma='kv_vst%d' % (i % 2))
    c.release(m)


def attn_phase(c, x_res, xkey, wq_d, wo_d, KT_d, V_d, negm, lnp, NT=NT_CORE):
    P = c.P
    Dm = c.dm['D']
    KC = Dm // 128
    nh = Dm // 512
    NBQ = NT // 128
    NB = SEQ // 128
    qb0 = NB - NBQ
    m = c.mark()
    qT = c.sb("a_qT", [128, 8, NT], BF16)
    oT = c.sb("a_oT", [128, 8, NT], BF16)
    m2 = c.mark()
    wq = c.sb("a_wq", [128, KC, 1024], BF16)
    for kc in range(KC):
        P.add('gpsimd', lambda e, kc=kc: e.dma_start(out=wq[:, kc, :], in_=wq_d[kc * 128:(kc + 1) * 128, :]),
              writes=[('a_wq', kc)], dma='a_wq')
    P.seal([('a_wq', kc) for kc in range(KC)], 'a_wq')
    wkeys = [('a_wq', kc) for kc in range(KC)]
    xTa = [c.sb("a_xT%d" % i, [128, KC, 512], BF16) for i in range(2)]
    for g in range(NT // 512):
        xT = xTa[g % 2]
        transposes_to_xT(c, x_res, xkey, list(range(4 * g, 4 * g + 4)), xT, ('a_xT', g % 2), banks=[0, 1])
        xk = [(('a_xT', g % 2), i) for i in range(4)]
        for hp in range(8):
            bank = 2 + hp % 4
            for kc in range(KC):
                P.add('tensor', lambda e, hp=hp, kc=kc, bank=bank, xT=xT: e.matmul(
                    c.ps[bank][:], lhsT=wq[:, kc, hp * 128:(hp + 1) * 128], rhs=xT[:, kc, :],
                    start=(kc == 0), stop=(kc == KC - 1)), reads=wkeys + xk, writes=[c.psk(bank)])
            P.add('scalar', lambda e, hp=hp, bank=bank, g=g: e.activation(
                out=qT[:, hp, g * 512:(g + 1) * 512], in_=c.ps[bank][:], func=ACTF.Copy, scale=float(SBD ** -0.5)),
                reads=[c.psk(bank)], writes=[('a_qT', hp)])
    c.release(m2)
    zeros = c.sb("a_zeros", [128, 512], F32)
    P.add('gpsimd', lambda e: e.memset(zeros[:], 0.0), writes=['a_zeros'])
    KTb = [c.sb("a_KT%d" % i, [128, SEQ], BF16) for i in range(2)]
    Vb = [c.sb("a_V%d" % i, [128, NB, 128], BF16) for i in range(2)]
    NPB, NA, NAT = 6, 6, 3
    pbs = [c.sb("a_pb%d" % i, [128, 513], F32) for i in range(NPB)]
    As = [c.sb("a_A%d" % i, [128, 512], BF16) for i in range(NA)]
    ATs = [c.sb("a_AT%d" % i, [128, 512], BF16) for i in range(NAT)]
    tiles = []
    head_i = 0
    for hp in range(8):
        for qbl in range(NBQ):
            qb = qb0 + qbl
            r0 = 128 * (NB - 1 - qb)
            nk = 128 * (qb + 1)
            ntile = (nk + 511) // 512
            for kt in range(ntile):
                for half in range(2):
                    c0 = r0 + 512 * kt
                    tiles.append(dict(hp=hp, qbl=qbl, half=half, kt=kt, ntile=ntile, c0=c0, w=min(512, SEQ - c0),
                                      head_i=head_i + half, idx=len(tiles)))
            head_i += 2

    def load_kv(hp):
        s = hp % 2
        P.add('sync', lambda e, hp=hp, s=s: e.dma_start(out=KTb[s][:], in_=KT_d[hp, :, :]), writes=[('a_KT', s)], dma='a_ldk%d' % s)
        P.add('sync', lambda e, hp=hp, s=s: e.dma_start(out=Vb[s][:], in_=V_d[hp, :, :, :]), writes=[('a_V', s)], dma='a_ldv%d' % s)

    def stage_A(t):
        hp, half, kt, c0, w, i = t['hp'], t['half'], t['kt'], t['c0'], t['w'], t['idx']
        s = hp % 2
        prow = slice(half * 64, (half + 1) * 64)
        qcol = slice(t['qbl'] * 128, (t['qbl'] + 1) * 128)
        zb = 2 + i % 2
        ps_ = i % NPB
        as_ = i % NA
        pb, A = pbs[ps_], As[as_]
        P.add('tensor', lambda e: e.matmul(c.ps[zb][:, 0:w], lhsT=qT[prow, hp, qcol], rhs=KTb[s][prow, c0:c0 + w],
                                           start=True, stop=(kt != 0)),
              reads=[('a_qT', hp), ('a_KT', s)], writes=[c.psk(zb)])
        if kt == 0:
            P.add('tensor', lambda e: e.matmul(c.ps[zb][:, 0:w], lhsT=c.identb[:], rhs=negm[:, 0:w], start=False, stop=True),
                  reads=['identb', 'a_negm'], writes=[c.psk(zb)])
        P.add('scalar', lambda e: e.activation(out=pb[:, 1:w + 1], in_=c.ps[zb][:, 0:w], func=ACTF.Sigmoid, scale=-1.0),
              reads=[c.psk(zb)], writes=[('a_pb', ps_)])
        if kt == 0:
            P.add('scalar', lambda e: e.copy(out=pb[:, 0:1], in_=c.oneb[:, 0:1]), reads=['oneb'], writes=[('a_pb0', ps_)])
            P.add('vector', lambda e: e.tensor_tensor_scan(out=pb[:, 1:w + 1], data0=pb[:, 1:w + 1], data1=zeros[:, 0:w],
                                                           initial=1.0, op0=ALU.mult, op1=ALU.add),
                  reads=[('a_pb', ps_), 'a_zeros'], writes=[('a_pb', ps_)])
        else:
            pps = (i - 2) % NPB
            ppb, pw = pbs[pps], tiles[i - 2]['w']
            P.add('scalar', lambda e: e.copy(out=pb[:, 0:1], in_=ppb[:, pw:pw + 1]), reads=[('a_pb', pps)], writes=[('a_pb0', ps_)])
            P.add('vector', lambda e: e.tensor_tensor_scan(out=pb[:, 1:w + 1], data0=pb[:, 1:w + 1], data1=zeros[:, 0:w],
                                                           initial=ppb[:, pw:pw + 1], op0=ALU.mult, op1=ALU.add),
                  reads=[('a_pb', ps_), ('a_pb', pps), 'a_zeros'], writes=[('a_pb', ps_)])
        P.add('gpsimd', lambda e: e.tensor_tensor(out=A[:, 0:w], in0=pb[:, 0:w], in1=pb[:, 1:w + 1], op=ALU.subtract),
              reads=[('a_pb', ps_), ('a_pb0', ps_)], writes=[('a_A', as_)])

    def stage_B(t):
        w, i = t['w'], t['idx']
        ab = 4 + i % 2
        as_ = i % NA
        at_ = i % NAT
        A, AT = As[as_], ATs[at_]
        psb = c.ps[ab][:].bitcast(BF16)
        for bi in range(w // 128):
            P.add('tensor', lambda e, bi=bi: e.transpose(out=psb[:, bi * 128:(bi + 1) * 128],
                                                         in_=A[:, bi * 128:(bi + 1) * 128], identity=c.identb[:]),
                  reads=[('a_A', as_), 'identb'], writes=[c.psk(ab)])
        P.add('scalar', lambda e: e.copy(out=AT[:, 0:w], in_=psb[:, 0:w]), reads=[c.psk(ab)], writes=[('a_AT', at_)])

    def stage_C(t):
        hp, half, kt, c0, w, i = t['hp'], t['half'], t['kt'], t['c0'], t['w'], t['idx']
        s = hp % 2
        at_ = i % NAT
        AT = ATs[at_]
        ob = (6, 7, 0, 1)[t['head_i'] % 4]
        nblk = w // 128
        for bi in range(nblk):
            vt = c0 // 128 + bi
            first = (kt == 0 and bi == 0)
            last = (kt == t['ntile'] - 1 and bi == nblk - 1)
            P.add('tensor', lambda e, bi=bi, vt=vt, first=first, last=last: e.matmul(
                c.ps[ob][:, 0:128], lhsT=Vb[s][:, vt, :], rhs=AT[:, bi * 128:(bi + 1) * 128], start=first, stop=last),
                reads=[('a_V', s), ('a_AT', at_)], writes=[c.psk(ob)])
        if kt == t['ntile'] - 1:
            prow = slice(half * 64, (half + 1) * 64)
            qcol = slice(t['qbl'] * 128, (t['qbl'] + 1) * 128)
            if half == 0:
                P.add('vector', lambda e: e.tensor_copy(out=oT[prow, hp, qcol], in_=c.ps[ob][prow, 0:128]),
                      reads=[c.psk(ob)], writes=[('a_oT', hp, t['qbl'], half)])
            else:
                P.add('scalar', lambda e: e.copy(out=oT[prow, hp, qcol], in_=c.ps[ob][prow, 0:128]),
                      reads=[c.psk(ob)], writes=[('a_oT', hp, t['qbl'], half)])

    n = len(tiles)
    DB, DC = 3, 4
    load_kv(0)
    for s_ in range(n + DC):
        if s_ < n:
            t = tiles[s_]
            if t['qbl'] == 0 and t['half'] == 0 and t['kt'] == 0 and t['hp'] + 1 < 8:
                load_kv(t['hp'] + 1)
            stage_A(t)
        if 0 <= s_ - DB < n:
            stage_B(tiles[s_ - DB])
        if 0 <= s_ - DC < n:
            stage_C(tiles[s_ - DC])
    c.release(m2)
    wo = c.sb("a_wo", [128, 8, Dm], BF16)
    lntmp = alloc_ln_tmp(c, "aln")
    for kc in range(8):
        P.add('gpsimd', lambda e, kc=kc: e.dma_start(out=wo[:, kc, :], in_=wo_d[kc * 128:(kc + 1) * 128, :]),
              writes=[('a_wo', kc)], dma='a_wo')
    P.seal([('a_wo', kc) for kc in range(8)], 'a_wo')
    for tt in range(NT // 128):
        par = tt % 2
        banks2 = [4 * par + hh for hh in range(nh)]
        for hh in range(nh):
            for kc in range(8):
                P.add('tensor', lambda e, kc=kc, hh=hh, tt=tt, banks2=banks2: e.matmul(
                    c.ps[banks2[hh]][:], lhsT=oT[:, kc, tt * 128:(tt + 1) * 128], rhs=wo[:, kc, hh * 512:(hh + 1) * 512],
                    start=(kc == 0), stop=(kc == 7)),
                    reads=[('a_wo', kc)], writes=[c.psk(banks2[hh])])
        ln_epilogue(c, banks2, x_res, xkey, tt, lnp, lntmp[par], ('alntmp', par))
    c.release(m)


def build_fused(dims=FULL, NT=NT_CORE):
    c = Ctx(dims)
    P, nc = c.P, c.nc
    Dm, Dff = dims['D'], dims['DFF']
    x_in = c.inp("x_in", [2 * NT, Dm])
    flag_d = c.inp("flag", [128, 1])
    ln_g = c.inp("ln_g", [DEPTH, 3, Dm])
    ln_b = c.inp("ln_b", [DEPTH, 3, Dm])
    wup = c.inp("ffn_w_up", [DEPTH, 2, Dm, 2 * Dff])
    wdn = c.inp("ffn_w_down", [DEPTH, 2, Dff, Dm])
    w = gla_weight_inputs(c, Dm)
    wkv_d = c.inp("sb_w_kv", [Dm, 2048])
    wq_d = c.inp("sb_w_q", [Dm, 1024])
    wo_d = c.inp("sb_w_out", [1024, Dm])
    negm_d = c.inp("c_negm", [128, 512])
    anti_d = c.inp("c_antiid", [128, 128])
    out = c.outp("out", [NT, Dm])
    KT_d = nc.dram_tensor("kt_scratch", [8, 128, SEQ], BF16).ap()
    V_d = nc.dram_tensor("v_scratch", [8, 128, SEQ // 128, 128], BF16).ap()
    load_consts(c)
    load_gla_consts(c)
    flag = c.sb("flag", [128, 1], F32)
    P.add('sync', lambda e: e.dma_start(out=flag[:], in_=flag_d), writes=['flag'], dma='c4')
    c.antiid = c.sb("antiid", [128, 128], F32)
    P.add('sync', lambda e: e.dma_start(out=c.antiid[:], in_=anti_d), writes=['antiid'], dma='c5')
    negm = c.sb("a_negm", [128, 512], BF16)
    P.add('gpsimd', lambda e: e.dma_start(out=negm[:], in_=negm_d), writes=['a_negm'], dma='c6')
    S = c.sb("g_S", [128, 4, 256], F32)
    P.add('vector', lambda e: e.memset(S[:], 0.0), writes=[('g_S', h) for h in range(GH)])
    x_res = c.sb("x_res", [128, NT // 128, Dm], F32)
    lnp = alloc_lnp(c)
    c.P.barrier()
    base = c.mark()

    def ffn(layer, idx):
        m = c.mark()
        bufs = alloc_ffn_bufs(c, NT)
        set_lnp(c, lnp, ln_g[layer, 2 * idx, :], ln_b[layer, 2 * idx, :])
        ffn_sublayer(c, x_res, 'x', NT, wup[layer, idx], wdn[layer, idx], lnp, bufs)
        c.release(m)

    for hf in range(2):
        load_x(c, x_in[hf * NT:(hf + 1) * NT, :], x_res, 'x', NT)
        ffn(0, 0)
        m = c.mark()
        gb = alloc_gla_bufs(c, True, S=S)
        set_lnp(c, lnp, ln_g[0, 1, :], ln_b[0, 1, :])
        if hf == 1:
            for h in range(GH):
                P.add('vector', lambda e, h=h: e.tensor_scalar(out=S[:, h, :], in0=S[:, h, :], scalar1=flag[:, 0:1], scalar2=None,
                                                              op0=ALU.mult), reads=[('g_S', h), 'flag'], writes=[('g_S', h)])
        ww = dict(w)
        ww['s_init'] = 'keep'
        ww['f_out'] = None
        gla_pass(c, x_res, 'x', NT, True, ww, gb, lnp)
        c.release(m)
        ffn(0, 1)
        kv_phase(c, x_res, 'x', hf, wkv_d, KT_d, V_d, flag, NT)
    ffn(1, 0)
    set_lnp(c, lnp, ln_g[1, 1, :], ln_b[1, 1, :])
    attn_phase(c, x_res, 'x', wq_d, wo_d, KT_d, V_d, negm, lnp, NT)
    ffn(1, 1)
    store_x(c, out, x_res, 'x', NT)
    return finish(c)


def kernel_fused(x, ln_g, ln_b, ffn_w_up, ffn_w_down, gla_w_in, gla_w_gk, gla_b_gk, gla_norm_g, gla_w_out,
                 sb_w_kv, sb_w_q, sb_w_out):
    f = lambda a: np.ascontiguousarray(np.asarray(a, dtype=np.float32))
    x = f(x)
    B = x.shape[0]
    t2s, ind = gla_consts_host()
    common = {"ln_g": f(ln_g), "ln_b": f(ln_b), "ffn_w_up": f(ffn_w_up), "ffn_w_down": f(ffn_w_down),
              "gla_w_in": f(gla_w_in[0]), "gla_w_gk": f(gla_w_gk[0]), "gla_b_gk": f(gla_b_gk[0]),
              "gla_norm_g": f(gla_norm_g[0]), "gla_w_out": f(gla_w_out[0]), "sb_w_kv": f(sb_w_kv),
              "sb_w_q": f(sb_w_q[0]), "sb_w_out": f(sb_w_out[0]), "c_ident": np.eye(128, dtype=np.float32),
              "c_t2s": t2s, "c_ind": ind, "c_negm": attn_consts_host(),
              "c_antiid": np.ascontiguousarray(np.eye(128, dtype=np.float32)[::-1])}
    cores = [(b, h) for b in range(B) for h in range(2)]
    ims = []
    for (b, h) in cores:
        d = dict(common)
        d["x_in"] = np.ascontiguousarray(np.concatenate([x[b, :NT_CORE], x[b, h * NT_CORE:(h + 1) * NT_CORE]], axis=0))
        d["flag"] = np.full((128, 1), float(h), np.float32)
        ims.append(d)
    r = _run(_prog("FUSED", build_fused), ims)
    out = np.empty((B, SEQ, D), np.float32)
    for i, (b, h) in enumerate(cores):
        out[b, h * NT_CORE:(h + 1) * NT_CORE] = r[i]["out"]
    return out


def kernel(x, ln_g, ln_b, ffn_w_up, ffn_w_down, gla_w_in, gla_w_gk, gla_b_gk, gla_norm_g, gla_w_out,
           sb_w_kv, sb_w_q, sb_w_out):
    return kernel_fused(x, ln_g, ln_b, ffn_w_up, ffn_w_down, gla_w_in, gla_w_gk, gla_b_gk, gla_norm_g, gla_w_out,
                        sb_w_kv, sb_w_q, sb_w_out)
```

```python
from contextlib import ExitStack
import numpy as np
import ml_dtypes
import concourse.bass as bass
import concourse.mybir as mybir
from concourse.bass_utils import run_bass_kernel_spmd

F32 = mybir.dt.float32
BF16 = mybir.dt.bfloat16
ALU = mybir.AluOpType
ACTF = mybir.ActivationFunctionType

ENGS = ['sync', 'tensor', 'vector', 'scalar', 'gpsimd']
SAME_ENGINE_SYNC = {'vector': True, 'scalar': True, 'gpsimd': True, 'tensor': False, 'sync': False}

D = 1024
DFF = 2816
DEPTH = 2
ALPHA = float((2 * DEPTH) ** 0.25)
LN_EPS = 1e-5
RMS_EPS = 1e-6
GH, GDK, GDV = 4, 128, 256
GATE_RANK = 16
GATE_TAU = 16.0
SBH, SBD = 16, 64
NEG_BIG = -240.0


class Op:
    __slots__ = ('eng', 'fn', 'deps', 'dma', 'tok', 'sig')

    def __init__(self, eng, fn, deps, dma):
        self.eng, self.fn, self.deps, self.dma = eng, fn, deps, dma
        self.tok = None
        self.sig = 0


class Prog:
    def __init__(self, nc):
        self.nc = nc
        self.ops = {e: [] for e in ENGS}
        self.bufs = {}
        self.dma_counts = {}
        self.keep = set()
        self._defer = None

    def add(self, eng, fn, reads=(), writes=(), dma=None, deps=()):
        if self._defer is not None:
            self._defer.append((eng, fn, tuple(reads), tuple(writes), dma, tuple(deps)))
            return None
        d = set(t for t in deps if t is not None)
        for k in reads:
            st = self.bufs.get(k)
            if st is not None and st[0] is not None:
                d.add(st[0])
        for k in writes:
            st = self.bufs.get(k)
            if st is not None:
                if st[0] is not None:
                    d.add(st[0])
                d.update(st[1].values())
        op = Op(eng, fn, d, dma)
        idx = len(self.ops[eng])
        self.ops[eng].append(op)
        if dma is not None:
            c = self.dma_counts.get(dma, 0) + 1
            self.dma_counts[dma] = c
            tok = ('d', dma, c)
        else:
            tok = ('e', eng, idx)
        op.tok = tok
        for k in reads:
            st = self.bufs.setdefault(k, [None, {}])
            rk = eng if dma is None else ('d', dma)
            st[1][rk] = tok
        for k in writes:
            self.bufs[k] = [tok, {}]
        return tok

    def seal(self, keys, dma_key):
        tok = ('d', dma_key, self.dma_counts[dma_key])
        for k in keys:
            st = self.bufs.get(k)
            if st is None:
                continue
            if st[0] is not None and st[0][0] == 'd' and st[0][1] == dma_key:
                st[0] = tok
            rk = ('d', dma_key)
            if rk in st[1]:
                st[1][rk] = tok

    def barrier(self):
        toks = [('d', k, cnt) for k, cnt in self.dma_counts.items()]
        for e in ENGS:
            for op in reversed(self.ops[e]):
                if op.dma is None and op.fn is not None:
                    toks.append(op.tok)
                    break
        for e in ENGS:
            self.add(e, None, deps=toks)
        self.bufs = {k: v for k, v in self.bufs.items() if k in self.keep}

    def all_dma_tokens(self, prefix=None):
        return [('d', k, c) for k, c in self.dma_counts.items()
                if prefix is None or str(k).startswith(prefix)]

    def emit(self):
        nc = self.nc
        needed = set()
        for e in ENGS:
            for op in self.ops[e]:
                for t in op.deps:
                    if t[0] == 'e':
                        if t[1] == e and not SAME_ENGINE_SYNC[e]:
                            continue
                        needed.add(t)
        for e in ENGS:
            n = 0
            for op in self.ops[e]:
                if op.dma is None and op.tok in needed:
                    n += 1
                    op.sig = n
        with ExitStack() as es:
            esem = {e: es.enter_context(nc.semaphore("s_" + e)) for e in ENGS}
            dsem = {}
            for i, k in enumerate(self.dma_counts):
                dsem[k] = es.enter_context(nc.semaphore("d%d" % i))
            block = es.enter_context(nc.Block())
            for e in ENGS:
                ops = self.ops[e]

                def body(eng, e=e, ops=ops):
                    waited = {}
                    for op in ops:
                        best = {}
                        for t in op.deps:
                            if t[0] == 'e':
                                if t[1] == e and not SAME_ENGINE_SYNC[e]:
                                    continue
                                sem = esem[t[1]]
                                val = self.ops[t[1]][t[2]].sig
                                key = ('e', t[1])
                            else:
                                sem = dsem[t[1]]
                                val = 16 * t[2]
                                key = ('d', t[1])
                            if key not in best or best[key][1] < val:
                                best[key] = (sem, val)
                        for key, (sem, val) in best.items():
                            if waited.get(key, 0) >= val:
                                continue
                            eng.wait_ge(sem, val)
                            waited[key] = val
                        if op.fn is None:
                            continue
                        ins = op.fn(eng)
                        if op.dma is not None:
                            ins.then_inc(dsem[op.dma], 16)
                        elif op.sig:
                            ins.then_inc(esem[e], 1)
                getattr(block, e)(body)


class Ctx:
    ARENA_BYTES = 207 * 1024

    def __init__(self, dims):
        self.dm = dims
        self.nc = bass.Bass("TRN2", target_bir_lowering=False)
        self.P = Prog(self.nc)
        self.ps = [self.nc.alloc_psum_tensor("psb%d" % i, [128, 512], F32) for i in range(8)]
        self.uid = 0
        self.dram = {}
        self.arena = self.nc.alloc_sbuf_tensor("arena", [128, self.ARENA_BYTES // 4], F32)
        self.off = 0

    def inp(self, name, shape, dt=F32):
        t = self.nc.dram_tensor(name, list(shape), dt, kind="ExternalInput").ap()
        self.dram[name] = t
        return t

    def outp(self, name, shape, dt=F32):
        t = self.nc.dram_tensor(name, list(shape), dt, kind="ExternalOutput").ap()
        self.dram[name] = t
        return t

    def sb(self, name, shape, dt=F32):
        esz = 2 if dt == BF16 else 4
        n = 1
        for d in shape[1:]:
            n *= d
        nbytes = (n * esz + 63) // 64 * 64
        if self.off + nbytes > self.ARENA_BYTES:
            raise RuntimeError("SBUF arena overflow allocating %s (%d + %d)" % (name, self.off, nbytes))
        a = self.arena[0:shape[0], self.off // 4:(self.off + nbytes) // 4]
        self.off += nbytes
        if dt != F32:
            a = a.bitcast(dt)
        a = a[:, 0:n]
        if len(shape) == 3:
            a = a.rearrange("p (a b) -> p a b", a=shape[1])
        elif len(shape) != 2:
            raise RuntimeError("bad shape")
        return a

    def mark(self):
        return self.off

    def release(self, mark):
        self.off = mark
        self.P.barrier()

    def psk(self, i):
        return ('ps', i)


def load_consts(c):
    P = c.P
    ident_d = c.inp("c_ident", [128, 128])
    c.ident = c.sb("ident", [128, 128], F32)
    c.identb = c.sb("identb", [128, 128], BF16)
    P.add('sync', lambda e: e.dma_start(out=c.ident[:], in_=ident_d), writes=['ident'], dma='c0')
    P.add('gpsimd', lambda e: e.dma_start(out=c.identb[:], in_=ident_d), writes=['identb'], dma='c1')
    c.epsb = c.sb("epsb", [128, 2], F32)
    c.oneb = c.sb("oneb", [128, 1], F32)
    P.add('vector', lambda e: e.memset(c.oneb[:], 1.0), writes=['oneb'])
    P.add('vector', lambda e: e.memset(c.epsb[:, 0:1], LN_EPS), writes=['epsb'])
    P.add('vector', lambda e: e.memset(c.epsb[:, 1:2], RMS_EPS), writes=['epsb'])


def load_ln_params(c, name, g_d, b_d):
    Dm = c.dm['D']
    g = c.sb(name + "_g", [128, Dm], F32)
    b = c.sb(name + "_b", [128, Dm], F32)
    c.P.add('sync', lambda e: e.dma_start(out=g[:], in_=g_d.partition_broadcast(128)), writes=[name + '_g'], dma='lnp')
    c.P.add('sync', lambda e: e.dma_start(out=b[:], in_=b_d.partition_broadcast(128)), writes=[name + '_b'], dma='lnp')
    c.P.seal([name + '_g', name + '_b'], 'lnp')
    return (g, b, name + '_g', name + '_b')


def transposes_to_xT(c, x_res, xkey, tts, xT, xTkey, banks):
    P = c.P
    KC = c.dm['D'] // 128
    bi = 0
    for i, tt in enumerate(tts):
        for k0 in range(0, KC, 4):
            bank = banks[bi % len(banks)]
            bi += 1
            pt = c.ps[bank]
            for kk in range(4):
                kc = k0 + kk
                P.add('tensor', lambda e, pt=pt, kk=kk, kc=kc, tt=tt: e.transpose(
                    out=pt[:, kk * 128:(kk + 1) * 128], in_=x_res[:, tt, kc * 128:(kc + 1) * 128], identity=c.ident[:]),
                    reads=[(xkey, tt), 'ident'], writes=[c.psk(bank)])
            eng = 'scalar' if (bi % 2 == 0) else 'vector'
            src = pt[:].rearrange("p (a b) -> p a b", a=4)
            dst = xT[:, k0:k0 + 4, i * 128:(i + 1) * 128]
            if eng == 'scalar':
                P.add('scalar', lambda e, src=src, dst=dst: e.copy(out=dst, in_=src),
                      reads=[c.psk(bank)], writes=[(xTkey, i)])
            else:
                P.add('vector', lambda e, src=src, dst=dst: e.tensor_copy(out=dst, in_=src),
                      reads=[c.psk(bank)], writes=[(xTkey, i)])


def ln_epilogue(c, banks2, x_res, xkey, tt, lnp, tmp, tmpkey, extra_reads=()):
    P = c.P
    Dm = c.dm['D']
    g, b, gk, bk = lnp
    t2, st = tmp
    nh = Dm // 512
    xk = (xkey, tt)
    for h in range(nh):
        P.add('vector', lambda e, h=h: e.scalar_tensor_tensor(
            out=x_res[:, tt, h * 512:(h + 1) * 512], in0=x_res[:, tt, h * 512:(h + 1) * 512], scalar=ALPHA,
            in1=c.ps[banks2[h]][:], op0=ALU.mult, op1=ALU.add),
            reads=[xk, c.psk(banks2[h])] + list(extra_reads), writes=[xk])
    for h in range(nh):
        P.add('vector', lambda e, h=h: e.bn_stats(out=st[:, 6 * h:6 * h + 6], in_=x_res[:, tt, h * 512:(h + 1) * 512]),
              reads=[xk], writes=[(tmpkey, 'st', h)])
    P.add('vector', lambda e: e.bn_aggr(out=st[:, 12:14], in_=st[:, 0:6 * nh]),
          reads=[(tmpkey, 'st', h) for h in range(nh)], writes=[(tmpkey, 'mv')])
    P.add('scalar', lambda e: e.activation(out=st[:, 16:17], in_=st[:, 13:14], func=ACTF.Ln, bias=c.epsb[:, 0:1]),
          reads=[(tmpkey, 'mv'), 'epsb'], writes=[(tmpkey, 'lnv')])
    P.add('scalar', lambda e: e.activation(out=st[:, 14:15], in_=st[:, 16:17], func=ACTF.Exp, scale=-0.5),
          reads=[(tmpkey, 'lnv')], writes=[(tmpkey, 'rstd')])
    P.add('vector', lambda e: e.scalar_tensor_tensor(out=st[:, 15:16], in0=st[:, 12:13], scalar=-1.0, in1=st[:, 14:15],
                                                     op0=ALU.mult, op1=ALU.mult),
          reads=[(tmpkey, 'mv'), (tmpkey, 'rstd')], writes=[(tmpkey, 'nb')])
    P.add('scalar', lambda e: e.activation(out=t2[:], in_=x_res[:, tt, :], func=ACTF.Identity, bias=st[:, 15:16], scale=st[:, 14:15]),
          reads=[xk, (tmpkey, 'nb'), (tmpkey, 'rstd')], writes=[(tmpkey, 't2')])
    P.add('vector', lambda e: e.tensor_tensor(out=t2[:], in0=t2[:], in1=g[:], op=ALU.mult),
          reads=[(tmpkey, 't2'), gk], writes=[(tmpkey, 't2')])
    P.add('gpsimd', lambda e: e.tensor_tensor(out=x_res[:, tt, :], in0=t2[:], in1=b[:], op=ALU.add),
          reads=[(tmpkey, 't2'), bk], writes=[xk])


def alloc_ln_tmp(c, name):
    Dm = c.dm['D']
    return [(c.sb(name + "_t2_%d" % i, [128, Dm], F32), c.sb(name + "_st_%d" % i, [128, 32], F32)) for i in range(2)]


def ffn_sublayer(c, x_res, xkey, NT, w_up_d, w_dn_d, lnp, bufs):
    P = c.P
    Dm, Dff = c.dm['D'], c.dm['DFF']
    KC, JC = Dm // 128, Dff // 128
    ST = min(1024, NT)
    nst = NT // ST
    xT, hT, wd, wslots, sgs, lntmp = bufs['xT'], bufs['hT'], bufs['wd'], bufs['wslots'], bufs['sg'], bufs['lntmp']
    uid = c.uid
    c.uid += 1
    nsl = len(wslots)
    nh = Dm // 512
    for st in range(nst):
        tts = list(range(st * (ST // 128), (st + 1) * (ST // 128)))
        transposes_to_xT(c, x_res, xkey, tts, xT, 'xT', banks=[0, 2])
        for j in range(JC):
            P.add('gpsimd', lambda e, j=j: e.dma_start(out=wd[:, j, :], in_=w_dn_d[j * 128:(j + 1) * 128, :]),
                  writes=[('wd', j)], dma='wd')
        P.seal([('wd', j) for j in range(JC)], 'wd')
        ntk = ST // 512 if ST >= 512 else 1
        tw = min(512, ST)
        for j in range(JC):
            s = (uid * 1000 + st * JC + j) % nsl
            wg, wu = wslots[s]
            P.add('gpsimd', lambda e, j=j, wg=wg: e.dma_start(
                out=wg[:], in_=w_up_d[:, j * 128:(j + 1) * 128].rearrange("(kc p) n -> p kc n", p=128)),
                writes=[('wg', s)], dma='wg%d' % s)
            P.add('gpsimd', lambda e, j=j, wu=wu: e.dma_start(
                out=wu[:], in_=w_up_d[:, Dff + j * 128:Dff + (j + 1) * 128].rearrange("(kc p) n -> p kc n", p=128)),
                writes=[('wu', s)], dma='wu%d' % s)
            for t5 in range(ntk):
                par = (j * ntk + t5) % 2
                bg, bu = (0, 1) if par == 0 else (2, 3)
                cols = slice(t5 * tw, (t5 + 1) * tw)
                xkeys = [('xT', i) for i in range(t5 * (tw // 128), (t5 + 1) * (tw // 128))]
                for kc in range(KC):
                    P.add('tensor', lambda e, kc=kc, wg=wg, bg=bg, cols=cols: e.matmul(
                        c.ps[bg][:, 0:tw], lhsT=wg[:, kc, :], rhs=xT[:, kc, cols], start=(kc == 0), stop=(kc == KC - 1)),
                        reads=[('wg', s)] + xkeys, writes=[c.psk(bg)])
                for kc in range(KC):
                    P.add('tensor', lambda e, kc=kc, wu=wu, bu=bu, cols=cols: e.matmul(
                        c.ps[bu][:, 0:tw], lhsT=wu[:, kc, :], rhs=xT[:, kc, cols], start=(kc == 0), stop=(kc == KC - 1)),
                        reads=[('wu', s)] + xkeys, writes=[c.psk(bu)])
                sg = sgs[par]
                P.add('scalar', lambda e, sg=sg, bg=bg: e.activation(out=sg[:, 0:tw], in_=c.ps[bg][:, 0:tw], func=ACTF.Silu),
                      reads=[c.psk(bg)], writes=[('sg', par)])
                P.add('vector', lambda e, sg=sg, bu=bu, j=j, cols=cols: e.scalar_tensor_tensor(
                    out=hT[:, j, cols], in0=sg[:, 0:tw], scalar=0.5, in1=c.ps[bu][:, 0:tw], op0=ALU.mult, op1=ALU.mult),
                    reads=[('sg', par), c.psk(bu)], writes=[('hT', j, t5)])
        for i, tt in enumerate(tts):
            par = i % 2
            banks2 = [4 + 2 * par + h for h in range(nh)]
            t5 = (i * 128) // tw
            for h in range(nh):
                for j in range(JC):
                    P.add('tensor', lambda e, j=j, h=h, i=i, banks2=banks2: e.matmul(
                        c.ps[banks2[h]][:], lhsT=hT[:, j, i * 128:(i + 1) * 128], rhs=wd[:, j, h * 512:(h + 1) * 512],
                        start=(j == 0), stop=(j == JC - 1)),
                        reads=[('hT', j, t5), ('wd', j)], writes=[c.psk(banks2[h])])
            ln_epilogue(c, banks2, x_res, xkey, tt, lnp, lntmp[par], ('lntmp', par))


def alloc_ffn_bufs(c, NT):
    Dm, Dff = c.dm['D'], c.dm['DFF']
    KC, JC = Dm // 128, Dff // 128
    ST = min(1024, NT)
    b = {}
    b['xT'] = c.sb("xT", [128, KC, ST], BF16)
    b['hT'] = c.sb("hT", [128, JC, ST], BF16)
    b['wd'] = c.sb("wd", [128, JC, Dm], BF16)
    b['wslots'] = [(c.sb("wg%d" % i, [128, KC, 128], BF16), c.sb("wu%d" % i, [128, KC, 128], BF16)) for i in range(2)]
    b['sg'] = [c.sb("sg%d" % i, [128, 512], F32) for i in range(2)]
    b['lntmp'] = alloc_ln_tmp(c, "ln")
    return b


def load_x(c, x_d, x_res, xkey, NT):
    for tt in range(NT // 128):
        c.P.add('sync', lambda e, tt=tt: e.dma_start(out=x_res[:, tt, :], in_=x_d[tt * 128:(tt + 1) * 128, :]),
                writes=[(xkey, tt)], dma='ldx')
    c.P.seal([(xkey, tt) for tt in range(NT // 128)], 'ldx')


def store_x(c, x_d, x_res, xkey, NT, tag='stx'):
    for tt in range(NT // 128):
        c.P.add('sync', lambda e, tt=tt: e.dma_start(out=x_d[tt * 128:(tt + 1) * 128, :], in_=x_res[:, tt, :]),
                reads=[(xkey, tt)], dma=tag)
    c.P.seal([(xkey, tt) for tt in range(NT // 128)], tag)


def finish(c):
    toks = c.P.all_dma_tokens()
    c.P.add('sync', None, deps=toks)
    c.P.emit()
    return c.nc


def build_ffn_test(dims, NT):
    c = Ctx(dims)
    Dm, Dff = dims['D'], dims['DFF']
    x_d = c.inp("x", [NT, Dm])
    wup = c.inp("w_up", [Dm, 2 * Dff])
    wdn = c.inp("w_dn", [Dff, Dm])
    lng = c.inp("ln_g", [Dm])
    lnb = c.inp("ln_b", [Dm])
    out = c.outp("out", [NT, Dm])
    load_consts(c)
    x_res = c.sb("x_res", [128, NT // 128, Dm], F32)
    load_x(c, x_d, x_res, 'x', NT)
    lnp = load_ln_params(c, "ln0", lng, lnb)
    bufs = alloc_ffn_bufs(c, NT)
    ffn_sublayer(c, x_res, 'x', NT, wup, wdn, lnp, bufs)
    store_x(c, out, x_res, 'x', NT)
    return finish(c)


def load_gla_consts(c):
    t2_d = c.inp("c_t2s", [128, 128])
    ind_d = c.inp("c_ind", [128, 2])
    c.t2s = c.sb("t2s", [128, 128], F32)
    c.ind = c.sb("ind", [128, 2], F32)
    c.P.add('sync', lambda e: e.dma_start(out=c.t2s[:], in_=t2_d), writes=['t2s'], dma='c2')
    c.P.add('sync', lambda e: e.dma_start(out=c.ind[:], in_=ind_d), writes=['ind'], dma='c3')


def gla_consts_host():
    t = np.arange(128)
    same = (t[:, None] // 64) == (t[None, :] // 64)
    t2s = np.where(same & (t[:, None] > t[None, :]), -1.0 / GATE_TAU, 0.0).astype(np.float32)
    ind = np.zeros((128, 2), np.float32)
    ind[:64, 0] = -1.0 / GATE_TAU
    ind[64:, 1] = -1.0 / GATE_TAU
    return t2s, ind


def alloc_gla_bufs(c, full, S=None):
    Dm = c.dm['D']
    KC = Dm // 128
    b = {}
    b['win'] = c.sb("g_win", [128, KC, 3088], BF16)
    b['wgk'] = c.sb("g_wgk", [17, 512], BF16)
    b['xT'] = c.sb("g_xT", [128, KC, 512], BF16)
    b['lowT'] = c.sb("g_lowT", [17, 512], BF16)
    b['e'] = c.sb("g_e", [128, 512], F32)
    b['l'] = c.sb("g_l", [128, 512], F32)
    b['ed'] = c.sb("g_ed", [128, 512], F32)
    b['kdec'] = c.sb("g_kdec", [128, 512], BF16)
    b['v'] = c.sb("g_v", [128, 1024], BF16)
    b['decT'] = c.sb("g_decT", [128, 8], F32)
    b['S'] = S if S is not None else c.sb("g_S", [128, 4, 256], F32)
    if full:
        b['wout'] = c.sb("g_wout", [128, 8, Dm], BF16)
        b['qT'] = c.sb("g_qT", [128, 4, 512], BF16)
        b['sr'] = c.sb("g_sr", [128, 1024], F32)
        b['sr2'] = c.sb("g_sr2", [128, 1024], F32)
        b['o2'] = c.sb("g_o2", [128, 4, 256], F32)
        b['Sb'] = c.sb("g_Sb", [128, 4, 256], BF16)
        b['o'] = c.sb("g_o", [128, 4, 256], F32)
        b['junk'] = c.sb("g_junk", [128, 256], F32)
        b['ss'] = c.sb("g_ss", [128, 8], F32)
        b['gated'] = c.sb("g_gated", [128, 1024], BF16)
        b['gT'] = c.sb("g_gT", [128, 8, 128], BF16)
        b['ng'] = c.sb("g_ng", [128, 256], F32)
        b['lntmp'] = alloc_ln_tmp(c, "gln")
    return b


def gla_pass(c, x_res, xkey, NT, full, w, b, lnp=None):
    P = c.P
    Dm = c.dm['D']
    KC = Dm // 128
    nh = Dm // 512
    win, wgk, xT, lowT = b['win'], b['wgk'], b['xT'], b['lowT']
    S = b['S']
    for kc in range(KC):
        P.add('gpsimd', lambda e, kc=kc: e.dma_start(out=win[:, kc, :], in_=w['w_in'][kc * 128:(kc + 1) * 128, :]),
              writes=[('g_win', kc)], dma='g_win')
    P.seal([('g_win', kc) for kc in range(KC)], 'g_win')
    winkeys = [('g_win', kc) for kc in range(KC)]
    P.add('gpsimd', lambda e: e.dma_start(out=wgk[0:16, :], in_=w['w_gk']), writes=['g_wgk0'], dma='g_wgk')
    P.add('gpsimd', lambda e: e.dma_start(out=wgk[16:17, :], in_=w['b_gk'].rearrange("(o n) -> o n", o=1)),
          writes=['g_wgk1'], dma='g_wgk')
    P.seal(['g_wgk0', 'g_wgk1'], 'g_wgk')
    if isinstance(w.get('s_init'), str):
        pass
    elif w.get('s_init') is not None:
        P.add('sync', lambda e: e.dma_start(out=S[:], in_=w['s_init'].rearrange("h p d -> p h d")), writes=[('g_S', h) for h in range(GH)], dma='g_sinit')
    else:
        P.add('vector', lambda e: e.memset(S[:], 0.0), writes=[('g_S', h) for h in range(GH)])
    P.add('vector', lambda e: e.memset(lowT[:], 1.0), writes=['g_lowT'])
    if full:
        wout, qT, sr, Sb, o_sb, junk, ss, gated, gT, ng = (b[k] for k in ('wout', 'qT', 'sr', 'Sb', 'o', 'junk', 'ss', 'gated', 'gT', 'ng'))
        for kc in range(8):
            P.add('gpsimd', lambda e, kc=kc: e.dma_start(out=wout[:, kc, :], in_=w['w_out'][kc * 128:(kc + 1) * 128, :]),
                  writes=[('g_wout', kc)], dma='g_wout')
        P.seal([('g_wout', kc) for kc in range(8)], 'g_wout')
        P.add('sync', lambda e: e.dma_start(out=ng[:], in_=w['norm_g'].partition_broadcast(128)), writes=['g_ng'], dma='g_ng')
    e_sb, l_sb, ed, kdec, v_sb, decT = b['e'], b['l'], b['ed'], b['kdec'], b['v'], b['decT']
    prev_tail = []
    GW = min(512, NT)
    ngrp = NT // GW
    tpg = GW // 128
    for gi in range(ngrp):
        tts = list(range(gi * tpg, (gi + 1) * tpg))
        transposes_to_xT(c, x_res, xkey, tts, xT, 'g_xT', banks=[0])
        xkeys = [('g_xT', i) for i in range(tpg)]
        if full:
            for h in range(GH):
                for kc in range(KC):
                    P.add('tensor', lambda e, h=h, kc=kc: e.matmul(
                        c.ps[0][:, 0:GW], lhsT=win[:, kc, h * 128:(h + 1) * 128], rhs=xT[:, kc, 0:GW],
                        start=(kc == 0), stop=(kc == KC - 1)),
                        reads=winkeys + xkeys, writes=[c.psk(0)])
                P.add('scalar', lambda e, h=h: e.activation(out=qT[:, h, 0:GW], in_=c.ps[0][:, 0:GW], func=ACTF.Copy,
                                                            scale=float(GDK ** -0.5)),
                      reads=[c.psk(0)], writes=[('g_qT', h)])
        for kc in range(KC):
            P.add('tensor', lambda e, kc=kc: e.matmul(
                c.ps[0][0:16, 0:GW], lhsT=win[:, kc, 3072:3088], rhs=xT[:, kc, 0:GW], start=(kc == 0), stop=(kc == KC - 1)),
                reads=winkeys + xkeys, writes=[c.psk(0)])
        P.add('vector', lambda e: e.tensor_copy(out=lowT[0:16, 0:GW], in_=c.ps[0][0:16, 0:GW]),
              reads=[c.psk(0)], writes=['g_lowT'])
        for ti, tt in enumerate(tts):
            tcol = slice(ti * 128, (ti + 1) * 128)
            par = tt % 2
            if full:
                sr = (b['sr'], b['sr2'])[par]
                o_sb = (b['o'], b['o2'])[par]
                P._defer = []
            for kc in range(KC):
                P.add('tensor', lambda e, kc=kc, tcol=tcol: e.matmul(
                    c.ps[1][:], lhsT=xT[:, kc, tcol], rhs=win[:, kc, 512:1024], start=(kc == 0), stop=(kc == KC - 1)),
                    reads=winkeys + [('g_xT', ti)], writes=[c.psk(1)])
            for hv in range(2):
                for kc in range(KC):
                    P.add('tensor', lambda e, kc=kc, hv=hv, tcol=tcol: e.matmul(
                        c.ps[2 + hv][:], lhsT=xT[:, kc, tcol], rhs=win[:, kc, 1024 + hv * 512:1024 + (hv + 1) * 512],
                        start=(kc == 0), stop=(kc == KC - 1)),
                        reads=winkeys + [('g_xT', ti)], writes=[c.psk(2 + hv)])
            if full:
                for hv in range(2):
                    for kc in range(KC):
                        P.add('tensor', lambda e, kc=kc, hv=hv, tcol=tcol: e.matmul(
                            c.ps[4 + hv][:], lhsT=xT[:, kc, tcol], rhs=win[:, kc, 2048 + hv * 512:2048 + (hv + 1) * 512],
                            start=(kc == 0), stop=(kc == KC - 1)),
                            reads=winkeys + [('g_xT', ti)], writes=[c.psk(4 + hv)])
            P.add('tensor', lambda e, tcol=tcol: e.matmul(c.ps[0][:], lhsT=lowT[:, tcol], rhs=wgk[:], start=True, stop=True),
                  reads=['g_lowT', 'g_wgk0', 'g_wgk1'], writes=[c.psk(0)])
            P.add('scalar', lambda e: e.activation(out=e_sb[:], in_=c.ps[0][:], func=ACTF.Exp, scale=-1.0),
                  reads=[c.psk(0)], writes=['g_e'])
            P.add('scalar', lambda e: e.activation(out=l_sb[:], in_=e_sb[:], func=ACTF.Ln, bias=c.oneb[:, 0:1]),
                  reads=['g_e', 'oneb'], writes=['g_l'])
            P.add('tensor', lambda e: e.matmul(c.ps[0][:], lhsT=c.t2s[:], rhs=l_sb[:], start=True, stop=True),
                  reads=['t2s', 'g_l'], writes=[c.psk(0)])
            P.add('scalar', lambda e: e.activation(out=ed[:], in_=c.ps[0][:], func=ACTF.Exp),
                  reads=[c.psk(0)], writes=['g_ed'])
            for h in range(GH):
                P.add('tensor', lambda e, h=h: e.matmul(c.ps[0][:, 2 * h:2 * h + 2], lhsT=l_sb[:, h * 128:(h + 1) * 128],
                                                        rhs=c.ind[:], start=True, stop=True),
                      reads=['g_l', 'ind'], writes=[c.psk(0)])
            P.add('scalar', lambda e: e.activation(out=decT[:], in_=c.ps[0][:, 0:8], func=ACTF.Exp),
                  reads=[c.psk(0)], writes=['g_decT'])
            P.add('vector', lambda e: e.tensor_tensor(out=kdec[:], in0=c.ps[1][:], in1=ed[:], op=ALU.mult),
                  reads=[c.psk(1), 'g_ed'], writes=['g_kdec'])
            for hv in range(2):
                P.add('scalar' if hv == 0 else 'vector',
                      (lambda e, hv=hv: e.copy(out=v_sb[:, hv * 512:(hv + 1) * 512], in_=c.ps[2 + hv][:])) if hv == 0 else
                      (lambda e, hv=hv: e.tensor_copy(out=v_sb[:, hv * 512:(hv + 1) * 512], in_=c.ps[2 + hv][:])),
                      reads=[c.psk(2 + hv)], writes=[('g_v', hv)])
            if full:
                for hv in range(2):
                    P.add('scalar', lambda e, hv=hv, sr=sr: e.activation(out=sr[:, hv * 512:(hv + 1) * 512], in_=c.ps[4 + hv][:], func=ACTF.Silu),
                          reads=[c.psk(4 + hv)], writes=[('g_sr', par, hv)])
            for cc in range(2):
                rows = slice(cc * 64, (cc + 1) * 64)
                for h in range(GH):
                    P.add('tensor', lambda e, h=h, rows=rows: e.matmul(
                        c.ps[2 + h // 2][:, (h % 2) * 256:(h % 2 + 1) * 256], lhsT=kdec[rows, h * 128:(h + 1) * 128],
                        rhs=v_sb[rows, h * 256:(h + 1) * 256], start=True, stop=True),
                        reads=['g_kdec', ('g_v', h // 2)], writes=[c.psk(2 + h // 2)])
                for h in range(GH):
                    P.add('vector', lambda e, h=h, cc=cc: e.scalar_tensor_tensor(
                        out=S[:, h, :], in0=S[:, h, :], scalar=decT[:, 2 * h + cc:2 * h + cc + 1],
                        in1=c.ps[2 + h // 2][:, (h % 2) * 256:(h % 2 + 1) * 256], op0=ALU.mult, op1=ALU.add),
                        reads=[('g_S', h), 'g_decT', c.psk(2 + h // 2)], writes=[('g_S', h)])
                if not full:
                    continue
                P.add('scalar', lambda e: e.copy(out=Sb[:], in_=S[:]), reads=[('g_S', h) for h in range(GH)], writes=['g_Sb'])
                for h in range(GH):
                    P.add('tensor', lambda e, h=h, tcol=tcol: e.matmul(
                        c.ps[4 + h // 2][:, (h % 2) * 256:(h % 2 + 1) * 256], lhsT=qT[:, h, tcol], rhs=Sb[:, h, :],
                        start=True, stop=True),
                        reads=[('g_qT', h), 'g_Sb'], writes=[c.psk(4 + h // 2)])
                for hv in range(2):
                    src = c.ps[4 + hv][rows, :].rearrange("p (a b) -> p a b", a=2)
                    dst = o_sb[rows, 2 * hv:2 * hv + 2, :]
                    if hv == 0:
                        P.add('vector', lambda e, src=src, dst=dst: e.tensor_copy(out=dst, in_=src),
                              reads=[c.psk(4 + hv)], writes=[('g_o', par, cc, hv)])
                    else:
                        P.add('scalar', lambda e, src=src, dst=dst: e.copy(out=dst, in_=src),
                              reads=[c.psk(4 + hv)], writes=[('g_o', par, cc, hv)])
            if not full:
                continue
            head_ops = P._defer
            P._defer = []
            okeys = [('g_o', par, cc, hv) for cc in range(2) for hv in range(2)]
            for h in range(GH):
                P.add('scalar', lambda e, h=h, o_sb=o_sb: e.activation(out=junk[:], in_=o_sb[:, h, :], func=ACTF.Square,
                                                            accum_out=ss[:, h:h + 1]),
                      reads=okeys, writes=[('g_ss', h), 'g_junk'])
            P.add('scalar', lambda e: e.activation(out=ss[:, 4:8], in_=ss[:, 0:4], func=ACTF.Ln, bias=c.epsb[:, 1:2],
                                                   scale=1.0 / GDV),
                  reads=[('g_ss', h) for h in range(GH)] + ['epsb'], writes=['g_ss2'])
            P.add('scalar', lambda e: e.activation(out=ss[:, 4:8], in_=ss[:, 4:8], func=ACTF.Exp, scale=-0.5),
                  reads=['g_ss2'], writes=['g_ss2'])
            for h in range(GH):
                P.add('vector', lambda e, h=h, o_sb=o_sb: e.scalar_tensor_tensor(
                    out=o_sb[:, h, :], in0=o_sb[:, h, :], scalar=ss[:, 4 + h:5 + h], in1=ng[:], op0=ALU.mult, op1=ALU.mult),
                    reads=okeys + ['g_ss2', 'g_ng'], writes=[('g_on', par, h)])
            for hv in range(2):
                P.add('gpsimd' if hv == 0 else 'vector', lambda e, hv=hv, o_sb=o_sb, sr=sr: e.tensor_tensor(
                    out=gated[:, hv * 512:(hv + 1) * 512],
                    in0=o_sb[:, 2 * hv:2 * hv + 2, :].rearrange("p a b -> p (a b)"),
                    in1=sr[:, hv * 512:(hv + 1) * 512], op=ALU.mult),
                    reads=[('g_on', par, 2 * hv), ('g_on', par, 2 * hv + 1), ('g_sr', par, hv)], writes=[('g_gated', hv)])
            psb = c.ps[6][:].bitcast(BF16)
            for kc in range(8):
                P.add('tensor', lambda e, kc=kc: e.transpose(out=psb[:, kc * 128:(kc + 1) * 128],
                                                             in_=gated[:, kc * 128:(kc + 1) * 128], identity=c.identb[:]),
                      reads=[('g_gated', kc // 4), 'identb'], writes=[c.psk(6)])
            P.add('vector', lambda e: e.tensor_copy(out=gT[:].rearrange("p a b -> p (a b)"), in_=psb),
                  reads=[c.psk(6)], writes=['g_gT'])
            mb = [7, 6][:nh]
            for hh in range(nh):
                for kc in range(8):
                    P.add('tensor', lambda e, kc=kc, hh=hh: e.matmul(
                        c.ps[mb[hh]][:], lhsT=gT[:, kc, :], rhs=wout[:, kc, hh * 512:(hh + 1) * 512],
                        start=(kc == 0), stop=(kc == 7)),
                        reads=['g_gT', ('g_wout', kc)], writes=[c.psk(mb[hh])])
            ln_epilogue(c, mb, x_res, xkey, tt, lnp, b['lntmp'][tt % 2], ('glntmp', tt % 2))
            tail_ops = P._defer
            P._defer = None
            nhd, ntl = len(head_ops), len(prev_tail)
            ih = it = 0
            while ih < nhd or it < ntl:
                if it >= ntl or (ih < nhd and ih * max(ntl, 1) <= it * nhd):
                    P.add(*head_ops[ih])
                    ih += 1
                else:
                    P.add(*prev_tail[it])
                    it += 1
            prev_tail = tail_ops
    for op_ in prev_tail:
        P.add(*op_)
    if w.get('f_out') is not None:
        P.add('sync', lambda e: e.dma_start(out=w['f_out'].rearrange("h p d -> p h d"), in_=S[:]), reads=[('g_S', h) for h in range(GH)], dma='g_fout')


def build_gla_test(dims, NT, full):
    c = Ctx(dims)
    Dm = dims['D']
    x_d = c.inp("x", [NT, Dm])
    w = dict(w_in=c.inp("w_in", [Dm, 3088]), w_gk=c.inp("w_gk", [16, 512]), b_gk=c.inp("b_gk", [512]),
             norm_g=c.inp("norm_g", [256]), w_out=c.inp("w_out", [1024, Dm]), s_init=c.inp("s_init", [4, 128, 256]),
             f_out=c.outp("f_out", [4, 128, 256]))
    lng = c.inp("ln_g", [Dm])
    lnb = c.inp("ln_b", [Dm])
    out = c.outp("out", [NT, Dm])
    load_consts(c)
    load_gla_consts(c)
    x_res = c.sb("x_res", [128, NT // 128, Dm], F32)
    load_x(c, x_d, x_res, 'x', NT)
    lnp = load_ln_params(c, "ln0", lng, lnb)
    bufs = alloc_gla_bufs(c, full)
    gla_pass(c, x_res, 'x', NT, full, w, bufs, lnp)
    store_x(c, out, x_res, 'x', NT)
    return finish(c)


def attn_consts_host():
    i = np.arange(128)[:, None]
    j = np.arange(512)[None, :]
    return np.where((j <= 127 - i) & (j < 128), NEG_BIG, 0.0).astype(np.float32)


def attn_kernel(c, SEQ, x3r_d, x4_d, wk_d, wv_d, wq_d, oT_d, negmask_d):
    P = c.P
    Dm = c.dm['D']
    KC = Dm // 128
    NB = SEQ // 128
    NG = SEQ // 512
    wk = c.sb("a_wk", [128, KC, 512], BF16)
    wv = c.sb("a_wv", [128, KC, 512], BF16)
    wq = c.sb("a_wq", [128, KC, 512], BF16)
    for nm, t, d in (('a_wk', wk, wk_d), ('a_wv', wv, wv_d), ('a_wq', wq, wq_d)):
        P.add('gpsimd', lambda e, t=t, d=d: e.dma_start(out=t[:], in_=d.rearrange("(kc p) n -> p kc n", p=128)),
              writes=[nm], dma=nm)
    negm = c.sb("a_negm", [128, 512], BF16)
    P.add('gpsimd', lambda e: e.dma_start(out=negm[:], in_=negmask_d), writes=['a_negm'], dma='a_negm')
    zeros = c.sb("a_zeros", [128, 512], F32)
    P.add('gpsimd', lambda e: e.memset(zeros[:], 0.0), writes=['a_zeros'])
    KT = c.sb("a_KT", [128, 4, SEQ], BF16)
    V = c.sb("a_V", [128, NB, 512], BF16)
    qT = c.sb("a_qT", [128, 4, SEQ], BF16)
    oT = c.sb("a_oT", [128, 4, SEQ], BF16)
    xs = c.sb("a_xs", [128, 4, Dm], F32)
    xTa = c.sb("a_xT", [128, KC, 512], BF16)
    for which in range(2):
        src_d = x3r_d if which == 0 else x4_d
        for g in range(NG):
            for ti in range(4):
                P.add('sync', lambda e, g=g, ti=ti, src_d=src_d: e.dma_start(
                    out=xs[:, ti, :], in_=src_d[(g * 4 + ti) * 128:(g * 4 + ti + 1) * 128, :]),
                    writes=[('a_xs', ti)], dma='a_xs')
            P.seal([('a_xs', ti) for ti in range(4)], 'a_xs')
            transposes_to_xT(c, xs, 'a_xs', [0, 1, 2, 3], xTa, 'a_xT', banks=[0, 1])
            xkeys = [('a_xT', i) for i in range(4)]
            gcol = slice(g * 512, (g + 1) * 512)
            bk = 0
            if which == 0:
                for hp in range(4):
                    bank = bk % 2
                    bk += 1
                    for kc in range(KC):
                        P.add('tensor', lambda e, hp=hp, kc=kc, bank=bank: e.matmul(
                            c.ps[bank][:], lhsT=wk[:, kc, hp * 128:(hp + 1) * 128], rhs=xTa[:, kc, :],
                            start=(kc == 0), stop=(kc == KC - 1)), reads=['a_wk'] + xkeys, writes=[c.psk(bank)])
                    P.add('scalar', lambda e, hp=hp, bank=bank, gcol=gcol: e.copy(out=KT[:, hp, gcol], in_=c.ps[bank][:]),
                          reads=[c.psk(bank)], writes=[('a_KT', hp, g)])
                for ti in range(4):
                    bank = bk % 2
                    bk += 1
                    for kc in range(KC):
                        P.add('tensor', lambda e, ti=ti, kc=kc, bank=bank: e.matmul(
                            c.ps[bank][:], lhsT=xTa[:, kc, ti * 128:(ti + 1) * 128], rhs=wv[:, kc, :],
                            start=(kc == 0), stop=(kc == KC - 1)), reads=['a_wv', ('a_xT', ti)], writes=[c.psk(bank)])
                    P.add('vector', lambda e, ti=ti, bank=bank, g=g: e.tensor_copy(out=V[:, g * 4 + ti, :], in_=c.ps[bank][:]),
                          reads=[c.psk(bank)], writes=[('a_V', g * 4 + ti)])
            else:
                for hp in range(4):
                    bank = bk % 2
                    bk += 1
                    for kc in range(KC):
                        P.add('tensor', lambda e, hp=hp, kc=kc, bank=bank: e.matmul(
                            c.ps[bank][:], lhsT=wq[:, kc, hp * 128:(hp + 1) * 128], rhs=xTa[:, kc, :],
                            start=(kc == 0), stop=(kc == KC - 1)), reads=['a_wq'] + xkeys, writes=[c.psk(bank)])
                    P.add('scalar', lambda e, hp=hp, bank=bank, gcol=gcol: e.activation(
                        out=qT[:, hp, gcol], in_=c.ps[bank][:], func=ACTF.Copy, scale=float(SBD ** -0.5)),
                        reads=[c.psk(bank)], writes=[('a_qT', hp, g)])
    NPB = 3
    pbs = [c.sb("a_pb%d" % i, [128, 513], F32) for i in range(NPB)]
    As = [c.sb("a_A%d" % i, [128, 512], BF16) for i in range(2)]
    ATs = [c.sb("a_AT%d" % i, [128, 512], BF16) for i in range(2)]
    tile_i = 0
    head_i = 0
    for qb in range(NB):
        r0 = 128 * (NB - 1 - qb)
        nk = 128 * (qb + 1)
        ntile = (nk + 511) // 512
        qcol = slice(qb * 128, (qb + 1) * 128)
        for h in range(8):
            hp, half = h // 2, h % 2
            prow = slice(half * 64, (half + 1) * 64)
            ob = 6 + head_i % 2
            head_i += 1
            prev_slot = None
            for kt in range(ntile):
                c0 = r0 + 512 * kt
                w = min(512, SEQ - c0)
                nblk = w // 128
                zb = 2 + tile_i % 2
                ab = 4 + tile_i % 2
                ps_ = tile_i % NPB
                as_ = tile_i % 2
                tile_i += 1
                pb, A, AT = pbs[ps_], As[as_], ATs[as_]
                kkeys = [('a_KT', hp, gg) for gg in range(c0 // 512, (c0 + w - 1) // 512 + 1)]
                P.add('tensor', lambda e, hp=hp, prow=prow, qcol=qcol, c0=c0, w=w, zb=zb, kt=kt: e.matmul(
                    c.ps[zb][:, 0:w], lhsT=qT[prow, hp, qcol], rhs=KT[prow, hp, c0:c0 + w], start=True, stop=(kt != 0)),
                    reads=[('a_qT', hp, qb // 4)] + kkeys, writes=[c.psk(zb)])
                if kt == 0:
                    P.add('tensor', lambda e, w=w, zb=zb: e.matmul(
                        c.ps[zb][:, 0:w], lhsT=c.identb[:], rhs=negm[:, 0:w], start=False, stop=True),
                        reads=['identb', 'a_negm'], writes=[c.psk(zb)])
                P.add('scalar', lambda e, pb=pb, zb=zb, w=w: e.activation(out=pb[:, 1:w + 1], in_=c.ps[zb][:, 0:w],
                                                                          func=ACTF.Sigmoid, scale=-1.0),
                      reads=[c.psk(zb)], writes=[('a_pb', ps_)])
                if kt == 0:
                    P.add('vector', lambda e, pb=pb: e.memset(pb[:, 0:1], 1.0), writes=[('a_pb0', ps_)],
                          reads=[])
                else:
                    ppb, pw = pbs[prev_slot[0]], prev_slot[1]
                    P.add('vector', lambda e, pb=pb, ppb=ppb, pw=pw: e.tensor_copy(out=pb[:, 0:1], in_=ppb[:, pw:pw + 1]),
                          reads=[('a_pb', prev_slot[0])], writes=[('a_pb0', ps_)])
                P.add('vector', lambda e, pb=pb, w=w: e.tensor_tensor_scan(
                    out=pb[:, 1:w + 1], data0=pb[:, 1:w + 1], data1=zeros[:, 0:w], initial=pb[:, 0:1],
                    op0=ALU.mult, op1=ALU.add),
                    reads=[('a_pb', ps_), ('a_pb0', ps_), 'a_zeros'], writes=[('a_pb', ps_)])
                P.add('gpsimd', lambda e, pb=pb, A=A, w=w: e.tensor_tensor(out=A[:, 0:w], in0=pb[:, 0:w], in1=pb[:, 1:w + 1],
                                                                          op=ALU.subtract),
                      reads=[('a_pb', ps_), ('a_pb0', ps_)], writes=[('a_A', as_)])
                psb = c.ps[ab][:].bitcast(BF16)
                for bi in range(nblk):
                    P.add('tensor', lambda e, bi=bi, A=A, psb=psb: e.transpose(
                        out=psb[:, bi * 128:(bi + 1) * 128], in_=A[:, bi * 128:(bi + 1) * 128], identity=c.identb[:]),
                        reads=[('a_A', as_), 'identb'], writes=[c.psk(ab)])
                P.add('scalar', lambda e, AT=AT, psb=psb, w=w: e.copy(out=AT[:, 0:w], in_=psb[:, 0:w]),
                      reads=[c.psk(ab)], writes=[('a_AT', as_)])
                for bi in range(nblk):
                    vt = c0 // 128 + bi
                    first = (kt == 0 and bi == 0)
                    last = (kt == ntile - 1 and bi == nblk - 1)
                    P.add('tensor', lambda e, bi=bi, vt=vt, hp=hp, AT=AT, ob=ob, first=first, last=last: e.matmul(
                        c.ps[ob][:, 0:128], lhsT=V[:, vt, hp * 128:(hp + 1) * 128], rhs=AT[:, bi * 128:(bi + 1) * 128],
                        start=first, stop=last),
                        reads=[('a_V', vt), ('a_AT', as_)], writes=[c.psk(ob)])
                prev_slot = (ps_, w)
            eng = 'vector' if half == 0 else 'scalar'
            if eng == 'vector':
                P.add('vector', lambda e, prow=prow, hp=hp, qcol=qcol, ob=ob: e.tensor_copy(
                    out=oT[prow, hp, qcol], in_=c.ps[ob][prow, 0:128]), reads=[c.psk(ob)], writes=[('a_oT', hp, qb, half)])
            else:
                P.add('scalar', lambda e, prow=prow, hp=hp, qcol=qcol, ob=ob: e.copy(
                    out=oT[prow, hp, qcol], in_=c.ps[ob][prow, 0:128]), reads=[c.psk(ob)], writes=[('a_oT', hp, qb, half)])
    for hp in range(4):
        P.add('sync', lambda e, hp=hp: e.dma_start(out=oT_d[hp, :, :], in_=oT[:, hp, :]),
              reads=[('a_oT', hp, qb, half) for qb in range(NB) for half in range(2)], dma='a_out')


def build_attn(dims, SEQ):
    c = Ctx(dims)
    Dm = dims['D']
    x3r = c.inp("x3r", [SEQ, Dm])
    x4 = c.inp("x4", [SEQ, Dm])
    wk = c.inp("wk", [Dm, 512])
    wv = c.inp("wv", [Dm, 512])
    wq = c.inp("wq", [Dm, 512])
    negm = c.inp("c_negm", [128, 512])
    oT = c.outp("oT", [4, 128, SEQ], BF16)
    load_consts(c)
    attn_kernel(c, SEQ, x3r, x4, wk, wv, wq, oT, negm)
    return finish(c)


FULL = dict(D=D, DFF=DFF)
NT_CORE = 2048
SEQ = 4096


def alloc_lnp(c, name="lnp"):
    Dm = c.dm['D']
    return (c.sb(name + "_g", [128, Dm], F32), c.sb(name + "_b", [128, Dm], F32), name + '_g', name + '_b')


def set_lnp(c, lnp, g_d, b_d):
    g, b, gk, bk = lnp
    c.P.add('sync', lambda e: e.dma_start(out=g[:], in_=g_d.partition_broadcast(128)), writes=[gk], dma='lnp')
    c.P.add('sync', lambda e: e.dma_start(out=b[:], in_=b_d.partition_broadcast(128)), writes=[bk], dma='lnp')
    c.P.seal([gk, bk], 'lnp')


def gla_weight_inputs(c, Dm):
    return dict(w_in=c.inp("gla_w_in", [Dm, 3088]), w_gk=c.inp("gla_w_gk", [16, 512]), b_gk=c.inp("gla_b_gk", [512]),
                norm_g=c.inp("gla_norm_g", [256]), w_out=c.inp("gla_w_out", [1024, Dm]))


def build_L1(dims=FULL, NT=NT_CORE):
    c = Ctx(dims)
    Dm, Dff = dims['D'], dims['DFF']
    x_d = c.inp("x", [NT, Dm])
    wup = c.inp("w_up", [Dm, 2 * Dff])
    wdn = c.inp("w_dn", [Dff, Dm])
    lng = c.inp("ln_g", [Dm])
    lnb = c.inp("ln_b", [Dm])
    w = gla_weight_inputs(c, Dm)
    w['s_init'] = None
    w['f_out'] = c.outp("f_out", [4, 128, 256])
    x1 = c.outp("x1", [NT, Dm])
    load_consts(c)
    load_gla_consts(c)
    x_res = c.sb("x_res", [128, NT // 128, Dm], F32)
    lnp = alloc_lnp(c)
    load_x(c, x_d, x_res, 'x', NT)
    set_lnp(c, lnp, lng, lnb)
    m = c.mark()
    bufs = alloc_ffn_bufs(c, NT)
    ffn_sublayer(c, x_res, 'x', NT, wup, wdn, lnp, bufs)
    store_x(c, x1, x_res, 'x', NT)
    c.release(m)
    gb = alloc_gla_bufs(c, False)
    gla_pass(c, x_res, 'x', NT, False, w, gb, None)
    return finish(c)


def build_L2(dims=FULL, NT=NT_CORE):
    c = Ctx(dims)
    Dm, Dff = dims['D'], dims['DFF']
    x_d = c.inp("x1", [NT, Dm])
    w = gla_weight_inputs(c, Dm)
    w['s_init'] = c.inp("s_init", [4, 128, 256])
    w['f_out'] = None
    lng = [c.inp("ln_g%d" % i, [Dm]) for i in range(3)]
    lnb = [c.inp("ln_b%d" % i, [Dm]) for i in range(3)]
    wup = [c.inp("w_up%d" % i, [Dm, 2 * Dff]) for i in range(2)]
    wdn = [c.inp("w_dn%d" % i, [Dff, Dm]) for i in range(2)]
    x3 = c.outp("x3", [NT, Dm])
    x4 = c.outp("x4", [NT, Dm])
    load_consts(c)
    load_gla_consts(c)
    x_res = c.sb("x_res", [128, NT // 128, Dm], F32)
    lnp = alloc_lnp(c)
    load_x(c, x_d, x_res, 'x', NT)
    set_lnp(c, lnp, lng[0], lnb[0])
    m = c.mark()
    gb = alloc_gla_bufs(c, True)
    gla_pass(c, x_res, 'x', NT, True, w, gb, lnp)
    c.release(m)
    bufs = alloc_ffn_bufs(c, NT)
    set_lnp(c, lnp, lng[1], lnb[1])
    ffn_sublayer(c, x_res, 'x', NT, wup[0], wdn[0], lnp, bufs)
    store_x(c, x3, x_res, 'x', NT, tag='stx3')
    set_lnp(c, lnp, lng[2], lnb[2])
    ffn_sublayer(c, x_res, 'x', NT, wup[1], wdn[1], lnp, bufs)
    store_x(c, x4, x_res, 'x', NT, tag='stx4')
    return finish(c)


def build_L4(dims=FULL, NT=NT_CORE):
    c = Ctx(dims)
    P = c.P
    Dm, Dff = dims['D'], dims['DFF']
    nh = Dm // 512
    x_d = c.inp("x4", [NT, Dm])
    oT_d = c.inp("oT", [1024, NT], BF16)
    wo_d = c.inp("sb_w_out", [1024, Dm])
    lng = [c.inp("ln_g%d" % i, [Dm]) for i in range(2)]
    lnb = [c.inp("ln_b%d" % i, [Dm]) for i in range(2)]
    wup = c.inp("w_up", [Dm, 2 * Dff])
    wdn = c.inp("w_dn", [Dff, Dm])
    out = c.outp("out", [NT, Dm])
    load_consts(c)
    x_res = c.sb("x_res", [128, NT // 128, Dm], F32)
    lnp = alloc_lnp(c)
    load_x(c, x_d, x_res, 'x', NT)
    set_lnp(c, lnp, lng[0], lnb[0])
    m = c.mark()
    oTs = c.sb("oTs", [128, 8, NT], BF16)
    wo = c.sb("wo", [128, 8, Dm], BF16)
    lntmp = alloc_ln_tmp(c, "mln")
    for kc in range(8):
        P.add('sync', lambda e, kc=kc: e.dma_start(out=oTs[:, kc, :], in_=oT_d[kc * 128:(kc + 1) * 128, :]),
              writes=[('oTs', kc)], dma='oTs')
        P.add('gpsimd', lambda e, kc=kc: e.dma_start(out=wo[:, kc, :], in_=wo_d[kc * 128:(kc + 1) * 128, :]),
              writes=[('wo', kc)], dma='wo')
    P.seal([('oTs', kc) for kc in range(8)], 'oTs')
    P.seal([('wo', kc) for kc in range(8)], 'wo')
    for tt in range(NT // 128):
        par = tt % 2
        banks2 = [4 * par + hh for hh in range(nh)]
        for hh in range(nh):
            for kc in range(8):
                P.add('tensor', lambda e, kc=kc, hh=hh, tt=tt, banks2=banks2: e.matmul(
                    c.ps[banks2[hh]][:], lhsT=oTs[:, kc, tt * 128:(tt + 1) * 128], rhs=wo[:, kc, hh * 512:(hh + 1) * 512],
                    start=(kc == 0), stop=(kc == 7)),
                    reads=[('oTs', kc), ('wo', kc)], writes=[c.psk(banks2[hh])])
        ln_epilogue(c, banks2, x_res, 'x', tt, lnp, lntmp[par], ('mlntmp', par))
    c.release(m)
    bufs = alloc_ffn_bufs(c, NT)
    set_lnp(c, lnp, lng[1], lnb[1])
    ffn_sublayer(c, x_res, 'x', NT, wup, wdn, lnp, bufs)
    store_x(c, out, x_res, 'x', NT)
    return finish(c)


_CACHE = {}


def _prog(name, builder):
    if name not in _CACHE:
        _CACHE[name] = builder()
    return _CACHE[name]


def _run(nc, in_maps):
    res = run_bass_kernel_spmd(nc, in_maps, core_ids=list(range(len(in_maps))))
    return res.results


def kernel_multi(x, ln_g, ln_b, ffn_w_up, ffn_w_down, gla_w_in, gla_w_gk, gla_b_gk, gla_norm_g, gla_w_out,
                 sb_w_kv, sb_w_q, sb_w_out):
    f = lambda a: np.ascontiguousarray(np.asarray(a, dtype=np.float32))
    x, ln_g, ln_b, ffn_w_up, ffn_w_down = f(x), f(ln_g), f(ln_b), f(ffn_w_up), f(ffn_w_down)
    gla_w_in, gla_w_gk, gla_b_gk, gla_norm_g, gla_w_out = f(gla_w_in), f(gla_w_gk), f(gla_b_gk), f(gla_norm_g), f(gla_w_out)
    sb_w_kv, sb_w_q, sb_w_out = f(sb_w_kv), f(sb_w_q), f(sb_w_out)
    B = x.shape[0]
    ident = np.eye(128, dtype=np.float32)
    t2s, ind = gla_consts_host()
    negm = attn_consts_host()
    cores = [(b, h) for b in range(B) for h in range(2)]
    glaw = {"gla_w_in": gla_w_in[0], "gla_w_gk": gla_w_gk[0], "gla_b_gk": gla_b_gk[0], "gla_norm_g": gla_norm_g[0],
            "gla_w_out": gla_w_out[0], "c_ident": ident, "c_t2s": t2s, "c_ind": ind}
    ims = []
    for (b, h) in cores:
        d = dict(glaw)
        d.update({"x": f(x[b, h * NT_CORE:(h + 1) * NT_CORE]), "w_up": ffn_w_up[0, 0], "w_dn": ffn_w_down[0, 0],
                  "ln_g": ln_g[0, 0], "ln_b": ln_b[0, 0]})
        ims.append(d)
    r1 = _run(_prog("L1", build_L1), ims)
    ims = []
    zero_state = np.zeros((4, 128, 256), np.float32)
    for i, (b, h) in enumerate(cores):
        d = dict(glaw)
        d.update({"x1": r1[i]["x1"], "s_init": zero_state if h == 0 else r1[i - 1]["f_out"],
                  "ln_g0": ln_g[0, 1], "ln_b0": ln_b[0, 1], "ln_g1": ln_g[0, 2], "ln_b1": ln_b[0, 2],
                  "ln_g2": ln_g[1, 0], "ln_b2": ln_b[1, 0],
                  "w_up0": ffn_w_up[0, 1], "w_dn0": ffn_w_down[0, 1], "w_up1": ffn_w_up[1, 0], "w_dn1": ffn_w_down[1, 0]})
        ims.append(d)
    r2 = _run(_prog("L2", build_L2), ims)
    ims = []
    for (b, hh) in cores:
        x3 = np.concatenate([r2[2 * b]["x3"], r2[2 * b + 1]["x3"]], axis=0)
        x4 = np.concatenate([r2[2 * b]["x4"], r2[2 * b + 1]["x4"]], axis=0)
        ims.append({"x3r": np.ascontiguousarray(x3[::-1]), "x4": x4,
                    "wk": f(sb_w_kv[:, hh * 512:(hh + 1) * 512]), "wv": f(sb_w_kv[:, 1024 + hh * 512:1024 + (hh + 1) * 512]),
                    "wq": f(sb_w_q[0][:, hh * 512:(hh + 1) * 512]), "c_ident": ident, "c_negm": negm})
    r3 = _run(_prog("L3", lambda: build_attn(FULL, SEQ)), ims)
    ims = []
    for i, (b, h) in enumerate(cores):
        o0 = np.asarray(r3[2 * b]["oT"]).reshape(512, SEQ)
        o1 = np.asarray(r3[2 * b + 1]["oT"]).reshape(512, SEQ)
        oT = np.ascontiguousarray(np.concatenate([o0, o1], axis=0)[:, h * NT_CORE:(h + 1) * NT_CORE])
        ims.append({"x4": r2[i]["x4"], "oT": oT, "sb_w_out": sb_w_out[0], "ln_g0": ln_g[1, 1], "ln_b0": ln_b[1, 1],
                    "ln_g1": ln_g[1, 2], "ln_b1": ln_b[1, 2], "w_up": ffn_w_up[1, 1], "w_dn": ffn_w_down[1, 1],
                    "c_ident": ident})
    r4 = _run(_prog("L4", build_L4), ims)
    out = np.empty((B, SEQ, D), np.float32)
    for i, (b, h) in enumerate(cores):
        out[b, h * NT_CORE:(h + 1) * NT_CORE] = r4[i]["out"]
    return out


def kv_phase(c, x_res, xkey, hf, wkv_d, KT_d, V_d, flag, NT=NT_CORE):
    P = c.P
    Dm = c.dm['D']
    KC = Dm // 128
    m = c.mark()
    wkv = c.sb("kv_w", [128, KC, 2048], BF16)
    for kc in range(KC):
        P.add('gpsimd', lambda e, kc=kc: e.dma_start(out=wkv[:, kc, :], in_=wkv_d[kc * 128:(kc + 1) * 128, :]),
              writes=[('kv_w', kc)], dma='kv_w')
    P.seal([('kv_w', kc) for kc in range(KC)], 'kv_w')
    wkeys = [('kv_w', kc) for kc in range(KC)]
    xTr = [c.sb("kv_xT%d" % i, [128, KC, 512], BF16) for i in range(2)]
    kst = [c.sb("kv_ks%d" % i, [128, 512], BF16) for i in range(2)]
    vst = [c.sb("kv_vs%d" % i, [128, 1024], BF16) for i in range(2)]
    ntile = NT // 128
    cnt = 0
    for g in range(ntile // 4):
        xT = xTr[g % 2]
        for i in range(4):
            tt = 4 * g + 3 - i
            for k0 in range(0, KC, 4):
                bank = cnt % 2
                cnt += 1
                for kk in range(4):
                    kc = k0 + kk
                    P.add('tensor', lambda e, bank=bank, kk=kk, kc=kc, tt=tt: e.matmul(
                        c.ps[bank][:, kk * 128:(kk + 1) * 128], lhsT=x_res[:, tt, kc * 128:(kc + 1) * 128], rhs=c.antiid[:],
                        start=True, stop=True), reads=[(xkey, tt), 'antiid'], writes=[c.psk(bank)])
                src = c.ps[bank][:].rearrange("p (a b) -> p a b", a=4)
                dst = xT[:, k0:k0 + 4, i * 128:(i + 1) * 128]
                if cnt % 2 == 0:
                    P.add('scalar', lambda e, src=src, dst=dst: e.copy(out=dst, in_=src), reads=[c.psk(bank)],
                          writes=[('kv_xT', g % 2, i)])
                else:
                    P.add('vector', lambda e, src=src, dst=dst: e.tensor_copy(out=dst, in_=src), reads=[c.psk(bank)],
                          writes=[('kv_xT', g % 2, i)])
        xk = [('kv_xT', g % 2, i) for i in range(4)]
        vt0 = 31 - (hf * 16 + 4 * g + 3)
        for hp in range(8):
            bank = 2 + hp % 2
            ks = kst[hp % 2]
            for kc in range(KC):
                P.add('tensor', lambda e, hp=hp, kc=kc, bank=bank, xT=xT: e.matmul(
                    c.ps[bank][:], lhsT=wkv[:, kc, hp * 128:(hp + 1) * 128], rhs=xT[:, kc, :],
                    start=(kc == 0), stop=(kc == KC - 1)), reads=wkeys + xk, writes=[c.psk(bank)])
            P.add('scalar', lambda e, ks=ks, bank=bank: e.copy(out=ks[:], in_=c.ps[bank][:]), reads=[c.psk(bank)],
                  writes=[('kv_ks', hp % 2)])
            P.add('sync', lambda e, ks=ks, hp=hp, vt0=vt0: e.dma_start(out=KT_d[hp, :, vt0 * 128:vt0 * 128 + 512], in_=ks[:]),
                  reads=[('kv_ks', hp % 2)], dma='kv_kst%d' % (hp % 2))
        for i in range(4):
            vs = vst[i % 2]
            for hv in range(2):
                bank = 4 + 2 * (i % 2) + hv
                for kc in range(KC):
                    P.add('tensor', lambda e, i=i, hv=hv, kc=kc, bank=bank, xT=xT: e.matmul(
                        c.ps[bank][:], lhsT=xT[:, kc, i * 128:(i + 1) * 128], rhs=wkv[:, kc, 1024 + hv * 512:1024 + (hv + 1) * 512],
                        start=(kc == 0), stop=(kc == KC - 1)), reads=wkeys + [('kv_xT', g % 2, i)], writes=[c.psk(bank)])
                if hf == 0:
                    P.add('scalar', lambda e, vs=vs, hv=hv, bank=bank: e.activation(
                        out=vs[:, hv * 512:(hv + 1) * 512], in_=c.ps[bank][:], func=ACTF.Copy, scale=flag[:, 0:1]),
                        reads=[c.psk(bank), 'flag'], writes=[('kv_vs', i % 2, hv)])
                else:
                    P.add('vector', lambda e, vs=vs, hv=hv, bank=bank: e.tensor_copy(
                        out=vs[:, hv * 512:(hv + 1) * 512], in_=c.ps[bank][:]),
                        reads=[c.psk(bank)], writes=[('kv_vs', i % 2, hv)])
            P.add('sync', lambda e, vs=vs, i=i, vt0=vt0: e.dma_start(
                out=V_d[:, :, vt0 + i, :].rearrange("hp p d -> p hp d"), in_=vs[:].rearrange("p (a b) -> p a b", a=8)),
                reads=[('kv_vs', i % 2, 0), ('kv_vs', i % 2, 1)], dma='kv_vst%d' % (i % 2))
    c.release(m)


def attn_phase(c, x_res, xkey, wq_d, wo_d, KT_d, V_d, negm, lnp, NT=NT_CORE):
    P = c.P
    Dm = c.dm['D']
    KC = Dm // 128
    nh = Dm // 512
    NBQ = NT // 128
    NB = SEQ // 128
    qb0 = NB - NBQ
    m = c.mark()
    qT = c.sb("a_qT", [128, 8, NT], BF16)
    oT = c.sb("a_oT", [128, 8, NT], BF16)
    m2 = c.mark()
    wq = c.sb("a_wq", [128, KC, 1024], BF16)
    for kc in range(KC):
        P.add('gpsimd', lambda e, kc=kc: e.dma_start(out=wq[:, kc, :], in_=wq_d[kc * 128:(kc + 1) * 128, :]),
              writes=[('a_wq', kc)], dma='a_wq')
    P.seal([('a_wq', kc) for kc in range(KC)], 'a_wq')
    wkeys = [('a_wq', kc) for kc in range(KC)]
    xTa = [c.sb("a_xT%d" % i, [128, KC, 512], BF16) for i in range(2)]
    for g in range(NT // 512):
        xT = xTa[g % 2]
        transposes_to_xT(c, x_res, xkey, list(range(4 * g, 4 * g + 4)), xT, ('a_xT', g % 2), banks=[0, 1])
        xk = [(('a_xT', g % 2), i) for i in range(4)]
        for hp in range(8):
            bank = 2 + hp % 4
            for kc in range(KC):
                P.add('tensor', lambda e, hp=hp, kc=kc, bank=bank, xT=xT: e.matmul(
                    c.ps[bank][:], lhsT=wq[:, kc, hp * 128:(hp + 1) * 128], rhs=xT[:, kc, :],
                    start=(kc == 0), stop=(kc == KC - 1)), reads=wkeys + xk, writes=[c.psk(bank)])
            P.add('scalar', lambda e, hp=hp, bank=bank, g=g: e.activation(
                out=qT[:, hp, g * 512:(g + 1) * 512], in_=c.ps[bank][:], func=ACTF.Copy, scale=float(SBD ** -0.5)),
                reads=[c.psk(bank)], writes=[('a_qT', hp)])
    c.release(m2)
    zeros = c.sb("a_zeros", [128, 512], F32)
    P.add('gpsimd', lambda e: e.memset(zeros[:], 0.0), writes=['a_zeros'])
    KTb = [c.sb("a_KT%d" % i, [128, SEQ], BF16) for i in range(2)]
    Vb = [c.sb("a_V%d" % i, [128, NB, 128], BF16) for i in range(2)]
    NPB, NA, NAT = 6, 6, 3
    pbs = [c.sb("a_pb%d" % i, [128, 513], F32) for i in range(NPB)]
    As = [c.sb("a_A%d" % i, [128, 512], BF16) for i in range(NA)]
    ATs = [c.sb("a_AT%d" % i, [128, 512], BF16) for i in range(NAT)]
    tiles = []
    head_i = 0
    for hp in range(8):
        for qbl in range(NBQ):
            qb = qb0 + qbl
            r0 = 128 * (NB - 1 - qb)
            nk = 128 * (qb + 1)
            ntile = (nk + 511) // 512
            for kt in range(ntile):
                for half in range(2):
                    c0 = r0 + 512 * kt
                    tiles.append(dict(hp=hp, qbl=qbl, half=half, kt=kt, ntile=ntile, c0=c0, w=min(512, SEQ - c0),
                                      head_i=head_i + half, idx=len(tiles)))
            head_i += 2

    def load_kv(hp):
        s = hp % 2
        P.add('sync', lambda e, hp=hp, s=s: e.dma_start(out=KTb[s][:], in_=KT_d[hp, :, :]), writes=[('a_KT', s)], dma='a_ldk%d' % s)
        P.add('sync', lambda e, hp=hp, s=s: e.dma_start(out=Vb[s][:], in_=V_d[hp, :, :, :]), writes=[('a_V', s)], dma='a_ldv%d' % s)

    def stage_A(t):
        hp, half, kt, c0, w, i = t['hp'], t['half'], t['kt'], t['c0'], t['w'], t['idx']
        s = hp % 2
        prow = slice(half * 64, (half + 1) * 64)
        qcol = slice(t['qbl'] * 128, (t['qbl'] + 1) * 128)
        zb = 2 + i % 2
        ps_ = i % NPB
        as_ = i % NA
        pb, A = pbs[ps_], As[as_]
        P.add('tensor', lambda e: e.matmul(c.ps[zb][:, 0:w], lhsT=qT[prow, hp, qcol], rhs=KTb[s][prow, c0:c0 + w],
                                           start=True, stop=(kt != 0)),
              reads=[('a_qT', hp), ('a_KT', s)], writes=[c.psk(zb)])
        if kt == 0:
            P.add('tensor', lambda e: e.matmul(c.ps[zb][:, 0:w], lhsT=c.identb[:], rhs=negm[:, 0:w], start=False, stop=True),
                  reads=['identb', 'a_negm'], writes=[c.psk(zb)])
        P.add('scalar', lambda e: e.activation(out=pb[:, 1:w + 1], in_=c.ps[zb][:, 0:w], func=ACTF.Sigmoid, scale=-1.0),
              reads=[c.psk(zb)], writes=[('a_pb', ps_)])
        if kt == 0:
            P.add('scalar', lambda e: e.copy(out=pb[:, 0:1], in_=c.oneb[:, 0:1]), reads=['oneb'], writes=[('a_pb0', ps_)])
            P.add('vector', lambda e: e.tensor_tensor_scan(out=pb[:, 1:w + 1], data0=pb[:, 1:w + 1], data1=zeros[:, 0:w],
                                                           initial=1.0, op0=ALU.mult, op1=ALU.add),
                  reads=[('a_pb', ps_), 'a_zeros'], writes=[('a_pb', ps_)])
        else:
            pps = (i - 2) % NPB
            ppb, pw = pbs[pps], tiles[i - 2]['w']
            P.add('scalar', lambda e: e.copy(out=pb[:, 0:1], in_=ppb[:, pw:pw + 1]), reads=[('a_pb', pps)], writes=[('a_pb0', ps_)])
            P.add('vector', lambda e: e.tensor_tensor_scan(out=pb[:, 1:w + 1], data0=pb[:, 1:w + 1], data1=zeros[:, 0:w],
                                                           initial=ppb[:, pw:pw + 1], op0=ALU.mult, op1=ALU.add),
                  reads=[('a_pb', ps_), ('a_pb', pps), 'a_zeros'], writes=[('a_pb', ps_)])
        P.add('gpsimd', lambda e: e.tensor_tensor(out=A[:, 0:w], in0=pb[:, 0:w], in1=pb[:, 1:w + 1], op=ALU.subtract),
              reads=[('a_pb', ps_), ('a_pb0', ps_)], writes=[('a_A', as_)])

    def stage_B(t):
        w, i = t['w'], t['idx']
        ab = 4 + i % 2
        as_ = i % NA
        at_ = i % NAT
        A, AT = As[as_], ATs[at_]
        psb = c.ps[ab][:].bitcast(BF16)
        for bi in range(w // 128):
            P.add('tensor', lambda e, bi=bi: e.transpose(out=psb[:, bi * 128:(bi + 1) * 128],
                                                         in_=A[:, bi * 128:(bi + 1) * 128], identity=c.identb[:]),
                  reads=[('a_A', as_), 'identb'], writes=[c.psk(ab)])
        P.add('scalar', lambda e: e.copy(out=AT[:, 0:w], in_=psb[:, 0:w]), reads=[c.psk(ab)], writes=[('a_AT', at_)])

    def stage_C(t):
        hp, half, kt, c0, w, i = t['hp'], t['half'], t['kt'], t['c0'], t['w'], t['idx']
        s = hp % 2
        at_ = i % NAT
        AT = ATs[at_]
        ob = (6, 7, 0, 1)[t['head_i'] % 4]
        nblk = w // 128
        for bi in range(nblk):
            vt = c0 // 128 + bi
            first = (kt == 0 and bi == 0)
            last = (kt == t['ntile'] - 1 and bi == nblk - 1)
            P.add('tensor', lambda e, bi=bi, vt=vt, first=first, last=last: e.matmul(
                c.ps[ob][:, 0:128], lhsT=Vb[s][:, vt, :], rhs=AT[:, bi * 128:(bi + 1) * 128], start=first, stop=last),
                reads=[('a_V', s), ('a_AT', at_)], writes=[c.psk(ob)])
        if kt == t['ntile'] - 1:
            prow = slice(half * 64, (half + 1) * 64)
            qcol = slice(t['qbl'] * 128, (t['qbl'] + 1) * 128)
            if half == 0:
                P.add('vector', lambda e: e.tensor_copy(out=oT[prow, hp, qcol], in_=c.ps[ob][prow, 0:128]),
                      reads=[c.psk(ob)], writes=[('a_oT', hp, t['qbl'], half)])
            else:
                P.add('scalar', lambda e: e.copy(out=oT[prow, hp, qcol], in_=c.ps[ob][prow, 0:128]),
                      reads=[c.psk(ob)], writes=[('a_oT', hp, t['qbl'], half)])

    n = len(tiles)
    DB, DC = 3, 4
    load_kv(0)
    for s_ in range(n + DC):
        if s_ < n:
            t = tiles[s_]
            if t['qbl'] == 0 and t['half'] == 0 and t['kt'] == 0 and t['hp'] + 1 < 8:
                load_kv(t['hp'] + 1)
            stage_A(t)
        if 0 <= s_ - DB < n:
            stage_B(tiles[s_ - DB])
        if 0 <= s_ - DC < n:
            stage_C(tiles[s_ - DC])
    c.release(m2)
    wo = c.sb("a_wo", [128, 8, Dm], BF16)
    lntmp = alloc_ln_tmp(c, "aln")
    for kc in range(8):
        P.add('gpsimd', lambda e, kc=kc: e.dma_start(out=wo[:, kc, :], in_=wo_d[kc * 128:(kc + 1) * 128, :]),
              writes=[('a_wo', kc)], dma='a_wo')
    P.seal([('a_wo', kc) for kc in range(8)], 'a_wo')
    for tt in range(NT // 128):
        par = tt % 2
        banks2 = [4 * par + hh for hh in range(nh)]
        for hh in range(nh):
            for kc in range(8):
                P.add('tensor', lambda e, kc=kc, hh=hh, tt=tt, banks2=banks2: e.matmul(
                    c.ps[banks2[hh]][:], lhsT=oT[:, kc, tt * 128:(tt + 1) * 128], rhs=wo[:, kc, hh * 512:(hh + 1) * 512],
                    start=(kc == 0), stop=(kc == 7)),
                    reads=[('a_wo', kc)], writes=[c.psk(banks2[hh])])
        ln_epilogue(c, banks2, x_res, xkey, tt, lnp, lntmp[par], ('alntmp', par))
    c.release(m)


def build_fused(dims=FULL, NT=NT_CORE):
    c = Ctx(dims)
    P, nc = c.P, c.nc
    Dm, Dff = dims['D'], dims['DFF']
    x_in = c.inp("x_in", [2 * NT, Dm])
    flag_d = c.inp("flag", [128, 1])
    ln_g = c.inp("ln_g", [DEPTH, 3, Dm])
    ln_b = c.inp("ln_b", [DEPTH, 3, Dm])
    wup = c.inp("ffn_w_up", [DEPTH, 2, Dm, 2 * Dff])
    wdn = c.inp("ffn_w_down", [DEPTH, 2, Dff, Dm])
    w = gla_weight_inputs(c, Dm)
    wkv_d = c.inp("sb_w_kv", [Dm, 2048])
    wq_d = c.inp("sb_w_q", [Dm, 1024])
    wo_d = c.inp("sb_w_out", [1024, Dm])
    negm_d = c.inp("c_negm", [128, 512])
    anti_d = c.inp("c_antiid", [128, 128])
    out = c.outp("out", [NT, Dm])
    KT_d = nc.dram_tensor("kt_scratch", [8, 128, SEQ], BF16).ap()
    V_d = nc.dram_tensor("v_scratch", [8, 128, SEQ // 128, 128], BF16).ap()
    load_consts(c)
    load_gla_consts(c)
    flag = c.sb("flag", [128, 1], F32)
    P.add('sync', lambda e: e.dma_start(out=flag[:], in_=flag_d), writes=['flag'], dma='c4')
    c.antiid = c.sb("antiid", [128, 128], F32)
    P.add('sync', lambda e: e.dma_start(out=c.antiid[:], in_=anti_d), writes=['antiid'], dma='c5')
    negm = c.sb("a_negm", [128, 512], BF16)
    P.add('gpsimd', lambda e: e.dma_start(out=negm[:], in_=negm_d), writes=['a_negm'], dma='c6')
    S = c.sb("g_S", [128, 4, 256], F32)
    P.add('vector', lambda e: e.memset(S[:], 0.0), writes=[('g_S', h) for h in range(GH)])
    x_res = c.sb("x_res", [128, NT // 128, Dm], F32)
    lnp = alloc_lnp(c)
    c.P.barrier()
    base = c.mark()

    def ffn(layer, idx):
        m = c.mark()
        bufs = alloc_ffn_bufs(c, NT)
        set_lnp(c, lnp, ln_g[layer, 2 * idx, :], ln_b[layer, 2 * idx, :])
        ffn_sublayer(c, x_res, 'x', NT, wup[layer, idx], wdn[layer, idx], lnp, bufs)
        c.release(m)

    for hf in range(2):
        load_x(c, x_in[hf * NT:(hf + 1) * NT, :], x_res, 'x', NT)
        ffn(0, 0)
        m = c.mark()
        gb = alloc_gla_bufs(c, True, S=S)
        set_lnp(c, lnp, ln_g[0, 1, :], ln_b[0, 1, :])
        if hf == 1:
            for h in range(GH):
                P.add('vector', lambda e, h=h: e.tensor_scalar(out=S[:, h, :], in0=S[:, h, :], scalar1=flag[:, 0:1], scalar2=None,
                                                              op0=ALU.mult), reads=[('g_S', h), 'flag'], writes=[('g_S', h)])
        ww = dict(w)
        ww['s_init'] = 'keep'
        ww['f_out'] = None
        gla_pass(c, x_res, 'x', NT, True, ww, gb, lnp)
        c.release(m)
        ffn(0, 1)
        kv_phase(c, x_res, 'x', hf, wkv_d, KT_d, V_d, flag, NT)
    ffn(1, 0)
    set_lnp(c, lnp, ln_g[1, 1, :], ln_b[1, 1, :])
    attn_phase(c, x_res, 'x', wq_d, wo_d, KT_d, V_d, negm, lnp, NT)
    ffn(1, 1)
    store_x(c, out, x_res, 'x', NT)
    return finish(c)


def kernel_fused(x, ln_g, ln_b, ffn_w_up, ffn_w_down, gla_w_in, gla_w_gk, gla_b_gk, gla_norm_g, gla_w_out,
                 sb_w_kv, sb_w_q, sb_w_out):
    f = lambda a: np.ascontiguousarray(np.asarray(a, dtype=np.float32))
    x = f(x)
    B = x.shape[0]
    t2s, ind = gla_consts_host()
    common = {"ln_g": f(ln_g), "ln_b": f(ln_b), "ffn_w_up": f(ffn_w_up), "ffn_w_down": f(ffn_w_down),
              "gla_w_in": f(gla_w_in[0]), "gla_w_gk": f(gla_w_gk[0]), "gla_b_gk": f(gla_b_gk[0]),
              "gla_norm_g": f(gla_norm_g[0]), "gla_w_out": f(gla_w_out[0]), "sb_w_kv": f(sb_w_kv),
              "sb_w_q": f(sb_w_q[0]), "sb_w_out": f(sb_w_out[0]), "c_ident": np.eye(128, dtype=np.float32),
              "c_t2s": t2s, "c_ind": ind, "c_negm": attn_consts_host(),
              "c_antiid": np.ascontiguousarray(np.eye(128, dtype=np.float32)[::-1])}
    cores = [(b, h) for b in range(B) for h in range(2)]
    ims = []
    for (b, h) in cores:
        d = dict(common)
        d["x_in"] = np.ascontiguousarray(np.concatenate([x[b, :NT_CORE], x[b, h * NT_CORE:(h + 1) * NT_CORE]], axis=0))
        d["flag"] = np.full((128, 1), float(h), np.float32)
        ims.append(d)
    r = _run(_prog("FUSED", build_fused), ims)
    out = np.empty((B, SEQ, D), np.float32)
    for i, (b, h) in enumerate(cores):
        out[b, h * NT_CORE:(h + 1) * NT_CORE] = r[i]["out"]
    return out


def kernel(x, ln_g, ln_b, ffn_w_up, ffn_w_down, gla_w_in, gla_w_gk, gla_b_gk, gla_norm_g, gla_w_out,
           sb_w_kv, sb_w_q, sb_w_out):
    return kernel_fused(x, ln_g, ln_b, ffn_w_up, ffn_w_down, gla_w_in, gla_w_gk, gla_b_gk, gla_norm_g, gla_w_out,
                        sb_w_kv, sb_w_q, sb_w_out)
```
